# Optimizing a Trainium2 kernel written in Bass

```python
import math, functools
import jax, jax.numpy as jnp
from jax import lax
import numpy as np


D_MODEL = 2048
BATCH = 2
SEQ = 4096
DEPTH = 1
DEC_BATCH = 32
DEC_SEQ = 16
PAST_LEN = 4096

CHUNK = 64
Q_BLOCK = 128
A_HEADS = 8
A_DK = 64
A_DV = 2 * A_DK
A_WIDTH = A_HEADS * A_DV
B_HEAD = 64
B_WIDTH = D_MODEL - A_WIDTH
B_HEADS = B_WIDTH // B_HEAD
W_RANK = 64
A_RANK = 64
G_RANK = 128
RW_COLS = 3 * B_WIDTH + W_RANK + A_RANK + G_RANK
RW_SPLITS = [B_WIDTH, 2 * B_WIDTH, 3 * B_WIDTH, 3 * B_WIDTH + W_RANK, 3 * B_WIDTH + W_RANK + A_RANK]
IN_COLS = 3 * A_WIDTH + RW_COLS
D_FF = 5632
FFN_CONV = 3
EPS = 1e-6
GN_EPS = 64e-5

kernel_name = 'hymba_diffattn_rwkv7_convffn_stream'


def rms_norm(x, g):
    xf = x.astype(jnp.float32)
    y = xf * lax.rsqrt(jnp.mean(xf * xf, axis=-1, keepdims=True) + EPS)
    return (y * g.astype(jnp.float32)).astype(x.dtype)


def diff_attend(q, k, v, mask, lam):
    s = jnp.einsum('bqhmd,bkhmd->bhmqk', q, k).astype(jnp.float32) * (A_DK ** -0.5)
    if mask is not None:
        s = jnp.where(mask, s, -jnp.inf)
    p = jax.nn.softmax(s, axis=-1)
    a = p[:, :, 0] - lam * p[:, :, 1]
    return jnp.einsum('bhqk,bkhd->bqhd', a.astype(v.dtype), v)


def prompt_attention(q, k, v, lam):
    Bn, T = q.shape[0], q.shape[1]
    nb = T // Q_BLOCK
    qb = jnp.moveaxis(q.reshape(Bn, nb, Q_BLOCK, A_HEADS, 2, A_DK), 1, 0)
    key_chunk = jnp.arange(T) // CHUNK
    starts = jnp.arange(nb) * Q_BLOCK

    def block(args):
        qi, s0 = args
        q_chunk = (s0 + jnp.arange(Q_BLOCK)) // CHUNK
        mask = key_chunk[None, :] <= q_chunk[:, None]
        return diff_attend(qi, k, v, mask, lam)

    o = lax.map(block, (qb, starts))
    return jnp.moveaxis(o, 0, 1).reshape(Bn, T, A_HEADS, A_DV)


def sample_attention(q, k, v, lam, cache_k, cache_v):
    Bn, P = cache_k.shape[0], cache_k.shape[1]
    k_all = jnp.concatenate([cache_k.reshape(Bn, P, A_HEADS, 2, A_DK).astype(k.dtype), k], axis=1)
    v_all = jnp.concatenate([cache_v.astype(v.dtype), v], axis=1)
    return diff_attend(q, k_all, v_all, None, lam)


def wkv_scan(r, decay, k, v, a, b, s0):
    def step(S, inp):
        r_t, d_t, k_t, v_t, a_t, b_t = inp
        sa = jnp.einsum('bhvk,bhk->bhv', S, a_t)
        S = S * d_t[:, :, None, :] + sa[..., None] * b_t[:, :, None, :] + v_t[..., None] * k_t[:, :, None, :]
        y = jnp.einsum('bhvk,bhk->bhv', S, r_t)
        return S, y

    xs = tuple(jnp.moveaxis(t, 1, 0) for t in (r, decay, k, v, a, b))
    S, ys = lax.scan(step, s0, xs)
    return jnp.moveaxis(ys, 0, 1), S


def rwkv7_mix(p, prev, wkv0, lp):
    Bn, T = p.shape[0], p.shape[1]
    p_prev = jnp.concatenate([prev[:, None, :].astype(p.dtype), p[:, :-1]], axis=1)
    xs = p + (p_prev - p) * lp['mu_shift']
    r, k, v, wd, ad, gd = jnp.split(xs, RW_SPLITS, axis=-1)
    w = -jax.nn.softplus(-(lp['w0'] + jnp.tanh(wd) @ lp['w_w2'])) - 0.5
    a = jax.nn.sigmoid(lp['a0'] + ad @ lp['w_a2'])
    g = jax.nn.sigmoid(gd) @ lp['w_g2']

    def heads(t):
        return t.reshape(Bn, T, B_HEADS, B_HEAD).astype(jnp.float32)

    kk = heads(k * lp['k_k'])
    kk = kk / jnp.maximum(jnp.sqrt(jnp.sum(kk * kk, axis=-1, keepdims=True)), 1e-12)
    k = k * (1 + (a - 1) * lp['k_a'])
    r_h, k_h, v_h, a_h = heads(r), heads(k), heads(v), heads(a)
    decay = jnp.exp(-jnp.exp(heads(w)))
    y, s_last = wkv_scan(r_h, decay, k_h, v_h, -kk, kk * a_h, wkv0.astype(jnp.float32))
    mu = jnp.mean(y, axis=-1, keepdims=True)
    var = jnp.mean(jnp.square(y - mu), axis=-1, keepdims=True)
    y = ((y - mu) * lax.rsqrt(var + GN_EPS)).reshape(Bn, T, B_WIDTH)
    y = y * lp['ln_x_w'].astype(jnp.float32) + lp['ln_x_b'].astype(jnp.float32)
    bonus = jnp.sum(r_h * k_h * lp['r_k'].astype(jnp.float32), axis=-1, keepdims=True) * v_h
    y = (y + bonus.reshape(Bn, T, B_WIDTH)) * g.astype(jnp.float32)
    return y.astype(p.dtype), s_last.astype(wkv0.dtype), p[:, -1]


def conv_ffn(h, conv_prev, w_up, w_conv, w_down):
    T = h.shape[1]
    u = h @ w_up
    ext = jnp.concatenate([conv_prev.astype(u.dtype), u], axis=1)
    z = ext[:, 0:T] * w_conv[0]
    for j in range(1, FFN_CONV):
        z = z + ext[:, j:j + T] * w_conv[j]
    gate, val = jnp.split(z, 2, axis=-1)
    out = (jax.nn.silu(gate) * val) @ w_down
    return out, ext[:, -(FFN_CONV - 1):]


def layer_forward(x, c, lp, layer_idx, attn_fn, shift_prev, wkv0, conv_prev):
    Bn, T = x.shape[0], x.shape[1]
    mod = (jax.nn.silu(c) @ lp['w_ada'] + lp['b_ada'])[:, None, :]
    sh_m, sc_m, gt_m, sh_f, sc_f, gt_f = jnp.split(mod, 6, axis=-1)
    h = rms_norm(x, lp['g_pre_mix']) * (1 + sc_m) + sh_m
    proj = h @ lp['w_in']
    q = proj[..., :A_WIDTH].reshape(Bn, T, A_HEADS, 2, A_DK)
    k = proj[..., A_WIDTH:2 * A_WIDTH].reshape(Bn, T, A_HEADS, 2, A_DK)
    v = proj[..., 2 * A_WIDTH:3 * A_WIDTH].reshape(Bn, T, A_HEADS, A_DV)
    lam_init = 0.8 - 0.6 * math.exp(-0.3 * layer_idx)
    f32 = jnp.float32
    lam = (jnp.exp(jnp.sum(lp['lam_q1'].astype(f32) * lp['lam_k1'].astype(f32)))
           - jnp.exp(jnp.sum(lp['lam_q2'].astype(f32) * lp['lam_k2'].astype(f32))) + lam_init)
    o_a = attn_fn(q, k, v, lam)
    o_a = (rms_norm(o_a, lp['g_subln']) * (1 - lam_init)).reshape(Bn, T, A_WIDTH)
    o_b, wkv_last, shift_last = rwkv7_mix(proj[..., 3 * A_WIDTH:], shift_prev, wkv0, lp)
    mix = jnp.concatenate([o_a, o_b.astype(o_a.dtype)], axis=-1) @ lp['w_out']
    x = x + gt_m * rms_norm(mix, lp['g_post_mix'])
    h = rms_norm(x, lp['g_pre_ffn']) * (1 + sc_f) + sh_f
    f, conv_last = conv_ffn(h, conv_prev, lp['w_up'], lp['w_conv_ffn'], lp['w_down'])
    x = x + gt_f * rms_norm(f, lp['g_post_ffn'])
    return x, k.reshape(Bn, T, A_HEADS, 2 * A_DK), v, wkv_last, shift_last, conv_last


def setup_inputs(seed: int = 0) -> dict:
    key = jax.random.key(seed)
    ks = iter(jax.random.split(key, 40))
    L, D = DEPTH, D_MODEL

    def nrm(shape, s=1.0):
        return s * jax.random.normal(next(ks), shape, jnp.float32)

    def unif(shape, lo, hi):
        return jax.random.uniform(next(ks), shape, jnp.float32, lo, hi)

    return {
        'x_prompt': nrm((BATCH, SEQ, D)),
        'x_sample': nrm((DEC_BATCH, DEC_SEQ, D)),
        'c_prompt': nrm((BATCH, D)),
        'c_sample': nrm((DEC_BATCH, D)),
        'cache_k': nrm((L, DEC_BATCH, PAST_LEN, A_HEADS, 2 * A_DK)),
        'cache_v': nrm((L, DEC_BATCH, PAST_LEN, A_HEADS, A_DV)),
        'state_wkv': nrm((L, DEC_BATCH, B_HEADS, B_HEAD, B_HEAD), 0.5),
        'state_shift': nrm((L, DEC_BATCH, RW_COLS)),
        'state_ffn_conv': nrm((L, DEC_BATCH, FFN_CONV - 1, 2 * D_FF)),
        'w_ada': nrm((L, D, 6 * D), D ** -0.5),
        'b_ada': nrm((L, 6 * D), 0.01),
        'g_pre_mix': 1.0 + nrm((L, D), 0.01),
        'g_post_mix': 1.0 + nrm((L, D), 0.01),
        'g_pre_ffn': 1.0 + nrm((L, D), 0.01),
        'g_post_ffn': 1.0 + nrm((L, D), 0.01),
        'w_in': nrm((L, D, IN_COLS), D ** -0.5),
        'lam_q1': nrm((L, A_DK), 0.1),
        'lam_k1': nrm((L, A_DK), 0.1),
        'lam_q2': nrm((L, A_DK), 0.1),
        'lam_k2': nrm((L, A_DK), 0.1),
        'g_subln': 1.0 + nrm((L, A_DV), 0.01),
        'mu_shift': unif((L, RW_COLS), 0.0, 1.0),
        'w0': unif((L, B_WIDTH), -3.0, 1.0),
        'w_w2': nrm((L, W_RANK, B_WIDTH), 0.1 * W_RANK ** -0.5),
        'a0': nrm((L, B_WIDTH), 0.1),
        'w_a2': nrm((L, A_RANK, B_WIDTH), 0.1 * A_RANK ** -0.5),
        'w_g2': nrm((L, G_RANK, B_WIDTH), G_RANK ** -0.5),
        'k_k': 0.85 + nrm((L, B_WIDTH), 0.05),
        'k_a': 1.0 + nrm((L, B_WIDTH), 0.05),
        'r_k': nrm((L, B_HEADS, B_HEAD), 0.1),
        'ln_x_w': 1.0 + nrm((L, B_WIDTH), 0.01),
        'ln_x_b': nrm((L, B_WIDTH), 0.01),
        'w_out': nrm((L, D, D), D ** -0.5),
        'w_up': nrm((L, D, 2 * D_FF), D ** -0.5),
        'w_conv_ffn': nrm((L, FFN_CONV, 2 * D_FF), FFN_CONV ** -0.5),
        'w_down': nrm((L, D_FF, D), D_FF ** -0.5),
    }


def reference(x_prompt, x_sample, c_prompt, c_sample, cache_k, cache_v, state_wkv, state_shift,
              state_ffn_conv, w_ada, b_ada, g_pre_mix, g_post_mix, g_pre_ffn, g_post_ffn, w_in,
              lam_q1, lam_k1, lam_q2, lam_k2, g_subln, mu_shift, w0, w_w2, a0, w_a2, w_g2, k_k, k_a,
              r_k, ln_x_w, ln_x_b, w_out, w_up, w_conv_ffn, w_down):
    yp, ys = x_prompt, x_sample
    bp = x_prompt.shape[0]
    kp_l, vp_l, sp_l, shp_l, cp_l = [], [], [], [], []
    ks_l, vs_l, ss_l, shs_l, cs_l = [], [], [], [], []
    for l in range(DEPTH):
        lp = {'w_ada': w_ada[l], 'b_ada': b_ada[l], 'g_pre_mix': g_pre_mix[l], 'g_post_mix': g_post_mix[l],
              'g_pre_ffn': g_pre_ffn[l], 'g_post_ffn': g_post_ffn[l], 'w_in': w_in[l],
              'lam_q1': lam_q1[l], 'lam_k1': lam_k1[l], 'lam_q2': lam_q2[l], 'lam_k2': lam_k2[l],
              'g_subln': g_subln[l], 'mu_shift': mu_shift[l], 'w0': w0[l], 'w_w2': w_w2[l], 'a0': a0[l],
              'w_a2': w_a2[l], 'w_g2': w_g2[l], 'k_k': k_k[l], 'k_a': k_a[l], 'r_k': r_k[l],
              'ln_x_w': ln_x_w[l], 'ln_x_b': ln_x_b[l], 'w_out': w_out[l], 'w_up': w_up[l],
              'w_conv_ffn': w_conv_ffn[l], 'w_down': w_down[l]}
        yp, kp, vp, sp, shp, cp = layer_forward(
            yp, c_prompt, lp, l, prompt_attention,
            jnp.zeros((bp, RW_COLS), x_prompt.dtype),
            jnp.zeros((bp, B_HEADS, B_HEAD, B_HEAD), state_wkv.dtype),
            jnp.zeros((bp, FFN_CONV - 1, 2 * D_FF), x_prompt.dtype))
        attn_s = functools.partial(sample_attention, cache_k=cache_k[l], cache_v=cache_v[l])
        ys, ksm, vsm, ssm, shs, csm = layer_forward(
            ys, c_sample, lp, l, attn_s, state_shift[l], state_wkv[l], state_ffn_conv[l])
        kp_l.append(kp); vp_l.append(vp); sp_l.append(sp); shp_l.append(shp); cp_l.append(cp)
        ks_l.append(ksm); vs_l.append(vsm); ss_l.append(ssm); shs_l.append(shs); cs_l.append(csm)
    return (yp, ys,
            jnp.stack(kp_l), jnp.stack(vp_l), jnp.stack(sp_l), jnp.stack(shp_l), jnp.stack(cp_l),
            jnp.stack(ks_l), jnp.stack(vs_l), jnp.stack(ss_l), jnp.stack(shs_l), jnp.stack(cs_l))
```

```python
import numpy as np
import concourse.bass as bass
import concourse.mybir as mybir
from concourse.bass_utils import run_bass_kernel_spmd
from contextlib import ExitStack
import math

F32 = mybir.dt.float32; BF16 = mybir.dt.bfloat16
AF = mybir.ActivationFunctionType; ALU = mybir.AluOpType; AX = mybir.AxisListType

D = 2048; NSEQ = 4096; PRE = 2944; HALO0 = 2944; OWN0 = 3072; NS = 64; NTOK = 4160
NREG = NTOK - HALO0
INC = 6400; DFF = 5632; RWC = 3328
EPS = 1e-6; GN_EPS = 64e-5
LAM_INIT = 0.8 - 0.6 * math.exp(0.0)
NEG = -30000.0


class Buf:
    __slots__ = ("name", "w", "r")
    def __init__(self, name):
        self.name = name; self.w = None; self.r = {}


class Sched:
    def __init__(self, nc):
        self.nc = nc
        self.eng = {"pe": nc.tensor, "act": nc.scalar, "dve": nc.vector, "pool": nc.gpsimd, "sp": nc.sync}
        self.sems = {}; self.cnt = {}
        self.seen = {e: {} for e in self.eng}
        self._ctx = []
        self.n_inst = 0
    def new_sem(self, key):
        cm = self.nc.semaphore("s%d" % len(self.sems))
        h = cm.__enter__(); self._ctx.append(cm)
        self.sems[key] = h; self.cnt[key] = 0
    def _deps(self, reads, writes):
        deps = []
        for b in reads:
            if b.w is not None: deps.append(b.w)
        for b in writes:
            if b.w is not None: deps.append(b.w)
            deps.extend(b.r.items())
        return deps
    def _wait(self, e, deps):
        need = {}
        for k, t in deps:
            if e == "pe" and k == "e:pe": continue
            if t > need.get(k, 0): need[k] = t
        for k, t in need.items():
            if self.seen[e].get(k, 0) >= t: continue
            self.eng[e].wait_ge(self.sems[k], t)
            self.seen[e][k] = t
    def _mark(self, k, tick, reads, writes):
        for b in reads:
            if b.r.get(k, 0) < tick: b.r[k] = tick
        for b in writes:
            b.w = (k, tick); b.r = {}
    def op(self, e, fn, reads=(), writes=(), inc=True):
        self._wait(e, self._deps(reads, writes))
        ins = fn(self.eng[e])
        self.n_inst += 1
        k = "e:" + e
        if k not in self.sems: self.new_sem(k)
        if inc:
            self.cnt[k] += 1
            ins.then_inc(self.sems[k], 1)
            tick = self.cnt[k]
        else:
            tick = self.cnt[k] + 1
        self._mark(k, tick, reads, writes)
        return ins
    def dma(self, q, out, in_, reads=(), writes=(), semkey=None, **kw):
        self._wait(q, self._deps(reads, writes))
        ins = self.eng[q].dma_start(out=out, in_=in_, **kw)
        self.n_inst += 1
        if reads:
            semkey = "st_" + reads[0].name
        k = "d:" + semkey
        if k not in self.sems: self.new_sem(k)
        self.cnt[k] += 16
        ins.then_inc(self.sems[k], 16)
        self._mark(k, self.cnt[k], reads, writes)
        return ins
    def barrier(self):
        for e in self.eng:
            for k, c in self.cnt.items():
                if c > 0 and self.seen[e].get(k, 0) < c:
                    self.eng[e].wait_ge(self.sems[k], c)
                    self.seen[e][k] = c
    def close(self):
        for cm in reversed(self._ctx): cm.__exit__(None, None, None)


class Ctx:
    pass


def build(dbg=0):
    nc = bass.Bass("TRN2", target_bir_lowering=False)
    S = Sched(nc)
    K = Ctx(); K.nc = nc; K.S = S
    ins = {}; outs = {}
    def din(name, shape):
        ins[name] = nc.dram_tensor(name, list(shape), F32, kind="ExternalInput").ap(); return ins[name]
    def dout(name, shape):
        outs[name] = nc.dram_tensor(name, list(shape), F32, kind="ExternalOutput").ap(); return outs[name]
    def dscr(name, shape, dt):
        return nc.dram_tensor(name, list(shape), dt, kind="Internal").ap()
    xT = din("xT", [D, NTOK]); valid = din("valid", [128, NTOK]); kbias = din("kbias", [128, 24])
    x_tm = din("x_tm", [NREG, D]); cT = din("cT", [D, 5])
    cache_kT = din("cache_kT", [4, 8, 128, 4096]); cache_v = din("cache_v", [4, 4096, 8, 128])
    st_wkv = din("st_wkv", [4, 16, 64, 64]); st_shift = din("st_shift", [128, 26, 4]); st_conv = din("st_conv", [128, 88, 4, 2])
    w_ada = din("w_ada", [D, 6 * D]); b_ada = din("b_ada", [6 * D])
    g_pre_mix = din("g_pre_mix", [D]); g_post_mix = din("g_post_mix", [D]); g_pre_ffn = din("g_pre_ffn", [D]); g_post_ffn = din("g_post_ffn", [D])
    w_in = din("w_in", [D, INC]); lamv = din("lamv", [4, 64]); g_subln = din("g_subln", [128])
    mu_shift = din("mu_shift", [RWC]); w0 = din("w0", [1024]); w_w2 = din("w_w2", [64, 1024]); a0 = din("a0", [1024])
    w_a2 = din("w_a2", [64, 1024]); w_g2 = din("w_g2", [128, 1024]); k_k = din("k_k", [1024]); k_a = din("k_a", [1024])
    r_k = din("r_k", [1024]); ln_x_w = din("ln_x_w", [1024]); ln_x_b = din("ln_x_b", [1024])
    w_out = din("w_out", [D, D]); w_up = din("w_up", [D, 2 * DFF]); w_conv = din("w_conv", [3, 2 * DFF]); w_down = din("w_down", [DFF, D])
    cmask = din("cmask", [128, 5, 128])
    y_o = dout("y_o", [NREG - 128, D])
    k_o = dout("k_o", [NREG - 128, 1024]); v_o = dout("v_o", [NREG - 128, 1024])
    wkv_p = dout("wkv_p", [16, 64, 64]); wkv_s = dout("wkv_s", [4, 16, 64, 64])
    shift_p = dout("shift_p", [128, 26]); shift_s = dout("shift_s", [128, 26, 4])
    conv_p = dout("conv_p", [128, 88, 2]); conv_s = dout("conv_s", [128, 88, 4, 2])
    w_in_b = dscr("w_in_b", [D, INC], BF16); w_out_b = dscr("w_out_b", [D, D], BF16)
    w_up_b = dscr("w_up_b", [D, 2 * DFF], BF16); w_down_b = dscr("w_down_b", [DFF, D], BF16)
    rows_d = dscr("rows_d", [4, 5, D], F32)
    qT_d = dscr("qT_d", [1024, NREG], BF16); kT_d = dscr("kT_d", [1024, NTOK], BF16)
    v_d = dscr("v_d", [NTOK, 1024], F32); pT_d = dscr("pT_d", [RWC, NTOK], F32)
    oT_d = dscr("oT_d", [D, NREG], BF16)
    x1_d = dscr("x1_d", [NREG, D], F32); h2T_d = dscr("h2T_d", [D, NREG], BF16)
    K.__dict__.update(locals())

    es_all = ExitStack()
    uid = [0]
    def sbt(es, name, shape, dt):
        uid[0] += 1
        return es.enter_context(nc.sbuf_tensor("%s_%d" % (name, uid[0]), list(shape), dt))
    def pst(es, name, shape, dt):
        uid[0] += 1
        return es.enter_context(nc.psum_tensor("%s_%d" % (name, uid[0]), list(shape), dt))
    K.sbt = sbt; K.pst = pst
    nc_allow = nc.allow_non_contiguous_dma(reason="small param layouts")
    nc_allow.__enter__()

    cm_f = sbt(es_all, "cm_f", [128, 5, 128], F32); cm_b = sbt(es_all, "cm_b", [128, 5, 128], BF16)
    B_cm = Buf("cm")
    S.dma("sp", cm_f[:, :, :], cmask[:, :, :], writes=[B_cm], semkey="cm_f")
    S.dma("pool", cm_b[:, :, :], cmask[:, :, :], writes=[B_cm], semkey="cm_b")
    ones_b = sbt(es_all, "ones_b", [128, 128], BF16); epsc = sbt(es_all, "epsc", [128, 4], F32)
    B_const = Buf("const")
    S.op("dve", lambda e: e.memset(ones_b[:, :], 1.0), writes=[B_const])
    S.op("dve", lambda e: e.memset(epsc[:, 0:1], EPS), writes=[B_const])
    S.op("dve", lambda e: e.memset(epsc[:, 1:2], GN_EPS), writes=[B_const])
    S.op("dve", lambda e: e.memset(epsc[:, 2:3], 1e-24), writes=[B_const])
    S.op("dve", lambda e: e.memset(epsc[:, 3:4], 0.0), writes=[B_const])
    Gm = sbt(es_all, "Gm", [128, 16, 5], F32); Shm = sbt(es_all, "Shm", [128, 16, 5], F32)
    B_G = Buf("G")
    K.__dict__.update(locals())

    K.B_wcast = Buf("wcast")
    phase_cast(K, "in")
    phase0(K)
    phase_cast(K, "rest")
    phase1(K)
    S.barrier()
    if dbg in (0, -1):
        phase3(K)
        S.barrier()
    if dbg == -1:
        phase4(K)
        S.barrier()
    S.barrier()
    es_all.close()
    nc_allow.__exit__(None, None, None)
    S.close()
    return nc, S


def phase_cast(K, which):
    S = K.S
    lst = {"in": ((K.w_in_b, K.w_in, D),), "rest": ((K.w_out_b, K.w_out, D), (K.w_up_b, K.w_up, D), (K.w_down_b, K.w_down, DFF))}[which]
    for (dst, src, rows) in lst:
        for r0 in range(0, rows, 128):
            S.dma("pool", dst[r0:r0 + 128, :], src[r0:r0 + 128, :], writes=[K.B_wcast], semkey="wcast_" + which)


def phase0(K):
    S = K.S; nc = K.nc
    with ExitStack() as es:
        sbt = lambda n, s, d: K.sbt(es, n, s, d); pst = lambda n, s, d: K.pst(es, n, s, d)
        csb = sbt("csb", [128, 16, 5], F32); sT = sbt("sT", [128, 16, 5], F32)
        bT = sbt("bT", [128, 96], F32); modT = sbt("modT", [128, 96, 5], F32)
        wa = [sbt("wa%d" % i, [128, 16, 512], F32) for i in range(2)]
        psA = [pst("psA%d" % i, [128, 8], F32) for i in range(2)]
        psB = [pst("psB%d" % i, [8, 512], F32) for i in range(2)]
        gv = sbt("gv", [128, 4, 16], F32); drow = sbt("drow", [5, 4, D], F32)
        bg = [sbt("bg%d" % i, [5, 512], F32) for i in range(2)]; gg = [sbt("gg%d" % i, [5, 512], F32) for i in range(2)]
        tmpr = sbt("tmpr", [5, 512], F32)
        B_c = Buf("c"); B_s = Buf("sT"); B_b = Buf("bT"); B_mod = Buf("modT")
        B_wa = [Buf("wa0"), Buf("wa1")]; B_pA = [Buf("pA0"), Buf("pA1")]; B_pB = [Buf("pB0"), Buf("pB1")]
        B_gv = Buf("gv"); B_dr = Buf("drow"); B_rows = Buf("rows_d"); B_bg = [Buf("bg0"), Buf("bg1")]; B_gg = [Buf("gg0"), Buf("gg1")]; B_tmp = Buf("tmpr")
        S.dma("sp", csb[:, :, :], K.cT.rearrange("(c p) r -> p c r", p=128), writes=[B_c], semkey="csb")
        S.dma("sp", bT[:, :], K.b_ada.rearrange("(c p) -> p c", p=128), writes=[B_b], semkey="bT")
        gl = (K.g_pre_mix, K.g_post_mix, K.g_pre_ffn, K.g_post_ffn)
        for i, g in enumerate(gl):
            S.dma("sp", gv[:, i, :], g.rearrange("(c p) -> p c", p=128), writes=[B_gv], semkey="gv")
        S.op("act", lambda e: e.activation(sT[:, :, :], csb[:, :, :], AF.Silu), reads=[B_c], writes=[B_s])
        for g in range(24):
            w = wa[g % 2]; Bw = B_wa[g % 2]
            S.dma("sp", w[:, :, :], K.w_ada[:, g * 512:(g + 1) * 512].rearrange("(c p) n -> p c n", p=128), writes=[Bw], semkey="wa%d" % (g % 2))
            for j in range(4):
                ch = g * 4 + j; pa = psA[ch % 2]; Bp = B_pA[ch % 2]
                for k in range(16):
                    S.op("pe", lambda e, k=k, j=j, pa=pa: e.matmul(pa[:, 0:5], w[:, k, j * 128:(j + 1) * 128], sT[:, k, :], start=(k == 0), stop=(k == 15)),
                         reads=[Bw, B_s], writes=[Bp], inc=(k == 15))
                S.op("dve", lambda e, pa=pa, ch=ch: e.tensor_scalar(modT[:, ch, :], pa[:, 0:5], bT[:, ch:ch + 1], None, ALU.add), reads=[Bp, B_b], writes=[B_mod])
            sec = g // 4; off = (g % 4) * 512
            if sec < 2: continue
            pb = psB[g % 2]; Bp = B_pB[g % 2]
            S.dma("sp", bg[g % 2][:, :], K.b_ada[g * 512:(g + 1) * 512].partition_broadcast(5), writes=[B_bg[g % 2]], semkey="bg%d" % (g % 2))
            gsel = {2: 1, 4: 2, 5: 3}.get(sec)
            if gsel is not None:
                S.dma("sp", gg[g % 2][:, :], gl[gsel][off:off + 512].partition_broadcast(5), writes=[B_gg[g % 2]], semkey="gg%d" % (g % 2))
            for k in range(16):
                S.op("pe", lambda e, k=k, pb=pb: e.matmul(pb[0:5, :], sT[:, k, :], w[:, k, :], start=(k == 0), stop=(k == 15)),
                     reads=[Bw, B_s], writes=[Bp], inc=(k == 15))
            S.op("dve", lambda e, pb=pb, g=g: e.tensor_tensor(tmpr[:, :], pb[0:5, :], bg[g % 2][:, :], ALU.add), reads=[Bp, B_bg[g % 2]], writes=[B_tmp])
            if sec == 2:
                S.op("dve", lambda e, g=g, off=off: e.tensor_tensor(drow[:, 0, off:off + 512], tmpr[:, :], gg[g % 2][:, :], ALU.mult), reads=[B_tmp, B_gg[g % 2]], writes=[B_dr])
            elif sec == 3:
                S.op("dve", lambda e, g=g, off=off: e.tensor_copy(drow[:, 2, off:off + 512], tmpr[:, :]), reads=[B_tmp], writes=[B_dr])
            elif sec == 4:
                S.op("dve", lambda e, g=g, off=off: e.scalar_tensor_tensor(drow[:, 1, off:off + 512], tmpr[:, :], 1.0, gg[g % 2][:, :], ALU.add, ALU.mult), reads=[B_tmp, B_gg[g % 2]], writes=[B_dr])
            else:
                S.op("dve", lambda e, g=g, off=off: e.tensor_tensor(drow[:, 3, off:off + 512], tmpr[:, :], gg[g % 2][:, :], ALU.mult), reads=[B_tmp, B_gg[g % 2]], writes=[B_dr])
        S.op("dve", lambda e: e.scalar_tensor_tensor(K.Gm[:, :, :], modT[:, 16:32, :], 1.0, gv[:, 0, :].unsqueeze(2).to_broadcast([128, 16, 5]), ALU.add, ALU.mult),
             reads=[B_mod, B_gv], writes=[K.B_G])
        S.op("dve", lambda e: e.tensor_copy(K.Shm[:, :, :], modT[:, 0:16, :]), reads=[B_mod], writes=[K.B_G])
        S.dma("sp", K.rows_d.rearrange("k r d -> r k d"), drow[:, :, :], reads=[B_dr], writes=[B_rows], semkey="rows_d")
        S.barrier()


def phase1(K):
    S = K.S; nc = K.nc
    with ExitStack() as es:
        sbt = lambda n, s, d: K.sbt(es, n, s, d); pst = lambda n, s, d: K.pst(es, n, s, d)
        hT = sbt("hT", [128, 16, 2112], BF16); B_h = Buf("hT")
        xt = sbt("xt", [128, 16, 512], F32); B_x = Buf("xt")
        sq = sbt("sq", [128, 16, 512], BF16); B_sq = Buf("sq")
        vl = sbt("vl", [128, 512], F32); B_vl = Buf("vl")
        rs = sbt("rs", [128, 512], F32); B_rs = Buf("rs")
        t1 = sbt("t1", [128, 512], F32); B_t1 = Buf("t1")
        wf = [sbt("wf%d" % i, [128, 16, 128], BF16) for i in range(2)]; B_wf = [Buf("wf0"), Buf("wf1")]
        wt = [sbt("wt%d" % i, [128, 16, 512], BF16) for i in range(2)]; B_wt = [Buf("wt0"), Buf("wt1")]
        stg_b = [sbt("stgb%d" % i, [128, 2112], BF16) for i in range(2)]; B_sb = [Buf("stgb0"), Buf("stgb1")]
        stg_f = [sbt("stgf%d" % i, [128, 2112], F32) for i in range(2)]; B_sf = [Buf("stgf0"), Buf("stgf1")]
        stg_t = [sbt("stgt%d" % i, [128, 512], F32) for i in range(2)]; B_st = [Buf("stgt0"), Buf("stgt1")]
        ps_ss = pst("ps_ss", [128, 512], F32); B_pss = Buf("ps_ss")
        ps = [pst("psm%d" % i, [128, 512], F32) for i in range(4)]; B_ps = [Buf("psm%d" % i) for i in range(4)]
        B_scr = {n: Buf(n) for n in ("qT", "kT", "pT", "v", "out")}
        psi = [0]; evi = [0]
        w_in_v = K.w_in_b.rearrange("(c p) n -> p c n", p=128)
        for sb_i, (tb0, tb1) in enumerate(((0, 2048), (2048, NTOK))):
            nt = tb1 - tb0
            for t0 in range(tb0, tb1, 512):
                n = min(512, tb1 - t0)
                S.dma("sp", xt[:, :, 0:n], K.xT[:, t0:t0 + n].rearrange("(c p) t -> p c t", p=128), writes=[B_x], semkey="xt")
                S.dma("sp", vl[:, 0:n], K.valid[:, t0:t0 + n], writes=[B_vl], semkey="vl")
                S.op("act", lambda e: e.activation(sq[:, :, 0:n], xt[:, :, 0:n], AF.Square), reads=[B_x], writes=[B_sq])
                for k in range(16):
                    S.op("pe", lambda e, k=k: e.matmul(ps_ss[:, 0:n], K.ones_b[:, :], sq[:, k, 0:n], start=(k == 0), stop=(k == 15)),
                         reads=[B_sq, K.B_const], writes=[B_pss], inc=(k == 15))
                S.op("act", lambda e: e.activation(rs[:, 0:n], ps_ss[:, 0:n], AF.Sqrt, bias=K.epsc[:, 0:1], scale=1.0 / D), reads=[B_pss, K.B_const], writes=[B_rs])
                S.op("dve", lambda e: e.reciprocal(rs[:, 0:n], rs[:, 0:n]), reads=[B_rs], writes=[B_rs])
                S.op("dve", lambda e: e.tensor_tensor(rs[:, 0:n], rs[:, 0:n], vl[:, 0:n], ALU.mult), reads=[B_rs, B_vl], writes=[B_rs])
                if t0 + n <= NSEQ:
                    groups = [(0, n, 0)]
                else:
                    groups = [(s * 16, 16, 1 + s) for s in range(4)]
                for k in range(16):
                    for (c0, cn, r) in groups:
                        eng = "dve" if (k % 2 == 0) else "pool"
                        S.op("dve", lambda e, k=k, c0=c0, cn=cn, r=r: e.scalar_tensor_tensor(t1[:, c0:c0 + cn], xt[:, k, c0:c0 + cn], K.Gm[:, k, r:r + 1], rs[:, c0:c0 + cn], ALU.mult, ALU.mult),
                             reads=[B_x, K.B_G, B_rs], writes=[B_t1])
                        S.op("dve", lambda e, k=k, c0=c0, cn=cn, r=r: e.scalar_tensor_tensor(hT[:, k, t0 - tb0 + c0:t0 - tb0 + c0 + cn], vl[:, c0:c0 + cn], K.Shm[:, k, r:r + 1], t1[:, c0:c0 + cn], ALU.mult, ALU.add),
                             reads=[B_vl, K.B_G, B_t1], writes=[B_h])
            chunks = [("kT", 1024 + 128 * i, i, tb0) for i in range(8)] + [("pT", 3072 + 128 * i, i, tb0) for i in range(26)]
            if sb_i == 1:
                chunks += [("qT", 128 * i, i, HALO0) for i in range(8)]
            for ci, (kind, col0, idx, tlo) in enumerate(chunks):
                w = wf[ci % 2]; Bw = B_wf[ci % 2]
                S.dma("sp", w[:, :, :], w_in_v[:, :, col0:col0 + 128], writes=[Bw], semkey="wf%d" % (ci % 2))
                isf = (kind == "pT")
                stg = (stg_f if isf else stg_b)[ci % 2]; Bs = (B_sf if isf else B_sb)[ci % 2]
                for t0 in range(tlo, tb1, 512):
                    n = min(512, tb1 - t0); o = t0 - tb0
                    p = ps[psi[0] % 4]; Bp = B_ps[psi[0] % 4]; psi[0] += 1
                    for k in range(16):
                        S.op("pe", lambda e, k=k, p=p, o=o, n=n: e.matmul(p[:, 0:n], w[:, k, :], hT[:, k, o:o + n], start=(k == 0), stop=(k == 15)),
                             reads=[Bw, B_h], writes=[Bp], inc=(k == 15))
                    ev = "act" if evi[0] % 2 == 0 else "dve"; evi[0] += 1
                    if ev == "act":
                        S.op("act", lambda e, p=p, o=o, n=n: e.copy(stg[:, o:o + n], p[:, 0:n]), reads=[Bp], writes=[Bs])
                    else:
                        S.op("dve", lambda e, p=p, o=o, n=n: e.tensor_copy(stg[:, o:o + n], p[:, 0:n]), reads=[Bp], writes=[Bs])
                lo = tlo - tb0
                if kind == "kT":
                    S.dma("pool", K.kT_d[idx * 128:(idx + 1) * 128, tlo:tb1], stg[:, lo:nt], reads=[Bs], writes=[B_scr["kT"]], semkey="st_kT%d" % (ci % 2))
                elif kind == "qT":
                    S.dma("pool", K.qT_d[idx * 128:(idx + 1) * 128, :], stg[:, lo:nt], reads=[Bs], writes=[B_scr["qT"]], semkey="st_qT%d" % (ci % 2))
                else:
                    S.dma("pool", K.pT_d[idx * 128:(idx + 1) * 128, tlo:tb1], stg[:, lo:nt], reads=[Bs], writes=[B_scr["pT"]], semkey="st_pT%d" % (ci % 2))
                    if sb_i == 1:
                        c = NSEQ - 1 - tb0
                        S.dma("pool", K.shift_p[:, idx:idx + 1], stg[:, c:c + 1], reads=[Bs], writes=[B_scr["out"]], semkey="st_out")
                        c = NSEQ - tb0
                        S.dma("pool", K.shift_s[:, idx, :], stg[:, c:c + 64].rearrange("p (s t) -> p s t", t=16)[:, :, 15], reads=[Bs], writes=[B_scr["out"]], semkey="st_out")
            tms = [("v", 2048 + 512 * g, g, tb0) for g in range(2)]
            if sb_i == 1:
                tms += [("k", 1024 + 512 * g, g, OWN0) for g in range(2)]
            for ci, (kind, col0, g, tlo) in enumerate(tms):
                w = wt[ci % 2]; Bw = B_wt[ci % 2]
                S.dma("sp", w[:, :, :], w_in_v[:, :, col0:col0 + 512], writes=[Bw], semkey="wt%d" % (ci % 2))
                for t0 in range(tlo, tb1, 128):
                    n = min(128, tb1 - t0); o = t0 - tb0
                    p = ps[psi[0] % 4]; Bp = B_ps[psi[0] % 4]; psi[0] += 1
                    for k in range(16):
                        S.op("pe", lambda e, k=k, p=p, o=o, n=n: e.matmul(p[0:n, :], hT[:, k, o:o + n], w[:, k, :], start=(k == 0), stop=(k == 15)),
                             reads=[Bw, B_h], writes=[Bp], inc=(k == 15))
                    st = stg_t[evi[0] % 2]; Bst = B_st[evi[0] % 2]
                    ev = "act" if evi[0] % 2 == 0 else "dve"; evi[0] += 1
                    if ev == "act":
                        S.op("act", lambda e, p=p, n=n, st=st: e.copy(st[0:n, :], p[0:n, :]), reads=[Bp], writes=[Bst])
                    else:
                        S.op("dve", lambda e, p=p, n=n, st=st: e.tensor_copy(st[0:n, :], p[0:n, :]), reads=[Bp], writes=[Bst])
                    if kind == "v":
                        S.dma("pool", K.v_d[t0:t0 + n, g * 512:(g + 1) * 512], st[0:n, :], reads=[Bst], writes=[B_scr["v"]], semkey="st_v")
                    if t0 >= OWN0:
                        dst = K.v_o if kind == "v" else K.k_o
                        S.dma("pool", dst[t0 - OWN0:t0 - OWN0 + n, g * 512:(g + 1) * 512], st[0:n, :], reads=[Bst], writes=[B_scr["out"]], semkey="st_out")
        S.barrier()


def phase3(K):
    S = K.S; nc = K.nc
    NMAX = 256; NCM = 4
    S_ = K.S
    class Emit:
        def __init__(self): self.prog = None
        def op(self, *a, **k):
            return getattr(S_, "op")(*a, **k)
        def dma(self, *a, **k):
            return getattr(S_, "dma")(*a, **k)
    EM = Emit()
    with ExitStack() as es:
        sbt = lambda n, s, d: K.sbt(es, n, s, d); pst = lambda n, s, d: K.pst(es, n, s, d)
        TS = [dict(), dict()]; BS = [dict(), dict()]
        def mk(name, shape, dt, shared=False):
            if shared:
                t = sbt(name, shape, dt); bb = Buf(name)
                for li in range(2):
                    TS[li][name] = t; BS[li][name] = bb
            else:
                for li in range(2):
                    TS[li][name] = sbt(name + "_l%d" % li, shape, dt); BS[li][name] = Buf(name + "_l%d" % li)
        T = TS[0]; B = BS[0]
        pv = sbt("pv", [128, 7, 8], F32); B_pv = Buf("pv")
        mu = sbt("mu", [128, 26], F32)
        ww2 = sbt("ww2", [128, 1024], BF16); wa2 = sbt("wa2", [128, 1024], BF16); wg2 = sbt("wg2", [128, 1024], BF16)
        for i, v in enumerate((K.w0, K.a0, K.k_k, K.k_a, K.r_k, K.ln_x_w, K.ln_x_b)):
            EM.dma("sp", pv[:, i, :], v.rearrange("(c p) -> p c", p=128), writes=[B_pv], semkey="pv")
        EM.dma("sp", mu[:, :], K.mu_shift.rearrange("(c p) -> p c", p=128), writes=[B_pv], semkey="pv")
        EM.dma("pool", ww2[0:64, :], K.w_w2[:, :], writes=[B_pv], semkey="pvc")
        EM.dma("pool", wa2[64:128, :], K.w_a2[:, :], writes=[B_pv], semkey="pvc")
        EM.dma("pool", wg2[:, :], K.w_g2[:, :], writes=[B_pv], semkey="pvc")
        segm = sbt("segm", [128, NMAX], F32); vrw = sbt("vrw", [128, 64], F32)
        EM.op("dve", lambda e: e.memset(segm[:, :], 1.0), writes=[B_pv])
        EM.op("dve", lambda e: e.memset(segm[:, :].rearrange("p (c t) -> p c t", t=64)[:, :, 0:1], 0.0), writes=[B_pv])
        EM.op("dve", lambda e: e.memset(vrw[:, 0:48], 0.0), writes=[B_pv])
        EM.op("dve", lambda e: e.memset(vrw[:, 48:64], 1.0), writes=[B_pv])
        for nm in ("pr_r", "pr_k", "pr_v", "pr_wa", "pr_gd"):
            mk(nm, [128, NMAX + 1], F32)
        for nm in ("dtmp", "xr", "xk", "xv", "xwa", "xgd", "sigw", "av", "gvv", "ld", "L", "eL", "enL", "eLm", "kk", "kk2", "rn", "kp", "tb", "tk", "bv", "yT", "yc", "ysq"):
            mk(nm, [128, NMAX], F32)
        mk("twa", [128, NMAX], BF16); mk("sg", [128, NMAX], BF16); mk("ob", [128, NMAX], BF16)
        mk("PC", [128, NCM], F32)
        mk("bdAR", [128, NCM, 256], BF16); mk("bdB", [128, NCM, 128], BF16); mk("bdK", [128, NCM, 128], BF16)
        mk("bdBh", [128, NCM, 128], BF16); mk("bdKh", [128, NCM, 128], BF16); mk("bdV", [128, NCM, 128], BF16)
        for li in range(2):
            for nm in ("bdAR", "bdB", "bdK", "bdBh", "bdKh", "bdV"):
                EM.op("pool", lambda e, nm=nm, li=li: e.memset(TS[li][nm][:, :, :], 0.0), writes=[BS[li][nm]])
        mk("S1", [128, NCM, 256], BF16); mk("S2", [128, NCM, 256], BF16)
        for nm in ("N0", "N1", "M0", "M1", "T0", "T1", "Vb", "Kh", "Bh", "At", "AV", "Dm", "XT", "Gm", "Qc"):
            mk(nm, [128, NCM, 128], BF16)
        mk("E", [128, NCM, 128], F32); mk("Hall", [128, NCM + 1, 128], BF16)
        mk("tmpH0", [128, 128], F32); mk("tmpH1", [128, 128], F32)
        mk("Hf", [128, 128], F32)
        mk("mask2", [128, 2, 256], BF16, True); mk("identg", [128, 4, 128], BF16, True); mk("mSLg", [128, 4, 128], BF16, True)
        for i in range(2):
            EM.op("pool", lambda e, i=i: e.tensor_copy(T["mask2"][:, i, :], K.cm_b[:, 1:3, :].rearrange("p a b -> p (a b)")), reads=[K.B_cm], writes=[B["mask2"]])
        for i in range(4):
            EM.op("pool", lambda e, i=i: e.tensor_copy(T["identg"][:, i, :], K.cm_b[:, 0, :]), reads=[K.B_cm], writes=[B["identg"]])
            EM.op("pool", lambda e, i=i: e.tensor_copy(T["mSLg"][:, i, :], K.cm_b[:, 3, :]), reads=[K.B_cm], writes=[B["mSLg"]])
        banks = [pst("pb%d" % i, [128, 512], F32) for i in range(7)]
        bankt = pst("pbt", [128, 1024], BF16)
        bkB = [Buf("bank%d" % i) for i in range(8)]
        rots = [[0], [0]]
        cnts = [{"z": 0, "inv": 0, "ev": 0}, {"z": 0, "inv": 0, "ev": 0}]
        B_oT = Buf("oT"); B_out = Buf("out3")
        ident = K.cm_b[:, 0, :]; mSU = K.cm_b[:, 1, :]; mIU = K.cm_b[:, 2, :]; mSL = K.cm_b[:, 3, :]
        onesf = K.cm_f[:, 4, :]
        C1 = -math.exp(-0.5)

        def v3(ap, lo, hi, nch):
            return ap[lo:hi, 0:nch * 64].rearrange("p (c t) -> p c t", t=64)

        def evac(dst, dstB, src, srcB, extra_reads=()):
            e = "act" if cnt["ev"] % 2 == 0 else "dve"; cnt["ev"] += 1
            if e == "act":
                EM.op("act", lambda en: en.copy(dst, src), reads=[srcB] + list(extra_reads), writes=[dstB])
            else:
                EM.op("dve", lambda en: en.tensor_copy(dst, src), reads=[srcB] + list(extra_reads), writes=[dstB])

        def mm(pname, lhsT, rhs, rb, start=True, stop=True):
            EM.op("pe", lambda e: e.matmul(P[pname], lhsT, rhs, start=start, stop=stop), reads=rb, writes=[BP[pname]], inc=stop)

        def rw_block(li, hp, t0, nch, with_y, samp=None):
            T = TS[li]; B = BS[li]; cnt = cnts[li]; rot = rots[li]
            P = {"pz0": banks[2 * li], "pz1": banks[2 * li + 1]}; BP = {"pz0": bkB[2 * li], "pz1": bkB[2 * li + 1]}
            def nbank():
                i = 2 * li + (rot[0] % 2); rot[0] += 1
                return banks[i], bkB[i]
            def evac(dst, dstB, src, srcB, extra_reads=()):
                e = "act" if cnt["ev"] % 4 != 3 else "dve"; cnt["ev"] += 1
                if e == "act":
                    yield EM.op("act", lambda en: en.copy(dst, src), reads=[srcB] + list(extra_reads), writes=[dstB])
                else:
                    yield EM.op("dve", lambda en: en.tensor_copy(dst, src), reads=[srcB] + list(extra_reads), writes=[dstB])
            n = nch * 64
            chs = {"pr_r": hp, "pr_k": 8 + hp, "pr_v": 16 + hp, "pr_wa": 24, "pr_gd": 25}
            for nm, ch in chs.items():
                t = T[nm]; rows = K.pT_d[ch * 128:(ch + 1) * 128, :]
                if samp is not None:
                    yield EM.op("pool", lambda e, t=t: e.memset(t[:, 0:48], 0.0), writes=[B[nm]])
                    yield EM.dma("sp", t[:, 48:49], K.st_shift[:, ch, samp:samp + 1], writes=[B[nm]], semkey="ld%d_" % li + nm)
                    yield EM.dma("sp", t[:, 49:65], rows[:, NSEQ + 16 * samp:NSEQ + 16 * samp + 16], writes=[B[nm]], semkey="ld%d_" % li + nm)
                elif t0 == 0:
                    yield EM.op("pool", lambda e, t=t: e.memset(t[:, 0:1], 0.0), writes=[B[nm]])
                    yield EM.dma("sp", t[:, 1:n + 1], rows[:, 0:n], writes=[B[nm]], semkey="ld%d_" % li + nm)
                else:
                    yield EM.dma("sp", t[:, 0:n + 1], rows[:, t0 - 1:t0 + n], writes=[B[nm]], semkey="ld%d_" % li + nm)
            for nm, xn, ch in (("pr_r", "xr", hp), ("pr_k", "xk", 8 + hp), ("pr_v", "xv", 16 + hp), ("pr_wa", "xwa", 24), ("pr_gd", "xgd", 25)):
                t = T[nm]
                yield EM.op("pool", lambda e, t=t: e.tensor_tensor(T["dtmp"][:, 0:n], t[:, 0:n], t[:, 1:n + 1], ALU.subtract), reads=[B[nm]], writes=[B["dtmp"]])
                yield EM.op("dve", lambda e, t=t, xn=xn, ch=ch: e.scalar_tensor_tensor(T[xn][:, 0:n], T["dtmp"][:, 0:n], mu[:, ch:ch + 1], t[:, 1:n + 1], ALU.mult, ALU.add),
                     reads=[B["dtmp"], B[nm], B_pv], writes=[B[xn]])
            if samp is not None:
                for xn in ("xr", "xk", "xv", "xwa", "xgd"):
                    yield EM.op("pool", lambda e, xn=xn: e.tensor_tensor(T[xn][:, 0:n], T[xn][:, 0:n], vrw[:, 0:n], ALU.mult), reads=[B[xn], B_pv], writes=[B[xn]])
            yield EM.op("act", lambda e: e.activation(T["twa"][0:64, 0:n], T["xwa"][0:64, 0:n], AF.Tanh), reads=[B["xwa"]], writes=[B["twa"]])
            yield EM.op("dve", lambda e: e.tensor_copy(T["twa"][64:128, 0:n], T["xwa"][64:128, 0:n]), reads=[B["xwa"]], writes=[B["twa"]])
            yield EM.op("act", lambda e: e.activation(T["sg"][:, 0:n], T["xgd"][:, 0:n], AF.Sigmoid), reads=[B["xgd"]], writes=[B["sg"]])
            cs = slice(hp * 128, (hp + 1) * 128)
            for c0 in range(0, n, 512):
                cn = min(512, n - c0)
                pz = "pz%d" % (cnt["z"] % 2); cnt["z"] += 1
                yield EM.op("pe", lambda e, pz=pz: e.matmul(P[pz][:, 0:cn], ww2[0:64, cs], T["twa"][0:64, c0:c0 + cn], start=True, stop=True), reads=[B["twa"], B_pv], writes=[BP[pz]])
                yield EM.op("act", lambda e, pz=pz: e.activation(T["sigw"][:, c0:c0 + cn], P[pz][:, 0:cn], AF.Sigmoid, bias=pv[:, 0, hp:hp + 1], scale=1.0), reads=[BP[pz], B_pv], writes=[B["sigw"]])
                pz = "pz%d" % (cnt["z"] % 2); cnt["z"] += 1
                yield EM.op("pe", lambda e, pz=pz: e.matmul(P[pz][:, 0:cn], wa2[64:128, cs], T["twa"][64:128, c0:c0 + cn], start=True, stop=True), reads=[B["twa"], B_pv], writes=[BP[pz]])
                yield EM.op("act", lambda e, pz=pz: e.activation(T["av"][:, c0:c0 + cn], P[pz][:, 0:cn], AF.Sigmoid, bias=pv[:, 1, hp:hp + 1], scale=1.0), reads=[BP[pz], B_pv], writes=[B["av"]])
                if with_y:
                    pz = "pz%d" % (cnt["z"] % 2); cnt["z"] += 1
                    yield EM.op("pe", lambda e, pz=pz: e.matmul(P[pz][:, 0:cn], wg2[:, cs], T["sg"][:, c0:c0 + cn], start=True, stop=True), reads=[B["sg"], B_pv], writes=[BP[pz]])
                    yield EM.op("dve", lambda e, pz=pz: e.tensor_copy(T["gvv"][:, c0:c0 + cn], P[pz][:, 0:cn]), reads=[BP[pz]], writes=[B["gvv"]])
            if samp is not None:
                yield EM.op("dve", lambda e: e.scalar_tensor_tensor(T["ld"][:, 0:n], T["sigw"][:, 0:n], C1, vrw[:, 0:n], ALU.mult, ALU.mult), reads=[B["sigw"], B_pv], writes=[B["ld"]])
            else:
                yield EM.op("dve", lambda e: e.tensor_scalar(T["ld"][:, 0:n], T["sigw"][:, 0:n], C1, None, ALU.mult), reads=[B["sigw"]], writes=[B["ld"]])
            yield EM.op("dve", lambda e: e.tensor_tensor_scan(T["L"][:, 0:n], segm[:, 0:n], T["ld"][:, 0:n], 0.0, ALU.mult, ALU.add), reads=[B["ld"], B_pv], writes=[B["L"]])
            yield EM.op("act", lambda e: e.activation(T["eL"][:, 0:n], T["L"][:, 0:n], AF.Exp), reads=[B["L"]], writes=[B["eL"]])
            yield EM.op("act", lambda e: e.activation(T["enL"][:, 0:n], T["L"][:, 0:n], AF.Exp, scale=-1.0), reads=[B["L"]], writes=[B["enL"]])
            yield EM.op("pool", lambda e: e.tensor_tensor(T["eLm"][:, 0:n], T["L"][:, 0:n], T["ld"][:, 0:n], ALU.subtract), reads=[B["L"], B["ld"]], writes=[B["eLm"]])
            yield EM.op("act", lambda e: e.activation(T["eLm"][:, 0:n], T["eLm"][:, 0:n], AF.Exp), reads=[B["eLm"]], writes=[B["eLm"]])
            yield EM.op("dve", lambda e: e.tensor_copy(T["PC"][:, 0:nch], T["eL"][:, 0:n].rearrange("p (c t) -> p c t", t=64)[:, :, 63]), reads=[B["eL"]], writes=[B["PC"]])
            yield EM.op("dve", lambda e: e.tensor_scalar(T["kk"][:, 0:n], T["xk"][:, 0:n], pv[:, 2, hp:hp + 1], None, ALU.mult), reads=[B["xk"], B_pv], writes=[B["kk"]])
            yield EM.op("pool", lambda e: e.tensor_tensor(T["kk2"][:, 0:n], T["kk"][:, 0:n], T["kk"][:, 0:n], ALU.mult), reads=[B["kk"]], writes=[B["kk2"]])
            for c0 in range(0, n, 512):
                cn = min(512, n - c0)
                pz = "pz%d" % (cnt["z"] % 2); cnt["z"] += 1
                yield EM.op("pe", lambda e, pz=pz: e.matmul(P[pz][:, 0:cn], onesf, T["kk2"][:, c0:c0 + cn], start=True, stop=True), reads=[B["kk2"], K.B_cm], writes=[BP[pz]])
                yield EM.op("act", lambda e, pz=pz: e.activation(T["rn"][:, c0:c0 + cn], P[pz][:, 0:cn], AF.Sqrt, bias=K.epsc[:, 2:3], scale=64.0), reads=[BP[pz], K.B_const], writes=[B["rn"]])
            yield EM.op("dve", lambda e: e.reciprocal(T["rn"][:, 0:n], T["rn"][:, 0:n]), reads=[B["rn"]], writes=[B["rn"]])
            yield EM.op("pool", lambda e: e.tensor_tensor(T["kk"][:, 0:n], T["kk"][:, 0:n], T["rn"][:, 0:n], ALU.mult), reads=[B["kk"], B["rn"]], writes=[B["kk"]])
            yield EM.op("dve", lambda e: e.tensor_scalar(T["kp"][:, 0:n], T["av"][:, 0:n], -1.0, pv[:, 3, hp:hp + 1], ALU.add, ALU.mult), reads=[B["av"], B_pv], writes=[B["kp"]])
            yield EM.op("dve", lambda e: e.scalar_tensor_tensor(T["kp"][:, 0:n], T["kp"][:, 0:n], 1.0, T["xk"][:, 0:n], ALU.add, ALU.mult), reads=[B["kp"], B["xk"]], writes=[B["kp"]])
            if with_y:
                yield EM.op("dve", lambda e: e.scalar_tensor_tensor(T["kk2"][:, 0:n], T["xr"][:, 0:n], pv[:, 4, hp:hp + 1], T["kp"][:, 0:n], ALU.mult, ALU.mult), reads=[B["xr"], B["kp"], B_pv], writes=[B["kk2"]])
                for c0 in range(0, n, 512):
                    cn = min(512, n - c0)
                    pz = "pz%d" % (cnt["z"] % 2); cnt["z"] += 1
                    yield EM.op("pe", lambda e, pz=pz: e.matmul(P[pz][:, 0:cn], onesf, T["kk2"][:, c0:c0 + cn], start=True, stop=True), reads=[B["kk2"], K.B_cm], writes=[BP[pz]])
                    yield EM.op("dve", lambda e, pz=pz: e.scalar_tensor_tensor(T["bv"][:, c0:c0 + cn], P[pz][:, 0:cn], 64.0, T["xv"][:, c0:c0 + cn], ALU.mult, ALU.mult), reads=[BP[pz], B["xv"]], writes=[B["bv"]])
            yield EM.op("pool", lambda e: e.tensor_tensor(T["tb"][:, 0:n], T["kk"][:, 0:n], T["av"][:, 0:n], ALU.mult), reads=[B["kk"], B["av"]], writes=[B["tb"]])
            yield EM.op("pool", lambda e: e.tensor_tensor(T["tb"][:, 0:n], T["tb"][:, 0:n], T["enL"][:, 0:n], ALU.mult), reads=[B["tb"], B["enL"]], writes=[B["tb"]])
            yield EM.op("pool", lambda e: e.tensor_tensor(T["tk"][:, 0:n], T["kp"][:, 0:n], T["enL"][:, 0:n], ALU.mult), reads=[B["kp"], B["enL"]], writes=[B["tk"]])
            for (lo, hi) in ((0, 64), (64, 128)):
                co = slice(lo, hi)
                pcb = T["PC"][lo:hi, 0:nch].unsqueeze(2).to_broadcast([64, nch, 64])
                yield EM.op("dve", lambda e, lo=lo, hi=hi: e.tensor_tensor(T["bdAR"][lo:hi, 0:nch, 128 + lo:128 + hi], v3(T["xr"], lo, hi, nch), v3(T["eL"], lo, hi, nch), ALU.mult), reads=[B["xr"], B["eL"]], writes=[B["bdAR"]])
                yield EM.op("dve", lambda e, lo=lo, hi=hi: e.scalar_tensor_tensor(T["bdAR"][lo:hi, 0:nch, lo:hi], v3(T["kk"], lo, hi, nch), -1.0, v3(T["eLm"], lo, hi, nch), ALU.mult, ALU.mult), reads=[B["kk"], B["eLm"]], writes=[B["bdAR"]])
                yield EM.op("pool", lambda e, lo=lo, hi=hi: e.tensor_copy(T["bdB"][lo:hi, 0:nch, lo:hi], v3(T["tb"], lo, hi, nch)), reads=[B["tb"]], writes=[B["bdB"]])
                yield EM.op("pool", lambda e, lo=lo, hi=hi: e.tensor_copy(T["bdK"][lo:hi, 0:nch, lo:hi], v3(T["tk"], lo, hi, nch)), reads=[B["tk"]], writes=[B["bdK"]])
                yield EM.op("pool", lambda e, lo=lo, hi=hi: e.tensor_copy(T["bdV"][lo:hi, 0:nch, lo:hi], v3(T["xv"], lo, hi, nch)), reads=[B["xv"]], writes=[B["bdV"]])
                yield EM.op("dve", lambda e, lo=lo, hi=hi, pcb=pcb: e.tensor_tensor(T["bdBh"][lo:hi, 0:nch, lo:hi], v3(T["tb"], lo, hi, nch), pcb, ALU.mult), reads=[B["tb"], B["PC"]], writes=[B["bdBh"]])
                yield EM.op("dve", lambda e, lo=lo, hi=hi, pcb=pcb: e.tensor_tensor(T["bdKh"][lo:hi, 0:nch, lo:hi], v3(T["tk"], lo, hi, nch), pcb, ALU.mult), reads=[B["tk"], B["PC"]], writes=[B["bdKh"]])
            def grp(dst, width, lhs_fn, rhs_fn, rb, post=None, gsz=None):
                gsz = gsz or (512 // width)
                for c0 in range(0, nch, gsz):
                    g = min(gsz, nch - c0)
                    bk, Bb = nbank()
                    for i in range(g):
                        yield EM.op("pe", lambda e, i=i, c=c0 + i: e.matmul(bk[:, i * width:(i + 1) * width], lhs_fn(c), rhs_fn(c), start=True, stop=True), reads=rb, writes=[Bb], inc=(i == g - 1))
                    src = bk[:, 0:g * width].rearrange("p (g t) -> p g t", t=width)
                    d = T[dst][:, c0:c0 + g, :]
                    if post is None:
                        yield from evac(d, B[dst], src, Bb)
                    else:
                        yield from post(d, src, Bb, c0, g)
            def post_mask2(dst):
                def f(d, src, Bb, c0, g):
                    yield EM.op("dve", lambda e: e.tensor_tensor(d, src, T["mask2"][:, 0:g, :], ALU.mult), reads=[Bb, B["mask2"]], writes=[B[dst]])
                return f
            A_ = lambda c: T["bdAR"][:, c, 0:128]; R_ = lambda c: T["bdAR"][:, c, 128:256]
            yield from grp("S1", 256, lambda c: T["bdB"][:, c, :], lambda c: T["bdAR"][:, c, :], [B["bdB"], B["bdAR"]], post=post_mask2("S1"))
            yield from grp("S2", 256, lambda c: T["bdK"][:, c, :], lambda c: T["bdAR"][:, c, :], [B["bdK"], B["bdAR"]], post=post_mask2("S2"))
            def post_N(d, src, Bb, c0, g):
                yield EM.op("dve", lambda e: e.tensor_tensor(d, src, T["mSLg"][:, 0:g, :], ALU.mult), reads=[Bb, B["mSLg"]], writes=[B["N0"]])
            yield from grp("N0", 128, A_, lambda c: T["bdB"][:, c, :], [B["bdB"], B["bdAR"]], post=post_N)
            yield EM.op("pool", lambda e: e.tensor_copy(T["M0"][:, 0:nch, :], T["S1"][:, 0:nch, 0:128]), reads=[B["S1"]], writes=[B["M0"]])
            for c0 in range(0, nch, 4):
                g = min(4, nch - c0)
                yield EM.op("pool", lambda e, c0=c0, g=g: e.tensor_tensor(T["T0"][:, c0:c0 + g, :], T["S1"][:, c0:c0 + g, 0:128], T["identg"][:, 0:g, :], ALU.add), reads=[B["S1"], B["identg"]], writes=[B["T0"]])
            cur = 0
            for k in range(1, 6):
                nx = 1 - cur
                Nc, Mc, Tc = "N%d" % cur, "M%d" % cur, "T%d" % cur; Nx, Mx, Tx = "N%d" % nx, "M%d" % nx, "T%d" % nx
                yield from grp(Nx, 128, lambda c: T[Mc][:, c, :], lambda c: T[Nc][:, c, :], [B[Mc], B[Nc]])
                if k < 5:
                    yield from grp(Mx, 128, lambda c: T[Nc][:, c, :], lambda c: T[Mc][:, c, :], [B[Mc], B[Nc]])
                def post_T(d, src, Bb, c0, g, Tc=Tc, Tx=Tx):
                    yield EM.op("dve", lambda e: e.tensor_tensor(d, src, T[Tc][:, c0:c0 + g, :], ALU.add), reads=[Bb, B[Tc]], writes=[B[Tx]])
                yield from grp(Tx, 128, lambda c: T[Nx][:, c, :], lambda c: T[Tc][:, c, :], [B[Nx], B[Tc]], post=post_T)
                cur = nx
            TT = "T%d" % cur
            for src_n, dst_n, sl_ in (("bdV", "Vb", None), ("bdKh", "Kh", None), ("bdBh", "Bh", None), ("bdAR", "At", slice(0, 128))):
                for c0 in range(0, nch, 4):
                    g = min(4, nch - c0)
                    for i in range(g):
                        c = c0 + i
                        src_ap = T[src_n][:, c, :] if sl_ is None else T[src_n][:, c, sl_]
                        yield EM.op("pe", lambda e, i=i, src_ap=src_ap: e.transpose(bankt[:, li * 512 + i * 128:li * 512 + (i + 1) * 128], src_ap, ident), reads=[B[src_n], K.B_cm], writes=[bkB[7]])
                    yield from evac(T[dst_n][:, c0:c0 + g, :], B[dst_n], bankt[:, li * 512:li * 512 + g * 128].rearrange("p (g t) -> p g t", t=128), bkB[7])
            yield from grp("AV", 128, lambda c: T["S2"][:, c, 0:128], lambda c: T["Vb"][:, c, :], [B["S2"], B["Vb"]])
            yield from grp("Dm", 128, lambda c: T[TT][:, c, :], lambda c: T["AV"][:, c, :], [B[TT], B["AV"]])
            yield from grp("XT", 128, lambda c: T[TT][:, c, :], lambda c: T["At"][:, c, :], [B[TT], B["At"]])
            yield from grp("Gm", 128, lambda c: T["XT"][:, c, :], lambda c: T["Bh"][:, c, :], [B["XT"], B["Bh"]])
            for c0 in range(0, nch, 4):
                g = min(4, nch - c0)
                bk, Bb = nbank()
                for i in range(g):
                    c = c0 + i
                    yield EM.op("pe", lambda e, i=i, c=c: e.matmul(bk[:, i * 128:(i + 1) * 128], T["Kh"][:, c, :], T["Vb"][:, c, :], start=True, stop=False), reads=[B["Kh"], B["Vb"]], writes=[Bb], inc=False)
                    yield EM.op("pe", lambda e, i=i, c=c: e.matmul(bk[:, i * 128:(i + 1) * 128], T["Bh"][:, c, :], T["Dm"][:, c, :], start=False, stop=True), reads=[B["Bh"], B["Dm"]], writes=[Bb], inc=(i == g - 1))
                yield EM.op("act", lambda e, c0=c0, g=g, bk=bk: e.copy(T["E"][:, c0:c0 + g, :], bk[:, 0:g * 128].rearrange("p (g t) -> p g t", t=128)), reads=[Bb], writes=[B["E"]])
            if with_y:
                def post_Q(d, src, Bb, c0, g):
                    yield EM.op("dve", lambda e: e.tensor_tensor(d, src, T["bdAR"][:, c0:c0 + g, 128:256], ALU.add), reads=[Bb, B["bdAR"]], writes=[B["Qc"]])
                yield from grp("Qc", 128, lambda c: T["XT"][:, c, :], lambda c: T["S1"][:, c, 128:256], [B["XT"], B["S1"]], post=post_Q)
            yield EM.op("act", lambda e: e.copy(T["Hall"][:, 0, :], T["Hf"][:, :]), reads=[B["Hf"]], writes=[B["Hall"]])
            for c in range(nch):
                tmpn = "tmpH%d" % (c % 2)
                yield EM.op("dve", lambda e, c=c, tmpn=tmpn: e.scalar_tensor_tensor(T[tmpn][:, :], T["Hf"][:, :], T["PC"][:, c:c + 1], T["E"][:, c, :], ALU.mult, ALU.add), reads=[B["Hf"], B["PC"], B["E"]], writes=[B[tmpn]])
                bk = banks[2 * li]; Bb = bkB[2 * li]
                yield EM.op("pe", lambda e, c=c, bk=bk: e.matmul(bk[:, 0:128], T["Gm"][:, c, :], T["Hall"][:, c, :], start=True, stop=True), reads=[B["Gm"], B["Hall"]], writes=[Bb])
                yield EM.op("dve", lambda e, c=c, bk=bk, tmpn=tmpn: e.tensor_tensor(T["Hall"][:, c + 1, :], bk[:, 0:128], T[tmpn][:, :], ALU.add), reads=[Bb, B[tmpn]], writes=[B["Hall"]])
                yield EM.op("dve", lambda e, c=c, bk=bk, tmpn=tmpn: e.tensor_tensor(T["Hf"][:, :], bk[:, 0:128], T[tmpn][:, :], ALU.add), reads=[Bb, B[tmpn]], writes=[B["Hf"]])
            if with_y:
                for c0 in range(0, nch, 4):
                    g = min(4, nch - c0)
                    bk, Bb = nbank()
                    for i in range(g):
                        c = c0 + i; reg = bk[:, i * 128:(i + 1) * 128]
                        yield EM.op("pe", lambda e, c=c, reg=reg: e.matmul(reg, T["Hall"][:, c, :], T["Qc"][:, c, :], start=True, stop=False), reads=[B["Hall"], B["Qc"]], writes=[Bb], inc=False)
                        yield EM.op("pe", lambda e, c=c, reg=reg: e.matmul(reg, T["Dm"][:, c, :], T["S1"][:, c, 128:256], start=False, stop=False), reads=[B["Dm"], B["S1"]], writes=[Bb], inc=False)
                        yield EM.op("pe", lambda e, c=c, reg=reg: e.matmul(reg, T["Vb"][:, c, :], T["S2"][:, c, 128:256], start=False, stop=True), reads=[B["Vb"], B["S2"]], writes=[Bb], inc=(i == g - 1))
                    v = bk[:, 0:g * 128].rearrange("p (g t) -> p g t", t=128)
                    yield EM.op("act", lambda e, c0=c0, g=g, v=v: e.copy(T["yT"][0:64, c0 * 64:(c0 + g) * 64].rearrange("p (g t) -> p g t", t=64), v[0:64, :, 0:64]), reads=[Bb], writes=[B["yT"]])
                    yield EM.op("dve", lambda e, c0=c0, g=g, v=v: e.tensor_copy(T["yT"][64:128, c0 * 64:(c0 + g) * 64].rearrange("p (g t) -> p g t", t=64), v[64:128, :, 64:128]), reads=[Bb], writes=[B["yT"]])
            if not with_y:
                return
            for c0 in range(0, n, 512):
                cn = min(512, n - c0)
                pz = "pz%d" % (cnt["z"] % 2); cnt["z"] += 1
                yield EM.op("pe", lambda e, pz=pz: e.matmul(P[pz][:, 0:cn], onesf, T["yT"][:, c0:c0 + cn], start=True, stop=True), reads=[B["yT"], K.B_cm], writes=[BP[pz]])
                yield EM.op("dve", lambda e, pz=pz: e.tensor_tensor(T["yc"][:, c0:c0 + cn], T["yT"][:, c0:c0 + cn], P[pz][:, 0:cn], ALU.subtract), reads=[B["yT"], BP[pz]], writes=[B["yc"]])
                yield EM.op("pool", lambda e: e.tensor_tensor(T["ysq"][:, c0:c0 + cn], T["yc"][:, c0:c0 + cn], T["yc"][:, c0:c0 + cn], ALU.mult), reads=[B["yc"]], writes=[B["ysq"]])
                pz = "pz%d" % (cnt["z"] % 2); cnt["z"] += 1
                yield EM.op("pe", lambda e, pz=pz: e.matmul(P[pz][:, 0:cn], onesf, T["ysq"][:, c0:c0 + cn], start=True, stop=True), reads=[B["ysq"], K.B_cm], writes=[BP[pz]])
                yield EM.op("act", lambda e, pz=pz: e.activation(T["ysq"][:, c0:c0 + cn], P[pz][:, 0:cn], AF.Sqrt, bias=K.epsc[:, 1:2], scale=1.0), reads=[BP[pz], K.B_const], writes=[B["ysq"]])
            yield EM.op("dve", lambda e: e.reciprocal(T["ysq"][:, 0:n], T["ysq"][:, 0:n]), reads=[B["ysq"]], writes=[B["ysq"]])
            yield EM.op("pool", lambda e: e.tensor_tensor(T["yc"][:, 0:n], T["yc"][:, 0:n], T["ysq"][:, 0:n], ALU.mult), reads=[B["yc"], B["ysq"]], writes=[B["yc"]])
            yield EM.op("dve", lambda e: e.tensor_scalar(T["yc"][:, 0:n], T["yc"][:, 0:n], pv[:, 5, hp:hp + 1], pv[:, 6, hp:hp + 1], ALU.mult, ALU.add), reads=[B["yc"], B_pv], writes=[B["yc"]])
            yield EM.op("pool", lambda e: e.tensor_tensor(T["yc"][:, 0:n], T["yc"][:, 0:n], T["bv"][:, 0:n], ALU.add), reads=[B["yc"], B["bv"]], writes=[B["yc"]])
            yield EM.op("dve", lambda e: e.tensor_tensor(T["ob"][:, 0:n], T["yc"][:, 0:n], T["gvv"][:, 0:n], ALU.mult), reads=[B["yc"], B["gvv"]], writes=[B["ob"]])
            rows = K.oT_d[1024 + hp * 128:1024 + (hp + 1) * 128, :]
            if samp is None:
                yield EM.dma("sp", rows[:, t0 - HALO0:t0 - HALO0 + n], T["ob"][:, 0:n], reads=[B["ob"]], writes=[B_oT], semkey="st_oT%d" % li)
            else:
                yield EM.dma("sp", rows[:, 1152 + 16 * samp:1152 + 16 * samp + 16], T["ob"][:, 48:64], reads=[B["ob"]], writes=[B_oT], semkey="st_oT%d" % li)

        def pair_prog(li, hp):
            T = TS[li]; B = BS[li]
            yield EM.op("pool", lambda e: e.memset(T["Hf"][:, :], 0.0), writes=[B["Hf"]])
            blocks = [(c0 * 64, min(NCM, 46 - c0), False) for c0 in range(0, 46, NCM)] + [(2944 + c0 * 64, min(NCM, 18 - c0), True) for c0 in range(0, 18, NCM)]
            for (t0, nch, wy) in blocks:
                yield from rw_block(li, hp, t0, nch, wy)
            yield EM.dma("sp", K.wkv_p[2 * hp, :, :], T["Hf"][0:64, 0:64], reads=[B["Hf"]], writes=[B_out], semkey="st_out3")
            yield EM.dma("sp", K.wkv_p[2 * hp + 1, :, :], T["Hf"][64:128, 64:128], reads=[B["Hf"]], writes=[B_out], semkey="st_out3")
            for s_ in range(4):
                yield EM.op("pool", lambda e: e.memset(T["Hf"][:, :], 0.0), writes=[B["Hf"]])
                yield EM.dma("sp", T["Hf"][0:64, 0:64], K.st_wkv[s_, 2 * hp, :, :], writes=[B["Hf"]], semkey="ld_H%d" % li)
                yield EM.dma("sp", T["Hf"][64:128, 64:128], K.st_wkv[s_, 2 * hp + 1, :, :], writes=[B["Hf"]], semkey="ld_H%d" % li)
                yield from rw_block(li, hp, 0, 1, True, samp=s_)
                yield EM.dma("sp", K.wkv_s[s_, 2 * hp, :, :], T["Hf"][0:64, 0:64], reads=[B["Hf"]], writes=[B_out], semkey="st_out3")
                yield EM.dma("sp", K.wkv_s[s_, 2 * hp + 1, :, :], T["Hf"][64:128, 64:128], reads=[B["Hf"]], writes=[B_out], semkey="st_out3")
        def lane_prog(li):
            for hp in range(li, 8, 2):
                yield from pair_prog(li, hp)
        g2 = phase2_gen(K, es, banks, bkB)
        lanes = [lane_prog(0), lane_prog(1)]
        alive = [g2] + lanes
        while alive:
            for g_ in list(alive):
                try:
                    next(g_)
                except StopIteration:
                    alive.remove(g_)
        S.barrier()


def phase2_gen(K, es, banks, bkB):
    S = K.S; nc = K.nc
    if True:
        sbt = lambda n, s, d: K.sbt(es, n, s, d); pst = lambda n, s, d: K.pst(es, n, s, d)
        T = {}; B = {}
        def mk(name, shape, dt):
            T[name] = sbt(name, shape, dt); B[name] = Buf(name)
        mk("lt", [128, 4, 64], F32); mk("lp", [128, 2, 64], F32); mk("lam", [128, 4], F32)
        mk("gsub", [128, 128], F32); mk("kb", [128, 24], F32)
        yield S.dma("sp", T["lt"][:, :, :], K.lamv.partition_broadcast(128), writes=[B["lt"]], semkey="a_lt")
        yield S.dma("sp", T["gsub"][:, :], K.g_subln.partition_broadcast(128), writes=[B["gsub"]], semkey="a_gs")
        yield S.dma("sp", T["kb"][:, :], K.kbias[:, :], writes=[B["kb"]], semkey="a_kb")
        yield S.op("dve", lambda e: e.tensor_tensor(T["lp"][:, :, :], T["lt"][:, :, :].rearrange("p (a b) d -> p a b d", b=2)[:, :, 0, :], T["lt"][:, :, :].rearrange("p (a b) d -> p a b d", b=2)[:, :, 1, :], ALU.mult), reads=[B["lt"]], writes=[B["lp"]])
        yield S.op("dve", lambda e: e.reduce_sum(T["lam"][:, 0:2], T["lp"][:, :, :], AX.X), reads=[B["lp"]], writes=[B["lam"]])
        yield S.op("act", lambda e: e.activation(T["lam"][:, 0:2], T["lam"][:, 0:2], AF.Exp), reads=[B["lam"]], writes=[B["lam"]])
        yield S.op("dve", lambda e: e.tensor_tensor(T["lam"][:, 2:3], T["lam"][:, 1:2], T["lam"][:, 0:1], ALU.subtract), reads=[B["lam"]], writes=[B["lam"]])
        yield S.op("dve", lambda e: e.tensor_scalar(T["lam"][:, 3:4], T["lam"][:, 2:3], -LAM_INIT, None, ALU.add), reads=[B["lam"]], writes=[B["lam"]])
        yield S.op("dve", lambda e: e.tensor_scalar(T["gsub"][:, :], T["gsub"][:, :], 1.0 - LAM_INIT, None, ALU.mult), reads=[B["gsub"]], writes=[B["gsub"]])
        nlam = T["lam"][:, 3:4]
        mk("kTh", [128, NTOK], BF16); mk("qTh", [128, NREG], BF16); mk("Vh", [128, 32, 129], BF16); mk("Vn", [16, 4, 129], BF16)
        for i in range(1):
            mk("ckT%d" % i, [128, 4096], BF16); mk("cV%d" % i, [128, 32, 129], BF16)
        for s_ in range(4):
            mk("Pp%d" % s_, [128, 32, 64], BF16); mk("Pn%d" % s_, [16, 64], BF16)
            yield S.op("pool", lambda e, s_=s_: e.memset(T["Pp%d" % s_][:, :, :], 0.0), writes=[B["Pp%d" % s_]])
            yield S.op("pool", lambda e, s_=s_: e.memset(T["Pn%d" % s_][:, :], 0.0), writes=[B["Pn%d" % s_]])
        for nm in ("Vh", "cV0"):
            yield S.op("pool", lambda e, nm=nm: e.memset(T[nm][:, :, 128:129], 1.0), writes=[B[nm]])
        yield S.op("pool", lambda e: e.memset(T["Vn"][:, :, 128:129], 1.0), writes=[B["Vn"]])
        for i in range(4):
            mk("PT%d" % i, [128, 384], BF16)
        mk("t0", [128, 128], F32); mk("o", [128, 128], F32); mk("osq", [128, 128], F32); mk("ob", [128, 128], F32)
        mk("sc", [128, 8], F32); mk("oTs", [128, NREG], BF16)
        PS = [banks[6][:, 0:384]]; BPS = [bkB[6]]
        PO = [[banks[4 + m][:, q * 129:(q + 1) * 129] for q in range(3)] for m in range(2)]; BPO = [[bkB[4 + m]] * 3 for m in range(2)]
        PSs = banks[6][:, :]; BPSs = bkB[6]
        PTR = [banks[6][:, 384:512]]; BPTR = [bkB[6]]
        B_oT = Buf("oTa")
        ident = K.cm_f[:, 0, :]
        cnt = {"s": 0, "tr": 0, "pt": 0}

        def finalize(h, np_, O0, O1, BO0, BO1, col0):
            sc = T["sc"]
            yield S.op("dve", lambda e: e.tensor_scalar(sc[0:np_, 0:1], O0[0:np_, 128:129], 1e-30, None, ALU.max), reads=[BO0], writes=[B["sc"]])
            yield S.op("dve", lambda e: e.tensor_scalar(sc[0:np_, 1:2], O1[0:np_, 128:129], 1e-30, None, ALU.max), reads=[BO1], writes=[B["sc"]])
            yield S.op("dve", lambda e: e.reciprocal(sc[0:np_, 0:2], sc[0:np_, 0:2]), reads=[B["sc"]], writes=[B["sc"]])
            yield S.op("dve", lambda e: e.tensor_tensor(sc[0:np_, 1:2], sc[0:np_, 1:2], nlam[0:np_, :], ALU.mult), reads=[B["sc"], B["lam"]], writes=[B["sc"]])
            yield S.op("dve", lambda e: e.tensor_scalar(T["t0"][0:np_, :], O0[0:np_, 0:128], sc[0:np_, 0:1], None, ALU.mult), reads=[BO0, B["sc"]], writes=[B["t0"]])
            yield S.op("dve", lambda e: e.scalar_tensor_tensor(T["o"][0:np_, :], O1[0:np_, 0:128], sc[0:np_, 1:2], T["t0"][0:np_, :], ALU.mult, ALU.add), reads=[BO1, B["sc"], B["t0"]], writes=[B["o"]])
            yield S.op("act", lambda e: e.activation(T["osq"][0:np_, :], T["o"][0:np_, :], AF.Square, accum_out=sc[0:np_, 2:3]), reads=[B["o"]], writes=[B["osq"], B["sc"]])
            yield S.op("act", lambda e: e.activation(sc[0:np_, 3:4], sc[0:np_, 2:3], AF.Sqrt, bias=K.epsc[0:np_, 0:1], scale=1.0 / 128), reads=[B["sc"], K.B_const], writes=[B["sc"]])
            yield S.op("dve", lambda e: e.reciprocal(sc[0:np_, 3:4], sc[0:np_, 3:4]), reads=[B["sc"]], writes=[B["sc"]])
            yield S.op("dve", lambda e: e.scalar_tensor_tensor(T["ob"][0:np_, :], T["o"][0:np_, :], sc[0:np_, 3:4], T["gsub"][0:np_, :], ALU.mult, ALU.mult), reads=[B["o"], B["sc"], B["gsub"]], writes=[B["ob"]])
            pt = PTR[0]; Bpt = BPTR[0]
            yield S.op("pe", lambda e: e.transpose(pt[:, 0:np_], T["ob"][0:np_, :], ident[0:np_, 0:np_]), reads=[B["ob"], K.B_cm], writes=[Bpt])
            yield S.op("act", lambda e: e.copy(T["oTs"][:, col0:col0 + np_], pt[:, 0:np_]), reads=[Bpt], writes=[B["oTs"]])

        vd_v = K.v_d[0:NSEQ, :].rearrange("(t p) c -> p t c", p=128)
        for h in range(8):
            hs = slice(h * 128, (h + 1) * 128)
            yield S.dma("sp", T["kTh"][:, :], K.kT_d[hs, :], writes=[B["kTh"]], semkey="a_kT")
            yield S.dma("sp", T["qTh"][:, :], K.qT_d[hs, :], writes=[B["qTh"]], semkey="a_qT")
            yield S.dma("pool", T["Vh"][:, :, 0:128], vd_v[:, :, hs], writes=[B["Vh"]], semkey="a_Vh")
            yield S.dma("pool", T["Vn"][:, :, 0:128], K.v_d[NSEQ:NTOK, hs].rearrange("(s t) c -> t s c", t=16), writes=[B["Vn"]], semkey="a_Vn")
            for G in range(3):
                last_kt = 23 + 3 * G + 2
                for kt in range(last_kt + 1):
                    vis = [qi for qi in range(3) if 23 + 3 * G + qi >= kt]
                    for m in range(2):
                        ms = slice(64 * m, 64 * m + 64)
                        si = 0; pti = cnt["pt"] % 4; cnt["pt"] += 1
                        yield S.op("pe", lambda e, si=si, ms=ms: e.matmul(PS[si], T["kTh"][ms, kt * 128:(kt + 1) * 128], T["qTh"][ms, G * 384:(G + 1) * 384], start=True, stop=True),
                             reads=[B["kTh"], B["qTh"]], writes=[BPS[si]])
                        PT = T["PT%d" % pti]; BPT = B["PT%d" % pti]
                        if kt < 24:
                            yield S.op("act", lambda e, si=si, PT=PT: e.activation(PT[:, :], PS[si], AF.Exp, bias=T["kb"][:, kt:kt + 1], scale=0.125), reads=[BPS[si], B["kb"]], writes=[BPT])
                        else:
                            yield S.op("act", lambda e, si=si, PT=PT: e.activation(PT[:, :], PS[si], AF.Exp, scale=0.125), reads=[BPS[si]], writes=[BPT])
                        for qi in vis:
                            if 23 + 3 * G + qi == kt:
                                yield S.op("pool", lambda e, PT=PT, qi=qi: e.memset(PT[64:128, qi * 128:qi * 128 + 64], 0.0), writes=[BPT])
                        for qi in vis:
                            yield S.op("pe", lambda e, PT=PT, qi=qi, m=m: e.matmul(PO[m][qi], PT[:, qi * 128:(qi + 1) * 128], T["Vh"][:, kt, :], start=(kt == 0 and qi == 0), stop=(kt == 23 + 3 * G + qi), skip_group_check=True),
                                 reads=[BPT, B["Vh"]], writes=[BPO[m][qi]], inc=(kt == 23 + 3 * G + qi))
                for qi in range(3):
                    yield from finalize(h, 128, PO[0][qi], PO[1][qi], BPO[0][qi], BPO[1][qi], (3 * G + qi) * 128)
            Os = [banks[4][0:64, 0:129], banks[5][0:64, 0:129]]; BOs = [BPO[0][0], BPO[1][0]]
            for s_ in range(4):
                i = 0
                ck = T["ckT%d" % i]; cv = T["cV%d" % i]; Bck = B["ckT%d" % i]; Bcv = B["cV%d" % i]
                yield S.dma("pool", ck[:, :], K.cache_kT[s_, h, :, :], writes=[Bck], semkey="a_ck%d" % i)
                yield S.dma("pool", cv[:, :, 0:128], K.cache_v[s_, :, h, :].rearrange("(t p) d -> p t d", p=128), writes=[Bcv], semkey="a_cv%d" % i)
                Pp = T["Pp%d" % s_]; Pn = T["Pn%d" % s_]; BPp = B["Pp%d" % s_]; BPn = B["Pn%d" % s_]
                qc = slice(1152 + 16 * s_, 1152 + 16 * s_ + 16)
                for m in range(2):
                    ms = slice(64 * m, 64 * m + 64)
                    for kt in range(32):
                        yield S.op("pe", lambda e, kt=kt, ms=ms: e.matmul(PSs[:, kt * 16:(kt + 1) * 16], ck[ms, kt * 128:(kt + 1) * 128], T["qTh"][ms, qc], start=True, stop=True),
                             reads=[Bck, B["qTh"]], writes=[BPSs], inc=(kt == 31))
                    yield S.op("act", lambda e: e.activation(Pp[:, :, 16 * s_:16 * s_ + 16], PSs.rearrange("p (k q) -> p k q", q=16), AF.Exp, scale=0.125), reads=[BPSs], writes=[BPp])
                    for kt in range(32):
                        yield S.op("pe", lambda e, kt=kt, m=m: e.matmul(Os[m], Pp[:, kt, :], cv[:, kt, :], start=(s_ == 0 and kt == 0), stop=False, skip_group_check=True),
                             reads=[BPp, Bcv], writes=[BOs[m]], inc=False)
                    si = 0
                    yield S.op("pe", lambda e, si=si, ms=ms: e.matmul(PS[si][0:16, 0:16], T["kTh"][ms, NSEQ + 16 * s_:NSEQ + 16 * s_ + 16], T["qTh"][ms, qc], start=True, stop=True),
                         reads=[B["kTh"], B["qTh"]], writes=[BPS[si]])
                    yield S.op("act", lambda e, si=si: e.activation(Pn[:, 16 * s_:16 * s_ + 16], PS[si][0:16, 0:16], AF.Exp, scale=0.125), reads=[BPS[si]], writes=[BPn])
                    yield S.op("pe", lambda e, m=m: e.matmul(Os[m], Pn[:, :], T["Vn"][:, s_, :], start=False, stop=(s_ == 3), skip_group_check=True),
                         reads=[BPn, B["Vn"]], writes=[BOs[m]], inc=True)
            yield from finalize(h, 64, Os[0], Os[1], BOs[0], BOs[1], 1152)
            yield S.dma("sp", K.oT_d[hs, :], T["oTs"][:, :], reads=[B["oTs"]], writes=[B_oT], semkey="st_oTa")


def phase4(K):
    S = K.S; nc = K.nc
    x1_d = K.x1_d
    NT = 10
    def trow(t):
        return (t * 128, 128) if t < 9 else (1152, 64)
    h2T_d = K.h2T_d
    with ExitStack() as es0:
        B_x1d = Buf("x1d"); B_y = Buf("y_out"); B_h2d = Buf("h2d")
        ident = K.cm_b[:, 0, :]
        with ExitStack() as es:
            sbt = lambda n, s, d: K.sbt(es, n, s, d); pst = lambda n, s, d: K.pst(es, n, s, d)
            rows = sbt("rowsA", [128, 2, 3, D], F32); B_rows = Buf("rowsA")
            S.dma("sp", rows[:, 0, :, :], K.rows_d[0:3, 0, :].partition_broadcast(128), writes=[B_rows], semkey="f_rows")
            for s_ in range(4):
                S.dma("sp", rows[16 * s_:16 * s_ + 16, 1, :, :], K.rows_d[0:3, 1 + s_, :].partition_broadcast(16), writes=[B_rows], semkey="f_rows")
            h2Ts = [sbt("h2Ts%d" % i, [128, 16, 128], BF16) for i in range(2)]; B_h2Ts = [Buf("h2Ts0"), Buf("h2Ts1")]
            wo = sbt("wo", [128, 16, D], BF16); B_wo = Buf("wo")
            S.dma("sp", wo[:, :, :], K.w_out_b.rearrange("(c p) n -> p c n", p=128), writes=[B_wo], semkey="f_wo")
            oTt = [sbt("oTt%d" % i, [128, 16, 128], BF16) for i in range(2)]; B_oTt = [Buf("oTt0"), Buf("oTt1")]
            xt = sbt("xt4", [128, D], F32); B_xt = Buf("xt4")
            tmp = sbt("tmp4", [128, D], F32); B_tmp = Buf("tmp4")
            x1 = sbt("x1", [128, D], F32); B_x1 = Buf("x1")
            h2 = sbt("h2", [128, D], BF16); B_h2 = Buf("h2")
            sc = sbt("sc4", [128, 16], F32); B_sc = Buf("sc4")
            vcol = sbt("vcol", [128, 1], F32); B_vc = Buf("vcol")
            pm = [pst("pm%d" % i, [128, 512], F32) for i in range(4)]; B_pm = [Buf("pm%d" % i) for i in range(4)]
            ptb = pst("ptb", [128, 1024], BF16); PTR = [ptb[:, i * 128:(i + 1) * 128] for i in range(4)]; _bp4 = Buf("ptr4"); B_ptr = [_bp4] * 4
            tr = [0]
            for t in range(NT):
                r0, nr = trow(t); ri = 0 if t < 9 else 1
                o = oTt[t % 2]; Bo = B_oTt[t % 2]
                S.dma("sp", o[:, :, 0:nr], K.oT_d[:, r0:r0 + nr].rearrange("(c p) t -> p c t", p=128), writes=[Bo], semkey="f_oT%d" % (t % 2))
                S.dma("sp", xt[0:nr, :], K.x_tm[r0:r0 + nr, :], writes=[B_xt], semkey="f_xt")
                S.dma("sp", vcol[0:nr, :], K.valid[0:1, HALO0 + r0:HALO0 + r0 + nr].rearrange("o t -> t o"), writes=[B_vc], semkey="f_vc")
                for g in range(4):
                    for k in range(16):
                        S.op("pe", lambda e, g=g, k=k: e.matmul(pm[g][0:nr, :], o[:, k, 0:nr], wo[:, k, g * 512:(g + 1) * 512], start=(k == 0), stop=(k == 15)),
                             reads=[Bo, B_wo], writes=[B_pm[g]], inc=(k == 15))
                for g in range(4):
                    S.op("act", lambda e, g=g: e.activation(tmp[0:nr, g * 512:(g + 1) * 512], pm[g][0:nr, :], AF.Square, accum_out=sc[0:nr, g:g + 1]), reads=[B_pm[g]], writes=[B_tmp, B_sc])
                S.op("dve", lambda e: e.reduce_sum(sc[0:nr, 4:5], sc[0:nr, 0:4], AX.X), reads=[B_sc], writes=[B_sc])
                S.op("act", lambda e: e.activation(sc[0:nr, 5:6], sc[0:nr, 4:5], AF.Sqrt, bias=K.epsc[0:nr, 0:1], scale=1.0 / D), reads=[B_sc, K.B_const], writes=[B_sc])
                S.op("dve", lambda e: e.reciprocal(sc[0:nr, 5:6], sc[0:nr, 5:6]), reads=[B_sc], writes=[B_sc])
                for g in range(4):
                    gs = slice(g * 512, (g + 1) * 512)
                    S.op("dve", lambda e, g=g, gs=gs: e.scalar_tensor_tensor(tmp[0:nr, gs], pm[g][0:nr, :], sc[0:nr, 5:6], rows[0:nr, ri, 0, gs], ALU.mult, ALU.mult), reads=[B_pm[g], B_sc, B_rows], writes=[B_tmp])
                S.op("pool", lambda e: e.tensor_tensor(x1[0:nr, :], xt[0:nr, :], tmp[0:nr, :], ALU.add), reads=[B_xt, B_tmp], writes=[B_x1])
                S.dma("pool", x1_d[r0:r0 + nr, :], x1[0:nr, :], reads=[B_x1], writes=[B_x1d], semkey="st_x1")
                S.op("act", lambda e: e.activation(tmp[0:nr, :], x1[0:nr, :], AF.Square, accum_out=sc[0:nr, 6:7]), reads=[B_x1], writes=[B_tmp, B_sc])
                S.op("act", lambda e: e.activation(sc[0:nr, 7:8], sc[0:nr, 6:7], AF.Sqrt, bias=K.epsc[0:nr, 0:1], scale=1.0 / D), reads=[B_sc, K.B_const], writes=[B_sc])
                S.op("dve", lambda e: e.reciprocal(sc[0:nr, 7:8], sc[0:nr, 7:8]), reads=[B_sc], writes=[B_sc])
                S.op("dve", lambda e: e.tensor_tensor(sc[0:nr, 7:8], sc[0:nr, 7:8], vcol[0:nr, :], ALU.mult), reads=[B_sc, B_vc], writes=[B_sc])
                S.op("dve", lambda e: e.scalar_tensor_tensor(tmp[0:nr, :], x1[0:nr, :], sc[0:nr, 7:8], rows[0:nr, ri, 1, :], ALU.mult, ALU.mult), reads=[B_x1, B_sc, B_rows], writes=[B_tmp])
                S.op("dve", lambda e: e.scalar_tensor_tensor(h2[0:nr, :], rows[0:nr, ri, 2, :], vcol[0:nr, :], tmp[0:nr, :], ALU.mult, ALU.add), reads=[B_rows, B_vc, B_tmp], writes=[B_h2])
                for c in range(16):
                    pt = PTR[tr[0] % 4]; Bpt = B_ptr[tr[0] % 4]; tr[0] += 1
                    S.op("pe", lambda e, c=c, pt=pt: e.transpose(pt[:, 0:nr], h2[0:nr, c * 128:(c + 1) * 128], ident[0:nr, 0:nr]), reads=[B_h2, K.B_cm], writes=[Bpt])
                    if c % 2 == 0:
                        S.op("act", lambda e, c=c, pt=pt: e.copy(h2Ts[t % 2][:, c, 0:nr], pt[:, 0:nr]), reads=[Bpt], writes=[B_h2Ts[t % 2]])
                    else:
                        S.op("dve", lambda e, c=c, pt=pt: e.tensor_copy(h2Ts[t % 2][:, c, 0:nr], pt[:, 0:nr]), reads=[Bpt], writes=[B_h2Ts[t % 2]])
                S.dma("pool", h2T_d[:, r0:r0 + nr].rearrange("(c p) t -> p c t", p=128), h2Ts[t % 2][:, :, 0:nr], reads=[B_h2Ts[t % 2]], writes=[B_h2d], semkey="st_h2")
            S.barrier()
        aT = K.sbt(es0, "aT", [128, 44, NREG], BF16); B_aT = Buf("aT")
        with ExitStack() as es:
            sbt = lambda n, s, d: K.sbt(es, n, s, d); pst = lambda n, s, d: K.pst(es, n, s, d)
            NE = 1226
            h2T = sbt("h2T", [128, 16, NREG], BF16); B_h2T = Buf("h2T")
            S.dma("sp", h2T[:, :, :], h2T_d.rearrange("(c p) t -> p c t", p=128), reads=[B_h2d], writes=[B_h2T], semkey="f_h2T")
            wc = sbt("wc", [128, 3, 88], F32); B_wc = Buf("wc")
            for kk_ in range(3):
                S.dma("sp", wc[:, kk_, :], K.w_conv[kk_, :].rearrange("(c p) -> p c", p=128), writes=[B_wc], semkey="f_wc")
            wu = [sbt("wu%d" % i, [128, 16, 128], BF16) for i in range(4)]; B_wu = [Buf("wu%d" % i) for i in range(4)]
            ue = [sbt("ue%d" % i, [128, NE], F32) for i in range(2)]; B_ue = [Buf("ue0"), Buf("ue1")]
            z = [sbt("z%d" % i, [128, NE], F32) for i in range(2)]; B_z = [Buf("z0"), Buf("z1")]
            for i in range(2):
                S.op("pool", lambda e, i=i: e.memset(ue[i][:, 0:2], 0.0), writes=[B_ue[i]])
            pu = [pst("pu%d" % i, [128, 512], F32) for i in range(6)]; B_pu = [Buf("pu%d" % i) for i in range(6)]
            B_co = Buf("conv_out")
            NTL = ((0, 512), (512, 512), (1024, 192))
            for j in range(44):
                for gv_ in range(2):
                    ch = j + 44 * gv_
                    w = wu[(2 * j + gv_) % 4]; Bw = B_wu[(2 * j + gv_) % 4]
                    S.dma("sp", w[:, :, :], K.w_up_b[:, ch * 128:(ch + 1) * 128].rearrange("(c p) n -> p c n", p=128), writes=[Bw], semkey="f_wu%d" % ((2 * j + gv_) % 4))
                    u = ue[gv_]; Bu = B_ue[gv_]
                    S.dma("sp", u[:, 1154:1226].rearrange("p (s t) -> p s t", t=18)[:, :, 0:2], K.st_conv[:, ch, :, :], writes=[Bu], semkey="f_cv%d" % gv_)
                    for ti, (c0, cn) in enumerate(NTL):
                        p = pu[3 * gv_ + ti]; Bp = B_pu[3 * gv_ + ti]
                        for k in range(16):
                            S.op("pe", lambda e, k=k, p=p, c0=c0, cn=cn: e.matmul(p[:, 0:cn], w[:, k, :], h2T[:, k, c0:c0 + cn], start=(k == 0), stop=(k == 15)),
                                 reads=[Bw, B_h2T], writes=[Bp], inc=(k == 15))
                        if ti < 2:
                            S.op("act", lambda e, p=p, c0=c0, cn=cn, u=u: e.copy(u[:, 2 + c0:2 + c0 + cn], p[:, 0:cn]), reads=[Bp], writes=[Bu])
                        else:
                            S.op("act", lambda e, p=p, u=u: e.copy(u[:, 2 + 1024:2 + 1152], p[:, 0:128]), reads=[Bp], writes=[Bu])
                            S.op("dve", lambda e, p=p, u=u: e.tensor_copy(u[:, 1154:1226].rearrange("p (s t) -> p s t", t=18)[:, :, 2:18], p[:, 128:192].rearrange("p (s t) -> p s t", t=16)), reads=[Bp], writes=[Bu])
                    S.dma("pool", K.conv_p[:, ch, :], u[:, 1152:1154], reads=[Bu], writes=[B_co], semkey="st_cv")
                    S.dma("pool", K.conv_s[:, ch, :, :], u[:, 1154:1226].rearrange("p (s t) -> p s t", t=18)[:, :, 16:18], reads=[Bu], writes=[B_co], semkey="st_cv")
                    zz = z[gv_]; Bz = B_z[gv_]
                    S.op("dve", lambda e, u=u, zz=zz, ch=ch: e.tensor_scalar(zz[:, 0:1224], u[:, 0:1224], wc[:, 0, ch:ch + 1], None, ALU.mult), reads=[Bu, B_wc], writes=[Bz])
                    S.op("dve", lambda e, u=u, zz=zz, ch=ch: e.scalar_tensor_tensor(zz[:, 0:1224], u[:, 1:1225], wc[:, 1, ch:ch + 1], zz[:, 0:1224], ALU.mult, ALU.add), reads=[Bu, B_wc, Bz], writes=[Bz])
                    S.op("dve", lambda e, u=u, zz=zz, ch=ch: e.scalar_tensor_tensor(zz[:, 0:1224], u[:, 2:1226], wc[:, 2, ch:ch + 1], zz[:, 0:1224], ALU.mult, ALU.add), reads=[Bu, B_wc, Bz], writes=[Bz])
                S.op("act", lambda e: e.activation(z[0][:, 0:1224], z[0][:, 0:1224], AF.Silu), reads=[B_z[0]], writes=[B_z[0]])
                S.op("pool", lambda e, j=j: e.tensor_tensor(aT[:, j, 0:1152], z[0][:, 0:1152], z[1][:, 0:1152], ALU.mult), reads=[B_z[0], B_z[1]], writes=[B_aT])
                S.op("pool", lambda e, j=j: e.tensor_tensor(aT[:, j, 1152:1216].rearrange("p (s t) -> p s t", t=16),
                                                            z[0][:, 1154:1226].rearrange("p (s t) -> p s t", t=18)[:, :, 0:16],
                                                            z[1][:, 1154:1226].rearrange("p (s t) -> p s t", t=18)[:, :, 0:16], ALU.mult), reads=[B_z[0], B_z[1]], writes=[B_aT])
            S.barrier()
        with ExitStack() as es:
            sbt = lambda n, s, d: K.sbt(es, n, s, d); pst = lambda n, s, d: K.pst(es, n, s, d)
            rows = sbt("rowsC", [128, 2, 1, D], F32); B_rows = Buf("rowsC")
            S.dma("sp", rows[:, 0, :, :], K.rows_d[3:4, 0, :].partition_broadcast(128), writes=[B_rows], semkey="f_rowsC")
            for s_ in range(4):
                S.dma("sp", rows[16 * s_:16 * s_ + 16, 1, :, :], K.rows_d[3:4, 1 + s_, :].partition_broadcast(16), writes=[B_rows], semkey="f_rowsC")
            pu = [pst("pf%d" % i, [128, 512], F32) for i in range(4)]; B_pu = [Buf("pf%d" % i) for i in range(4)]
            wd = [sbt("wd%d" % i, [128, D], BF16) for i in range(4)]; B_wd = [Buf("wd%d" % i) for i in range(4)]
            x1 = sbt("x1c", [128, D], F32); B_x1 = Buf("x1c")
            tmp = sbt("tmpc", [128, D], F32); B_tmp = Buf("tmpc")
            yo = sbt("yo", [128, D], F32); B_yo = Buf("yo")
            sc = sbt("scc", [128, 8], F32); B_sc = Buf("scc")
            pf = pu[0:4]; B_pf = B_pu[0:4]
            for t in range(1, NT):
                r0, nr = trow(t); ri = 0 if t < 9 else 1
                S.dma("sp", x1[0:nr, :], x1_d[r0:r0 + nr, :], reads=[B_x1d], writes=[B_x1], semkey="f_x1c")
                for j in range(44):
                    w = wd[j % 4]; Bw = B_wd[j % 4]
                    S.dma("sp", w[:, :], K.w_down_b[j * 128:(j + 1) * 128, :], writes=[Bw], semkey="f_wd%d" % (j % 4))
                    for g in range(4):
                        S.op("pe", lambda e, g=g, j=j, w=w: e.matmul(pf[g][0:nr, :], aT[:, j, r0:r0 + nr], w[:, g * 512:(g + 1) * 512], start=(j == 0), stop=(j == 43)),
                             reads=[Bw, B_aT], writes=[B_pf[g]], inc=(g == 3))
                for g in range(4):
                    S.op("act", lambda e, g=g: e.activation(tmp[0:nr, g * 512:(g + 1) * 512], pf[g][0:nr, :], AF.Square, accum_out=sc[0:nr, g:g + 1]), reads=[B_pf[g]], writes=[B_tmp, B_sc])
                S.op("dve", lambda e: e.reduce_sum(sc[0:nr, 4:5], sc[0:nr, 0:4], AX.X), reads=[B_sc], writes=[B_sc])
                S.op("act", lambda e: e.activation(sc[0:nr, 5:6], sc[0:nr, 4:5], AF.Sqrt, bias=K.epsc[0:nr, 0:1], scale=1.0 / D), reads=[B_sc, K.B_const], writes=[B_sc])
                S.op("dve", lambda e: e.reciprocal(sc[0:nr, 5:6], sc[0:nr, 5:6]), reads=[B_sc], writes=[B_sc])
                for g in range(4):
                    gs = slice(g * 512, (g + 1) * 512)
                    S.op("dve", lambda e, g=g, gs=gs: e.scalar_tensor_tensor(tmp[0:nr, gs], pf[g][0:nr, :], sc[0:nr, 5:6], rows[0:nr, ri, 0, gs], ALU.mult, ALU.mult), reads=[B_pf[g], B_sc, B_rows], writes=[B_tmp])
                S.op("pool", lambda e: e.tensor_tensor(yo[0:nr, :], x1[0:nr, :], tmp[0:nr, :], ALU.add), reads=[B_x1, B_tmp], writes=[B_yo])
                S.dma("pool", K.y_o[r0 - 128:r0 - 128 + nr, :], yo[0:nr, :], reads=[B_yo], writes=[B_y], semkey="st_y")
            S.barrier()


_CACHE = {}


def _consts():
    cm = np.zeros((128, 5, 128), np.float32)
    r = np.arange(128)[:, None]; c = np.arange(128)[None, :]
    cm[:, 0, :] = (r == c)
    cm[:, 1, :] = (r % 64) < (c % 64)
    cm[:, 2, :] = (r % 64) <= (c % 64)
    cm[:, 3, :] = (r % 64) > (c % 64)
    cm[:, 4, :] = ((r // 64) == (c // 64)) / 64.0
    return cm


def kernel(x_prompt, x_sample, c_prompt, c_sample, cache_k, cache_v, state_wkv, state_shift,
           state_ffn_conv, w_ada, b_ada, g_pre_mix, g_post_mix, g_pre_ffn, g_post_ffn, w_in,
           lam_q1, lam_k1, lam_q2, lam_k2, g_subln, mu_shift, w0, w_w2, a0, w_a2, w_g2, k_k, k_a,
           r_k, ln_x_w, ln_x_b, w_out, w_up, w_conv_ffn, w_down, _dbg=-1, _cores=None):
    f = lambda a: np.ascontiguousarray(np.asarray(a, dtype=np.float32))
    x_prompt = f(x_prompt); x_sample = f(x_sample); cache_k = f(cache_k); cache_v = f(cache_v)
    shared = {
        "w_ada": f(w_ada)[0], "b_ada": f(b_ada)[0], "g_pre_mix": f(g_pre_mix)[0], "g_post_mix": f(g_post_mix)[0],
        "g_pre_ffn": f(g_pre_ffn)[0], "g_post_ffn": f(g_post_ffn)[0], "w_in": f(w_in)[0],
        "lamv": np.stack([f(lam_q1)[0], f(lam_k1)[0], f(lam_q2)[0], f(lam_k2)[0]]), "g_subln": f(g_subln)[0],
        "mu_shift": f(mu_shift)[0], "w0": f(w0)[0], "w_w2": f(w_w2)[0], "a0": f(a0)[0], "w_a2": f(w_a2)[0],
        "w_g2": f(w_g2)[0], "k_k": f(k_k)[0], "k_a": f(k_a)[0], "r_k": f(r_k)[0].reshape(1024),
        "ln_x_w": f(ln_x_w)[0], "ln_x_b": f(ln_x_b)[0], "w_out": f(w_out)[0], "w_up": f(w_up)[0],
        "w_conv": f(w_conv_ffn)[0], "w_down": f(w_down)[0], "cmask": _consts(),
    }
    in_maps = []
    cores = list(range(8)) if _cores is None else list(_cores)
    for c in cores:
        b, j = c // 4, c % 4
        hi = (j + 1) * 1024; lo = hi - NSEQ
        npad = max(0, -lo)
        xs = np.zeros((NTOK, D), np.float32)
        xs[npad:NSEQ] = x_prompt[b, max(lo, 0):hi]
        xs[NSEQ:] = x_sample[4 * c:4 * c + 4].reshape(64, D)
        vmask = np.ones((NTOK,), np.float32); vmask[:npad] = 0.0
        kb = np.zeros((24,), np.float32)
        for t in range(24):
            if t * 128 < npad: kb[t] = NEG
        cT = np.concatenate([np.asarray(c_prompt, np.float32)[b:b + 1], np.asarray(c_sample, np.float32)[4 * c:4 * c + 4]], 0).T
        m = dict(shared)
        m.update({
            "xT": np.ascontiguousarray(xs.T), "valid": np.ascontiguousarray(np.broadcast_to(vmask[None, :], (128, NTOK))),
            "kbias": np.ascontiguousarray(np.broadcast_to(kb[None, :], (128, 24))),
            "x_tm": np.ascontiguousarray(xs[HALO0:]), "cT": np.ascontiguousarray(cT),
            "cache_kT": np.ascontiguousarray(cache_k[0, 4 * c:4 * c + 4].transpose(0, 2, 3, 1)),
            "cache_v": np.ascontiguousarray(cache_v[0, 4 * c:4 * c + 4]),
            "st_wkv": np.ascontiguousarray(f(state_wkv)[0, 4 * c:4 * c + 4].transpose(0, 1, 3, 2)),
            "st_shift": np.ascontiguousarray(f(state_shift)[0, 4 * c:4 * c + 4].reshape(4, 26, 128).transpose(2, 1, 0)),
            "st_conv": np.ascontiguousarray(f(state_ffn_conv)[0, 4 * c:4 * c + 4].reshape(4, 2, 88, 128).transpose(3, 2, 0, 1)),
        })
        in_maps.append(m)
    if _dbg not in _CACHE:
        _CACHE[_dbg] = build(_dbg)[0]
    nc = _CACHE[_dbg]
    res = run_bass_kernel_spmd(nc, in_maps, core_ids=list(range(len(cores)))).results
    yp = np.zeros((2, 4096, D), np.float32); ys = np.zeros((32, 16, D), np.float32)
    kp = np.zeros((1, 2, 4096, 8, 128), np.float32); vp = np.zeros_like(kp)
    ks = np.zeros((1, 32, 16, 8, 128), np.float32); vs = np.zeros_like(ks)
    wp = np.zeros((1, 2, 16, 64, 64), np.float32); wsm = np.zeros((1, 32, 16, 64, 64), np.float32)
    shp = np.zeros((1, 2, RWC), np.float32); shs = np.zeros((1, 32, RWC), np.float32)
    cp = np.zeros((1, 2, 2, 2 * DFF), np.float32); cs = np.zeros((1, 32, 2, 2 * DFF), np.float32)
    for ci, c in enumerate(cores):
        b, j = c // 4, c % 4; r = res[ci]
        yp[b, j * 1024:(j + 1) * 1024] = r["y_o"][:1024]; ys[4 * c:4 * c + 4] = r["y_o"][1024:].reshape(4, 16, D)
        kp[0, b, j * 1024:(j + 1) * 1024] = r["k_o"][:1024].reshape(1024, 8, 128); ks[0, 4 * c:4 * c + 4] = r["k_o"][1024:].reshape(4, 16, 8, 128)
        vp[0, b, j * 1024:(j + 1) * 1024] = r["v_o"][:1024].reshape(1024, 8, 128); vs[0, 4 * c:4 * c + 4] = r["v_o"][1024:].reshape(4, 16, 8, 128)
        wsm[0, 4 * c:4 * c + 4] = r["wkv_s"].transpose(0, 1, 3, 2)
        shs[0, 4 * c:4 * c + 4] = r["shift_s"].transpose(2, 1, 0).reshape(4, RWC)
        cs[0, 4 * c:4 * c + 4] = r["conv_s"].transpose(2, 3, 1, 0).reshape(4, 2, 2 * DFF)
        if j == 3:
            wp[0, b] = r["wkv_p"].transpose(0, 2, 1)
            shp[0, b] = r["shift_p"].T.reshape(RWC)
            cp[0, b] = r["conv_p"].transpose(2, 1, 0).reshape(2, 2 * DFF)
    return (yp, ys, kp, vp, wp, shp, cp, ks, vs, wsm, shs, cs)
```

```python
import numpy as np
import concourse.bass as bass
import concourse.mybir as mybir
from concourse.bass_utils import run_bass_kernel_spmd
from contextlib import ExitStack
import math

F32 = mybir.dt.float32; BF16 = mybir.dt.bfloat16
AF = mybir.ActivationFunctionType; ALU = mybir.AluOpType; AX = mybir.AxisListType

D = 2048; NSEQ = 4096; PRE = 2944; HALO0 = 2944; OWN0 = 3072; NS = 64; NTOK = 4160
NREG = NTOK - HALO0
INC = 6400; DFF = 5632; RWC = 3328
EPS = 1e-6; GN_EPS = 64e-5
LAM_INIT = 0.8 - 0.6 * math.exp(0.0)
NEG = -30000.0


class Buf:
    __slots__ = ("name", "w", "r")
    def __init__(self, name):
        self.name = name; self.w = None; self.r = {}


class Sched:
    def __init__(self, nc):
        self.nc = nc
        self.eng = {"pe": nc.tensor, "act": nc.scalar, "dve": nc.vector, "pool": nc.gpsimd, "sp": nc.sync}
        self.sems = {}; self.cnt = {}
        self.seen = {e: {} for e in self.eng}
        self._ctx = []
        self.n_inst = 0
    def new_sem(self, key):
        cm = self.nc.semaphore("s%d" % len(self.sems))
        h = cm.__enter__(); self._ctx.append(cm)
        self.sems[key] = h; self.cnt[key] = 0
    def _deps(self, reads, writes):
        deps = []
        for b in reads:
            if b.w is not None: deps.append(b.w)
        for b in writes:
            if b.w is not None: deps.append(b.w)
            deps.extend(b.r.items())
        return deps
    def _wait(self, e, deps):
        need = {}
        for k, t in deps:
            if e == "pe" and k == "e:pe": continue
            if t > need.get(k, 0): need[k] = t
        for k, t in need.items():
            if self.seen[e].get(k, 0) >= t: continue
            self.eng[e].wait_ge(self.sems[k], t)
            self.seen[e][k] = t
    def _mark(self, k, tick, reads, writes):
        for b in reads:
            if b.r.get(k, 0) < tick: b.r[k] = tick
        for b in writes:
            b.w = (k, tick); b.r = {}
    def op(self, e, fn, reads=(), writes=(), inc=True):
        self._wait(e, self._deps(reads, writes))
        ins = fn(self.eng[e])
        self.n_inst += 1
        k = "e:" + e
        if k not in self.sems: self.new_sem(k)
        if inc:
            self.cnt[k] += 1
            ins.then_inc(self.sems[k], 1)
            tick = self.cnt[k]
        else:
            tick = self.cnt[k] + 1
        self._mark(k, tick, reads, writes)
        return ins
    def dma(self, q, out, in_, reads=(), writes=(), semkey=None, **kw):
        self._wait(q, self._deps(reads, writes))
        ins = self.eng[q].dma_start(out=out, in_=in_, **kw)
        self.n_inst += 1
        if reads:
            semkey = "st_" + reads[0].name
        k = "d:" + semkey
        if k not in self.sems: self.new_sem(k)
        self.cnt[k] += 16
        ins.then_inc(self.sems[k], 16)
        self._mark(k, self.cnt[k], reads, writes)
        return ins
    def barrier(self):
        for e in self.eng:
            for k, c in self.cnt.items():
                if c > 0 and self.seen[e].get(k, 0) < c:
                    self.eng[e].wait_ge(self.sems[k], c)
                    self.seen[e][k] = c
    def close(self):
        for cm in reversed(self._ctx): cm.__exit__(None, None, None)


class Ctx:
    pass


def build(dbg=0):
    nc = bass.Bass("TRN2", target_bir_lowering=False)
    S = Sched(nc)
    K = Ctx(); K.nc = nc; K.S = S
    ins = {}; outs = {}
    def din(name, shape):
        ins[name] = nc.dram_tensor(name, list(shape), F32, kind="ExternalInput").ap(); return ins[name]
    def dout(name, shape):
        outs[name] = nc.dram_tensor(name, list(shape), F32, kind="ExternalOutput").ap(); return outs[name]
    def dscr(name, shape, dt):
        return nc.dram_tensor(name, list(shape), dt, kind="Internal").ap()
    xT = din("xT", [D, NTOK]); valid = din("valid", [128, NTOK]); kbias = din("kbias", [128, 24])
    x_tm = din("x_tm", [NREG, D]); cT = din("cT", [D, 5])
    cache_kT = din("cache_kT", [4, 8, 128, 4096]); cache_v = din("cache_v", [4, 4096, 8, 128])
    st_wkv = din("st_wkv", [4, 16, 64, 64]); st_shift = din("st_shift", [128, 26, 4]); st_conv = din("st_conv", [128, 88, 4, 2])
    w_ada = din("w_ada", [D, 6 * D]); b_ada = din("b_ada", [6 * D])
    g_pre_mix = din("g_pre_mix", [D]); g_post_mix = din("g_post_mix", [D]); g_pre_ffn = din("g_pre_ffn", [D]); g_post_ffn = din("g_post_ffn", [D])
    w_in = din("w_in", [D, INC]); lamv = din("lamv", [4, 64]); g_subln = din("g_subln", [128])
    mu_shift = din("mu_shift", [RWC]); w0 = din("w0", [1024]); w_w2 = din("w_w2", [64, 1024]); a0 = din("a0", [1024])
    w_a2 = din("w_a2", [64, 1024]); w_g2 = din("w_g2", [128, 1024]); k_k = din("k_k", [1024]); k_a = din("k_a", [1024])
    r_k = din("r_k", [1024]); ln_x_w = din("ln_x_w", [1024]); ln_x_b = din("ln_x_b", [1024])
    w_out = din("w_out", [D, D]); w_up = din("w_up", [D, 2 * DFF]); w_conv = din("w_conv", [3, 2 * DFF]); w_down = din("w_down", [DFF, D])
    cmask = din("cmask", [128, 5, 128])
    y_o = dout("y_o", [NREG - 128, D])
    k_o = dout("k_o", [NREG - 128, 1024]); v_o = dout("v_o", [NREG - 128, 1024])
    wkv_p = dout("wkv_p", [16, 64, 64]); wkv_s = dout("wkv_s", [4, 16, 64, 64])
    shift_p = dout("shift_p", [128, 26]); shift_s = dout("shift_s", [128, 26, 4])
    conv_p = dout("conv_p", [128, 88, 2]); conv_s = dout("conv_s", [128, 88, 4, 2])
    w_in_b = dscr("w_in_b", [D, INC], BF16); w_out_b = dscr("w_out_b", [D, D], BF16)
    w_up_b = dscr("w_up_b", [D, 2 * DFF], BF16); w_down_b = dscr("w_down_b", [DFF, D], BF16)
    rows_d = dscr("rows_d", [4, 5, D], F32)
    qT_d = dscr("qT_d", [1024, NREG], BF16); kT_d = dscr("kT_d", [1024, NTOK], BF16)
    v_d = dscr("v_d", [NTOK, 1024], F32); pT_d = dscr("pT_d", [RWC, NTOK], F32)
    oT_d = dscr("oT_d", [D, NREG], BF16)
    x1_d = dscr("x1_d", [NREG, D], F32); h2T_d = dscr("h2T_d", [D, NREG], BF16)
    K.__dict__.update(locals())

    es_all = ExitStack()
    uid = [0]
    def sbt(es, name, shape, dt):
        uid[0] += 1
        return es.enter_context(nc.sbuf_tensor("%s_%d" % (name, uid[0]), list(shape), dt))
    def pst(es, name, shape, dt):
        uid[0] += 1
        return es.enter_context(nc.psum_tensor("%s_%d" % (name, uid[0]), list(shape), dt))
    K.sbt = sbt; K.pst = pst
    nc_allow = nc.allow_non_contiguous_dma(reason="small param layouts")
    nc_allow.__enter__()

    cm_f = sbt(es_all, "cm_f", [128, 5, 128], F32); cm_b = sbt(es_all, "cm_b", [128, 5, 128], BF16)
    B_cm = Buf("cm")
    S.dma("sp", cm_f[:, :, :], cmask[:, :, :], writes=[B_cm], semkey="cm_f")
    S.dma("pool", cm_b[:, :, :], cmask[:, :, :], writes=[B_cm], semkey="cm_b")
    ones_b = sbt(es_all, "ones_b", [128, 128], BF16); epsc = sbt(es_all, "epsc", [128, 4], F32)
    B_const = Buf("const")
    S.op("dve", lambda e: e.memset(ones_b[:, :], 1.0), writes=[B_const])
    S.op("dve", lambda e: e.memset(epsc[:, 0:1], EPS), writes=[B_const])
    S.op("dve", lambda e: e.memset(epsc[:, 1:2], GN_EPS), writes=[B_const])
    S.op("dve", lambda e: e.memset(epsc[:, 2:3], 1e-24), writes=[B_const])
    S.op("dve", lambda e: e.memset(epsc[:, 3:4], 0.0), writes=[B_const])
    Gm = sbt(es_all, "Gm", [128, 16, 5], F32); Shm = sbt(es_all, "Shm", [128, 16, 5], F32)
    B_G = Buf("G")
    K.__dict__.update(locals())

    K.B_wcast = Buf("wcast")
    phase_cast(K, "in")
    phase0(K)
    phase_cast(K, "rest")
    phase1(K)
    S.barrier()
    if dbg in (0, -1):
        phase3(K)
        S.barrier()
    if dbg == -1:
        phase4(K)
        S.barrier()
    S.barrier()
    es_all.close()
    nc_allow.__exit__(None, None, None)
    S.close()
    return nc, S


def phase_cast(K, which):
    S = K.S
    lst = {"in": ((K.w_in_b, K.w_in, D),), "rest": ((K.w_out_b, K.w_out, D), (K.w_up_b, K.w_up, D), (K.w_down_b, K.w_down, DFF))}[which]
    for (dst, src, rows) in lst:
        for r0 in range(0, rows, 128):
            S.dma("pool", dst[r0:r0 + 128, :], src[r0:r0 + 128, :], writes=[K.B_wcast], semkey="wcast_" + which)


def phase0(K):
    S = K.S; nc = K.nc
    with ExitStack() as es:
        sbt = lambda n, s, d: K.sbt(es, n, s, d); pst = lambda n, s, d: K.pst(es, n, s, d)
        csb = sbt("csb", [128, 16, 5], F32); sT = sbt("sT", [128, 16, 5], F32)
        bT = sbt("bT", [128, 96], F32); modT = sbt("modT", [128, 96, 5], F32)
        wa = [sbt("wa%d" % i, [128, 16, 512], F32) for i in range(2)]
        psA = [pst("psA%d" % i, [128, 8], F32) for i in range(2)]
        psB = [pst("psB%d" % i, [8, 512], F32) for i in range(2)]
        gv = sbt("gv", [128, 4, 16], F32); drow = sbt("drow", [5, 4, D], F32)
        bg = [sbt("bg%d" % i, [5, 512], F32) for i in range(2)]; gg = [sbt("gg%d" % i, [5, 512], F32) for i in range(2)]
        tmpr = sbt("tmpr", [5, 512], F32)
        B_c = Buf("c"); B_s = Buf("sT"); B_b = Buf("bT"); B_mod = Buf("modT")
        B_wa = [Buf("wa0"), Buf("wa1")]; B_pA = [Buf("pA0"), Buf("pA1")]; B_pB = [Buf("pB0"), Buf("pB1")]
        B_gv = Buf("gv"); B_dr = Buf("drow"); B_rows = Buf("rows_d"); B_bg = [Buf("bg0"), Buf("bg1")]; B_gg = [Buf("gg0"), Buf("gg1")]; B_tmp = Buf("tmpr")
        S.dma("sp", csb[:, :, :], K.cT.rearrange("(c p) r -> p c r", p=128), writes=[B_c], semkey="csb")
        S.dma("sp", bT[:, :], K.b_ada.rearrange("(c p) -> p c", p=128), writes=[B_b], semkey="bT")
        gl = (K.g_pre_mix, K.g_post_mix, K.g_pre_ffn, K.g_post_ffn)
        for i, g in enumerate(gl):
            S.dma("sp", gv[:, i, :], g.rearrange("(c p) -> p c", p=128), writes=[B_gv], semkey="gv")
        S.op("act", lambda e: e.activation(sT[:, :, :], csb[:, :, :], AF.Silu), reads=[B_c], writes=[B_s])
        for g in range(24):
            w = wa[g % 2]; Bw = B_wa[g % 2]
            S.dma("sp", w[:, :, :], K.w_ada[:, g * 512:(g + 1) * 512].rearrange("(c p) n -> p c n", p=128), writes=[Bw], semkey="wa%d" % (g % 2))
            for j in range(4 if g < 8 else 0):
                ch = g * 4 + j; pa = psA[ch % 2]; Bp = B_pA[ch % 2]
                for k in range(16):
                    S.op("pe", lambda e, k=k, j=j, pa=pa: e.matmul(pa[:, 0:5], w[:, k, j * 128:(j + 1) * 128], sT[:, k, :], start=(k == 0), stop=(k == 15)),
                         reads=[Bw, B_s], writes=[Bp], inc=(k == 15))
                S.op("dve", lambda e, pa=pa, ch=ch: e.tensor_scalar(modT[:, ch, :], pa[:, 0:5], bT[:, ch:ch + 1], None, ALU.add), reads=[Bp, B_b], writes=[B_mod])
            sec = g // 4; off = (g % 4) * 512
            if sec < 2: continue
            pb = psB[g % 2]; Bp = B_pB[g % 2]
            S.dma("sp", bg[g % 2][:, :], K.b_ada[g * 512:(g + 1) * 512].partition_broadcast(5), writes=[B_bg[g % 2]], semkey="bg%d" % (g % 2))
            gsel = {2: 1, 4: 2, 5: 3}.get(sec)
            if gsel is not None:
                S.dma("sp", gg[g % 2][:, :], gl[gsel][off:off + 512].partition_broadcast(5), writes=[B_gg[g % 2]], semkey="gg%d" % (g % 2))
            for k in range(16):
                S.op("pe", lambda e, k=k, pb=pb: e.matmul(pb[0:5, :], sT[:, k, :], w[:, k, :], start=(k == 0), stop=(k == 15)),
                     reads=[Bw, B_s], writes=[Bp], inc=(k == 15))
            S.op("dve", lambda e, pb=pb, g=g: e.tensor_tensor(tmpr[:, :], pb[0:5, :], bg[g % 2][:, :], ALU.add), reads=[Bp, B_bg[g % 2]], writes=[B_tmp])
            if sec == 2:
                S.op("dve", lambda e, g=g, off=off: e.tensor_tensor(drow[:, 0, off:off + 512], tmpr[:, :], gg[g % 2][:, :], ALU.mult), reads=[B_tmp, B_gg[g % 2]], writes=[B_dr])
            elif sec == 3:
                S.op("dve", lambda e, g=g, off=off: e.tensor_copy(drow[:, 2, off:off + 512], tmpr[:, :]), reads=[B_tmp], writes=[B_dr])
            elif sec == 4:
                S.op("dve", lambda e, g=g, off=off: e.scalar_tensor_tensor(drow[:, 1, off:off + 512], tmpr[:, :], 1.0, gg[g % 2][:, :], ALU.add, ALU.mult), reads=[B_tmp, B_gg[g % 2]], writes=[B_dr])
            else:
                S.op("dve", lambda e, g=g, off=off: e.tensor_tensor(drow[:, 3, off:off + 512], tmpr[:, :], gg[g % 2][:, :], ALU.mult), reads=[B_tmp, B_gg[g % 2]], writes=[B_dr])
        S.op("dve", lambda e: e.scalar_tensor_tensor(K.Gm[:, :, :], modT[:, 16:32, :], 1.0, gv[:, 0, :].unsqueeze(2).to_broadcast([128, 16, 5]), ALU.add, ALU.mult),
             reads=[B_mod, B_gv], writes=[K.B_G])
        S.op("dve", lambda e: e.tensor_copy(K.Shm[:, :, :], modT[:, 0:16, :]), reads=[B_mod], writes=[K.B_G])
        S.dma("sp", K.rows_d.rearrange("k r d -> r k d"), drow[:, :, :], reads=[B_dr], writes=[B_rows], semkey="rows_d")
        S.barrier()


def phase1(K):
    S = K.S; nc = K.nc
    with ExitStack() as es:
        sbt = lambda n, s, d: K.sbt(es, n, s, d); pst = lambda n, s, d: K.pst(es, n, s, d)
        hT = sbt("hT", [128, 16, 2112], BF16); B_h = Buf("hT")
        xt = sbt("xt", [128, 16, 512], F32); B_x = Buf("xt")
        sq = sbt("sq", [128, 16, 512], BF16); B_sq = Buf("sq")
        vl = sbt("vl", [128, 512], F32); B_vl = Buf("vl")
        rs = sbt("rs", [128, 512], F32); B_rs = Buf("rs")
        t1 = sbt("t1", [128, 512], F32); B_t1 = Buf("t1")
        ta_ = [sbt("ta%d" % i, [128, 512], F32) for i in range(2)]; B_ta = [Buf("ta0"), Buf("ta1")]
        wf = [sbt("wf%d" % i, [128, 16, 128], BF16) for i in range(2)]; B_wf = [Buf("wf0"), Buf("wf1")]
        wt = [sbt("wt%d" % i, [128, 16, 512], BF16) for i in range(2)]; B_wt = [Buf("wt0"), Buf("wt1")]
        stg_b = [sbt("stgb%d" % i, [128, 2112], BF16) for i in range(2)]; B_sb = [Buf("stgb0"), Buf("stgb1")]
        stg_f = [sbt("stgf%d" % i, [128, 2112], F32) for i in range(2)]; B_sf = [Buf("stgf0"), Buf("stgf1")]
        stg_t = [sbt("stgt%d" % i, [128, 512], F32) for i in range(2)]; B_st = [Buf("stgt0"), Buf("stgt1")]
        ps_ss = pst("ps_ss", [128, 512], F32); B_pss = Buf("ps_ss")
        ps = [pst("psm%d" % i, [128, 512], F32) for i in range(4)]; B_ps = [Buf("psm%d" % i) for i in range(4)]
        B_scr = {n: Buf(n) for n in ("qT", "kT", "pT", "v", "out")}
        psi = [0]; evi = [0]
        w_in_v = K.w_in_b.rearrange("(c p) n -> p c n", p=128)
        for sb_i, (tb0, tb1) in enumerate(((0, 2048), (2048, NTOK))):
            nt = tb1 - tb0
            for t0 in range(tb0, tb1, 512):
                n = min(512, tb1 - t0)
                S.dma("sp", xt[:, :, 0:n], K.xT[:, t0:t0 + n].rearrange("(c p) t -> p c t", p=128), writes=[B_x], semkey="xt")
                S.dma("sp", vl[:, 0:n], K.valid[:, t0:t0 + n], writes=[B_vl], semkey="vl")
                S.op("act", lambda e: e.activation(sq[:, :, 0:n], xt[:, :, 0:n], AF.Square), reads=[B_x], writes=[B_sq])
                for k in range(16):
                    S.op("pe", lambda e, k=k: e.matmul(ps_ss[:, 0:n], K.ones_b[:, :], sq[:, k, 0:n], start=(k == 0), stop=(k == 15)),
                         reads=[B_sq, K.B_const], writes=[B_pss], inc=(k == 15))
                S.op("act", lambda e: e.activation(rs[:, 0:n], ps_ss[:, 0:n], AF.Sqrt, bias=K.epsc[:, 0:1], scale=1.0 / D), reads=[B_pss, K.B_const], writes=[B_rs])
                S.op("dve", lambda e: e.reciprocal(rs[:, 0:n], rs[:, 0:n]), reads=[B_rs], writes=[B_rs])
                S.op("dve", lambda e: e.tensor_tensor(rs[:, 0:n], rs[:, 0:n], vl[:, 0:n], ALU.mult), reads=[B_rs, B_vl], writes=[B_rs])
                if t0 + n <= NSEQ:
                    groups = [(0, n, 0)]
                else:
                    groups = [(s * 16, 16, 1 + s) for s in range(4)]
                for k in range(16):
                    for (c0, cn, r) in groups:
                        hs_ = slice(t0 - tb0 + c0, t0 - tb0 + c0 + cn); cs_ = slice(c0, c0 + cn)
                        if k % 2 == 0:
                            S.op("dve", lambda e, k=k, cs_=cs_, r=r: e.scalar_tensor_tensor(t1[:, cs_], xt[:, k, cs_], K.Gm[:, k, r:r + 1], rs[:, cs_], ALU.mult, ALU.mult),
                                 reads=[B_x, K.B_G, B_rs], writes=[B_t1])
                            S.op("dve", lambda e, k=k, cs_=cs_, hs_=hs_, r=r: e.scalar_tensor_tensor(hT[:, k, hs_], vl[:, cs_], K.Shm[:, k, r:r + 1], t1[:, cs_], ALU.mult, ALU.add),
                                 reads=[B_vl, K.B_G, B_t1], writes=[B_h])
                        else:
                            ta = ta_[(k // 2) % 2]; Bta = B_ta[(k // 2) % 2]
                            S.op("act", lambda e, k=k, cs_=cs_, r=r, ta=ta: e.activation(ta[:, cs_], xt[:, k, cs_], AF.Identity, scale=K.Gm[:, k, r:r + 1]), reads=[B_x, K.B_G], writes=[Bta])
                            S.op("pool", lambda e, cs_=cs_, ta=ta: e.tensor_tensor(ta[:, cs_], ta[:, cs_], rs[:, cs_], ALU.mult), reads=[Bta, B_rs], writes=[Bta])
                            S.op("act", lambda e, k=k, cs_=cs_, r=r, ta=ta: e.activation(ta[:, cs_], ta[:, cs_], AF.Identity, bias=K.Shm[:, k, r:r + 1], scale=1.0), reads=[Bta, K.B_G], writes=[Bta])
                            S.op("pool", lambda e, k=k, cs_=cs_, hs_=hs_, ta=ta: e.tensor_tensor(hT[:, k, hs_], ta[:, cs_], vl[:, cs_], ALU.mult), reads=[Bta, B_vl], writes=[B_h])
            chunks = [("kT", 1024 + 128 * i, i, tb0) for i in range(8)] + [("pT", 3072 + 128 * i, i, tb0) for i in range(26)]
            if sb_i == 1:
                chunks += [("qT", 128 * i, i, HALO0) for i in range(8)]
            for ci, (kind, col0, idx, tlo) in enumerate(chunks):
                w = wf[ci % 2]; Bw = B_wf[ci % 2]
                S.dma("sp", w[:, :, :], w_in_v[:, :, col0:col0 + 128], writes=[Bw], semkey="wf%d" % (ci % 2))
                isf = (kind == "pT")
                stg = (stg_f if isf else stg_b)[ci % 2]; Bs = (B_sf if isf else B_sb)[ci % 2]
                for t0 in range(tlo, tb1, 512):
                    n = min(512, tb1 - t0); o = t0 - tb0
                    p = ps[psi[0] % 4]; Bp = B_ps[psi[0] % 4]; psi[0] += 1
                    for k in range(16):
                        S.op("pe", lambda e, k=k, p=p, o=o, n=n: e.matmul(p[:, 0:n], w[:, k, :], hT[:, k, o:o + n], start=(k == 0), stop=(k == 15)),
                             reads=[Bw, B_h], writes=[Bp], inc=(k == 15))
                    ev = "act" if evi[0] % 2 == 0 else "dve"; evi[0] += 1
                    if ev == "act":
                        S.op("act", lambda e, p=p, o=o, n=n: e.copy(stg[:, o:o + n], p[:, 0:n]), reads=[Bp], writes=[Bs])
                    else:
                        S.op("dve", lambda e, p=p, o=o, n=n: e.tensor_copy(stg[:, o:o + n], p[:, 0:n]), reads=[Bp], writes=[Bs])
                lo = tlo - tb0
                if kind == "kT":
                    S.dma("pool", K.kT_d[idx * 128:(idx + 1) * 128, tlo:tb1], stg[:, lo:nt], reads=[Bs], writes=[B_scr["kT"]], semkey="st_kT%d" % (ci % 2))
                elif kind == "qT":
                    S.dma("pool", K.qT_d[idx * 128:(idx + 1) * 128, :], stg[:, lo:nt], reads=[Bs], writes=[B_scr["qT"]], semkey="st_qT%d" % (ci % 2))
                else:
                    S.dma("pool", K.pT_d[idx * 128:(idx + 1) * 128, tlo:tb1], stg[:, lo:nt], reads=[Bs], writes=[B_scr["pT"]], semkey="st_pT%d" % (ci % 2))
                    if sb_i == 1:
                        c = NSEQ - 1 - tb0
                        S.dma("pool", K.shift_p[:, idx:idx + 1], stg[:, c:c + 1], reads=[Bs], writes=[B_scr["out"]], semkey="st_out")
                        c = NSEQ - tb0
                        S.dma("pool", K.shift_s[:, idx, :], stg[:, c:c + 64].rearrange("p (s t) -> p s t", t=16)[:, :, 15], reads=[Bs], writes=[B_scr["out"]], semkey="st_out")
            tms = [("v", 2048 + 512 * g, g, tb0) for g in range(2)]
            if sb_i == 1:
                tms += [("k", 1024 + 512 * g, g, OWN0) for g in range(2)]
            for ci, (kind, col0, g, tlo) in enumerate(tms):
                w = wt[ci % 2]; Bw = B_wt[ci % 2]
                S.dma("sp", w[:, :, :], w_in_v[:, :, col0:col0 + 512], writes=[Bw], semkey="wt%d" % (ci % 2))
                for t0 in range(tlo, tb1, 128):
                    n = min(128, tb1 - t0); o = t0 - tb0
                    p = ps[psi[0] % 4]; Bp = B_ps[psi[0] % 4]; psi[0] += 1
                    for k in range(16):
                        S.op("pe", lambda e, k=k, p=p, o=o, n=n: e.matmul(p[0:n, :], hT[:, k, o:o + n], w[:, k, :], start=(k == 0), stop=(k == 15)),
                             reads=[Bw, B_h], writes=[Bp], inc=(k == 15))
                    st = stg_t[evi[0] % 2]; Bst = B_st[evi[0] % 2]
                    ev = "act" if evi[0] % 2 == 0 else "dve"; evi[0] += 1
                    if ev == "act":
                        S.op("act", lambda e, p=p, n=n, st=st: e.copy(st[0:n, :], p[0:n, :]), reads=[Bp], writes=[Bst])
                    else:
                        S.op("dve", lambda e, p=p, n=n, st=st: e.tensor_copy(st[0:n, :], p[0:n, :]), reads=[Bp], writes=[Bst])
                    if kind == "v":
                        S.dma("pool", K.v_d[t0:t0 + n, g * 512:(g + 1) * 512], st[0:n, :], reads=[Bst], writes=[B_scr["v"]], semkey="st_v")
                    if t0 >= OWN0:
                        dst = K.v_o if kind == "v" else K.k_o
                        S.dma("pool", dst[t0 - OWN0:t0 - OWN0 + n, g * 512:(g + 1) * 512], st[0:n, :], reads=[Bst], writes=[B_scr["out"]], semkey="st_out")
        S.barrier()


def phase3(K):
    S = K.S; nc = K.nc
    NMAX = 256; NCM = 4
    S_ = K.S
    class Emit:
        def __init__(self): self.prog = None
        def op(self, *a, **k):
            return getattr(S_, "op")(*a, **k)
        def dma(self, *a, **k):
            return getattr(S_, "dma")(*a, **k)
    EM = Emit()
    with ExitStack() as es:
        sbt = lambda n, s, d: K.sbt(es, n, s, d); pst = lambda n, s, d: K.pst(es, n, s, d)
        TS = [dict(), dict()]; BS = [dict(), dict()]
        def mk(name, shape, dt, shared=False):
            if shared:
                t = sbt(name, shape, dt); bb = Buf(name)
                for li in range(2):
                    TS[li][name] = t; BS[li][name] = bb
            else:
                for li in range(2):
                    TS[li][name] = sbt(name + "_l%d" % li, shape, dt); BS[li][name] = Buf(name + "_l%d" % li)
        T = TS[0]; B = BS[0]
        pv = sbt("pv", [128, 7, 8], F32); B_pv = Buf("pv")
        mu = sbt("mu", [128, 26], F32)
        ww2 = sbt("ww2", [128, 1024], BF16); wa2 = sbt("wa2", [128, 1024], BF16); wg2 = sbt("wg2", [128, 1024], BF16)
        for i, v in enumerate((K.w0, K.a0, K.k_k, K.k_a, K.r_k, K.ln_x_w, K.ln_x_b)):
            EM.dma("sp", pv[:, i, :], v.rearrange("(c p) -> p c", p=128), writes=[B_pv], semkey="pv")
        EM.dma("sp", mu[:, :], K.mu_shift.rearrange("(c p) -> p c", p=128), writes=[B_pv], semkey="pv")
        EM.dma("pool", ww2[0:64, :], K.w_w2[:, :], writes=[B_pv], semkey="pvc")
        EM.dma("pool", wa2[64:128, :], K.w_a2[:, :], writes=[B_pv], semkey="pvc")
        EM.dma("pool", wg2[:, :], K.w_g2[:, :], writes=[B_pv], semkey="pvc")
        segm = sbt("segm", [128, NMAX], F32); vrw = sbt("vrw", [128, 64], F32)
        EM.op("dve", lambda e: e.memset(segm[:, :], 1.0), writes=[B_pv])
        EM.op("dve", lambda e: e.memset(segm[:, :].rearrange("p (c t) -> p c t", t=64)[:, :, 0:1], 0.0), writes=[B_pv])
        EM.op("dve", lambda e: e.memset(vrw[:, 0:48], 0.0), writes=[B_pv])
        EM.op("dve", lambda e: e.memset(vrw[:, 48:64], 1.0), writes=[B_pv])
        for nm in ("pr_r", "pr_k", "pr_v", "pr_wa", "pr_gd"):
            mk(nm, [128, NMAX + 1], F32)
        for nm in ("dtmp", "xr", "xk", "xv", "xwa", "xgd", "sigw", "av", "gvv", "ld", "L", "eL", "enL", "eLm", "kk", "kk2", "rn", "kp", "tb", "tk", "bv", "yT", "yc", "ysq"):
            mk(nm, [128, NMAX], F32)
        mk("twa", [128, NMAX], BF16); mk("sg", [128, NMAX], BF16); mk("ob", [128, NMAX], BF16)
        mk("PC", [128, NCM], F32)
        mk("bdAR", [128, NCM, 256], BF16); mk("bdB", [128, NCM, 128], BF16); mk("bdK", [128, NCM, 128], BF16)
        mk("bdBh", [128, NCM, 128], BF16); mk("bdKh", [128, NCM, 128], BF16); mk("bdV", [128, NCM, 128], BF16)
        for li in range(2):
            for nm in ("bdAR", "bdB", "bdK", "bdBh", "bdKh", "bdV"):
                EM.op("pool", lambda e, nm=nm, li=li: e.memset(TS[li][nm][:, :, :], 0.0), writes=[BS[li][nm]])
        mk("S1", [128, NCM, 256], BF16); mk("S2", [128, NCM, 256], BF16)
        for nm in ("N0", "N1", "M0", "M1", "T0", "T1", "Vb", "Kh", "Bh", "At", "AV", "Dm", "XT", "Gm", "Qc"):
            mk(nm, [128, NCM, 128], BF16)
        mk("E", [128, NCM, 128], F32); mk("Hall", [128, NCM + 1, 128], BF16)
        mk("tmpH0", [128, 128], F32); mk("tmpH1", [128, 128], F32)
        mk("Hf", [128, 128], F32)
        mk("mask2", [128, 2, 256], BF16, True); mk("identg", [128, 4, 128], BF16, True); mk("mSLg", [128, 4, 128], BF16, True)
        for i in range(2):
            EM.op("pool", lambda e, i=i: e.tensor_copy(T["mask2"][:, i, :], K.cm_b[:, 1:3, :].rearrange("p a b -> p (a b)")), reads=[K.B_cm], writes=[B["mask2"]])
        for i in range(4):
            EM.op("pool", lambda e, i=i: e.tensor_copy(T["identg"][:, i, :], K.cm_b[:, 0, :]), reads=[K.B_cm], writes=[B["identg"]])
            EM.op("pool", lambda e, i=i: e.tensor_copy(T["mSLg"][:, i, :], K.cm_b[:, 3, :]), reads=[K.B_cm], writes=[B["mSLg"]])
        banks = [pst("pb%d" % i, [128, 512], F32) for i in range(7)]
        bankt = pst("pbt", [128, 1024], BF16)
        bkB = [Buf("bank%d" % i) for i in range(8)]
        rots = [[0], [0]]
        cnts = [{"z": 0, "inv": 0, "ev": 0}, {"z": 0, "inv": 0, "ev": 0}]
        B_oT = Buf("oT"); B_out = Buf("out3")
        ident = K.cm_b[:, 0, :]; mSU = K.cm_b[:, 1, :]; mIU = K.cm_b[:, 2, :]; mSL = K.cm_b[:, 3, :]
        onesf = K.cm_f[:, 4, :]
        C1 = -math.exp(-0.5)

        def v3(ap, lo, hi, nch):
            return ap[lo:hi, 0:nch * 64].rearrange("p (c t) -> p c t", t=64)

        def evac(dst, dstB, src, srcB, extra_reads=()):
            e = "act" if cnt["ev"] % 2 == 0 else "dve"; cnt["ev"] += 1
            if e == "act":
                EM.op("act", lambda en: en.copy(dst, src), reads=[srcB] + list(extra_reads), writes=[dstB])
            else:
                EM.op("dve", lambda en: en.tensor_copy(dst, src), reads=[srcB] + list(extra_reads), writes=[dstB])

        def mm(pname, lhsT, rhs, rb, start=True, stop=True):
            EM.op("pe", lambda e: e.matmul(P[pname], lhsT, rhs, start=start, stop=stop), reads=rb, writes=[BP[pname]], inc=stop)

        def rw_block(li, hp, t0, nch, with_y, samp=None):
            T = TS[li]; B = BS[li]; cnt = cnts[li]; rot = rots[li]
            P = {"pz0": banks[2 * li], "pz1": banks[2 * li + 1]}; BP = {"pz0": bkB[2 * li], "pz1": bkB[2 * li + 1]}
            def nbank():
                i = 2 * li + (rot[0] % 2); rot[0] += 1
                return banks[i], bkB[i]
            def evac(dst, dstB, src, srcB, extra_reads=()):
                e = "act" if cnt["ev"] % 4 != 3 else "dve"; cnt["ev"] += 1
                if e == "act":
                    yield EM.op("act", lambda en: en.copy(dst, src), reads=[srcB] + list(extra_reads), writes=[dstB])
                else:
                    yield EM.op("dve", lambda en: en.tensor_copy(dst, src), reads=[srcB] + list(extra_reads), writes=[dstB])
            n = nch * 64
            chs = {"pr_r": hp, "pr_k": 8 + hp, "pr_v": 16 + hp, "pr_wa": 24, "pr_gd": 25}
            for nm, ch in chs.items():
                t = T[nm]; rows = K.pT_d[ch * 128:(ch + 1) * 128, :]
                if samp is not None:
                    yield EM.op("pool", lambda e, t=t: e.memset(t[:, 0:48], 0.0), writes=[B[nm]])
                    yield EM.dma("sp", t[:, 48:49], K.st_shift[:, ch, samp:samp + 1], writes=[B[nm]], semkey="ld%d_" % li + nm)
                    yield EM.dma("sp", t[:, 49:65], rows[:, NSEQ + 16 * samp:NSEQ + 16 * samp + 16], writes=[B[nm]], semkey="ld%d_" % li + nm)
                elif t0 == 0:
                    yield EM.op("pool", lambda e, t=t: e.memset(t[:, 0:1], 0.0), writes=[B[nm]])
                    yield EM.dma("sp", t[:, 1:n + 1], rows[:, 0:n], writes=[B[nm]], semkey="ld%d_" % li + nm)
                else:
                    yield EM.dma("sp", t[:, 0:n + 1], rows[:, t0 - 1:t0 + n], writes=[B[nm]], semkey="ld%d_" % li + nm)
            for nm, xn, ch in (("pr_r", "xr", hp), ("pr_k", "xk", 8 + hp), ("pr_v", "xv", 16 + hp), ("pr_wa", "xwa", 24), ("pr_gd", "xgd", 25)):
                t = T[nm]
                yield EM.op("pool", lambda e, t=t: e.tensor_tensor(T["dtmp"][:, 0:n], t[:, 0:n], t[:, 1:n + 1], ALU.subtract), reads=[B[nm]], writes=[B["dtmp"]])
                yield EM.op("dve", lambda e, t=t, xn=xn, ch=ch: e.scalar_tensor_tensor(T[xn][:, 0:n], T["dtmp"][:, 0:n], mu[:, ch:ch + 1], t[:, 1:n + 1], ALU.mult, ALU.add),
                     reads=[B["dtmp"], B[nm], B_pv], writes=[B[xn]])
            if samp is not None:
                for xn in ("xr", "xk", "xv", "xwa", "xgd"):
                    yield EM.op("pool", lambda e, xn=xn: e.tensor_tensor(T[xn][:, 0:n], T[xn][:, 0:n], vrw[:, 0:n], ALU.mult), reads=[B[xn], B_pv], writes=[B[xn]])
            yield EM.op("act", lambda e: e.activation(T["twa"][0:64, 0:n], T["xwa"][0:64, 0:n], AF.Tanh), reads=[B["xwa"]], writes=[B["twa"]])
            yield EM.op("dve", lambda e: e.tensor_copy(T["twa"][64:128, 0:n], T["xwa"][64:128, 0:n]), reads=[B["xwa"]], writes=[B["twa"]])
            yield EM.op("act", lambda e: e.activation(T["sg"][:, 0:n], T["xgd"][:, 0:n], AF.Sigmoid), reads=[B["xgd"]], writes=[B["sg"]])
            cs = slice(hp * 128, (hp + 1) * 128)
            for c0 in range(0, n, 512):
                cn = min(512, n - c0)
                pz = "pz%d" % (cnt["z"] % 2); cnt["z"] += 1
                yield EM.op("pe", lambda e, pz=pz: e.matmul(P[pz][:, 0:cn], ww2[0:64, cs], T["twa"][0:64, c0:c0 + cn], start=True, stop=True), reads=[B["twa"], B_pv], writes=[BP[pz]])
                yield EM.op("act", lambda e, pz=pz: e.activation(T["sigw"][:, c0:c0 + cn], P[pz][:, 0:cn], AF.Sigmoid, bias=pv[:, 0, hp:hp + 1], scale=1.0), reads=[BP[pz], B_pv], writes=[B["sigw"]])
                pz = "pz%d" % (cnt["z"] % 2); cnt["z"] += 1
                yield EM.op("pe", lambda e, pz=pz: e.matmul(P[pz][:, 0:cn], wa2[64:128, cs], T["twa"][64:128, c0:c0 + cn], start=True, stop=True), reads=[B["twa"], B_pv], writes=[BP[pz]])
                yield EM.op("act", lambda e, pz=pz: e.activation(T["av"][:, c0:c0 + cn], P[pz][:, 0:cn], AF.Sigmoid, bias=pv[:, 1, hp:hp + 1], scale=1.0), reads=[BP[pz], B_pv], writes=[B["av"]])
                if with_y:
                    pz = "pz%d" % (cnt["z"] % 2); cnt["z"] += 1
                    yield EM.op("pe", lambda e, pz=pz: e.matmul(P[pz][:, 0:cn], wg2[:, cs], T["sg"][:, c0:c0 + cn], start=True, stop=True), reads=[B["sg"], B_pv], writes=[BP[pz]])
                    yield EM.op("dve", lambda e, pz=pz: e.tensor_copy(T["gvv"][:, c0:c0 + cn], P[pz][:, 0:cn]), reads=[BP[pz]], writes=[B["gvv"]])
            if samp is not None:
                yield EM.op("dve", lambda e: e.scalar_tensor_tensor(T["ld"][:, 0:n], T["sigw"][:, 0:n], C1, vrw[:, 0:n], ALU.mult, ALU.mult), reads=[B["sigw"], B_pv], writes=[B["ld"]])
            else:
                yield EM.op("dve", lambda e: e.tensor_scalar(T["ld"][:, 0:n], T["sigw"][:, 0:n], C1, None, ALU.mult), reads=[B["sigw"]], writes=[B["ld"]])
            yield EM.op("dve", lambda e: e.tensor_tensor_scan(T["L"][:, 0:n], segm[:, 0:n], T["ld"][:, 0:n], 0.0, ALU.mult, ALU.add), reads=[B["ld"], B_pv], writes=[B["L"]])
            yield EM.op("act", lambda e: e.activation(T["eL"][:, 0:n], T["L"][:, 0:n], AF.Exp), reads=[B["L"]], writes=[B["eL"]])
            yield EM.op("act", lambda e: e.activation(T["enL"][:, 0:n], T["L"][:, 0:n], AF.Exp, scale=-1.0), reads=[B["L"]], writes=[B["enL"]])
            yield EM.op("pool", lambda e: e.tensor_tensor(T["eLm"][:, 0:n], T["L"][:, 0:n], T["ld"][:, 0:n], ALU.subtract), reads=[B["L"], B["ld"]], writes=[B["eLm"]])
            yield EM.op("act", lambda e: e.activation(T["eLm"][:, 0:n], T["eLm"][:, 0:n], AF.Exp), reads=[B["eLm"]], writes=[B["eLm"]])
            yield EM.op("dve", lambda e: e.tensor_copy(T["PC"][:, 0:nch], T["eL"][:, 0:n].rearrange("p (c t) -> p c t", t=64)[:, :, 63]), reads=[B["eL"]], writes=[B["PC"]])
            yield EM.op("dve", lambda e: e.tensor_scalar(T["kk"][:, 0:n], T["xk"][:, 0:n], pv[:, 2, hp:hp + 1], None, ALU.mult), reads=[B["xk"], B_pv], writes=[B["kk"]])
            yield EM.op("pool", lambda e: e.tensor_tensor(T["kk2"][:, 0:n], T["kk"][:, 0:n], T["kk"][:, 0:n], ALU.mult), reads=[B["kk"]], writes=[B["kk2"]])
            for c0 in range(0, n, 512):
                cn = min(512, n - c0)
                pz = "pz%d" % (cnt["z"] % 2); cnt["z"] += 1
                yield EM.op("pe", lambda e, pz=pz: e.matmul(P[pz][:, 0:cn], onesf, T["kk2"][:, c0:c0 + cn], start=True, stop=True), reads=[B["kk2"], K.B_cm], writes=[BP[pz]])
                yield EM.op("act", lambda e, pz=pz: e.activation(T["rn"][:, c0:c0 + cn], P[pz][:, 0:cn], AF.Sqrt, bias=K.epsc[:, 2:3], scale=64.0), reads=[BP[pz], K.B_const], writes=[B["rn"]])
            yield EM.op("dve", lambda e: e.reciprocal(T["rn"][:, 0:n], T["rn"][:, 0:n]), reads=[B["rn"]], writes=[B["rn"]])
            yield EM.op("pool", lambda e: e.tensor_tensor(T["kk"][:, 0:n], T["kk"][:, 0:n], T["rn"][:, 0:n], ALU.mult), reads=[B["kk"], B["rn"]], writes=[B["kk"]])
            yield EM.op("dve", lambda e: e.tensor_scalar(T["kp"][:, 0:n], T["av"][:, 0:n], -1.0, pv[:, 3, hp:hp + 1], ALU.add, ALU.mult), reads=[B["av"], B_pv], writes=[B["kp"]])
            yield EM.op("dve", lambda e: e.scalar_tensor_tensor(T["kp"][:, 0:n], T["kp"][:, 0:n], 1.0, T["xk"][:, 0:n], ALU.add, ALU.mult), reads=[B["kp"], B["xk"]], writes=[B["kp"]])
            if with_y:
                yield EM.op("dve", lambda e: e.scalar_tensor_tensor(T["kk2"][:, 0:n], T["xr"][:, 0:n], pv[:, 4, hp:hp + 1], T["kp"][:, 0:n], ALU.mult, ALU.mult), reads=[B["xr"], B["kp"], B_pv], writes=[B["kk2"]])
                for c0 in range(0, n, 512):
                    cn = min(512, n - c0)
                    pz = "pz%d" % (cnt["z"] % 2); cnt["z"] += 1
                    yield EM.op("pe", lambda e, pz=pz: e.matmul(P[pz][:, 0:cn], onesf, T["kk2"][:, c0:c0 + cn], start=True, stop=True), reads=[B["kk2"], K.B_cm], writes=[BP[pz]])
                    yield EM.op("dve", lambda e, pz=pz: e.scalar_tensor_tensor(T["bv"][:, c0:c0 + cn], P[pz][:, 0:cn], 64.0, T["xv"][:, c0:c0 + cn], ALU.mult, ALU.mult), reads=[BP[pz], B["xv"]], writes=[B["bv"]])
            yield EM.op("pool", lambda e: e.tensor_tensor(T["tb"][:, 0:n], T["kk"][:, 0:n], T["av"][:, 0:n], ALU.mult), reads=[B["kk"], B["av"]], writes=[B["tb"]])
            yield EM.op("pool", lambda e: e.tensor_tensor(T["tb"][:, 0:n], T["tb"][:, 0:n], T["enL"][:, 0:n], ALU.mult), reads=[B["tb"], B["enL"]], writes=[B["tb"]])
            yield EM.op("pool", lambda e: e.tensor_tensor(T["tk"][:, 0:n], T["kp"][:, 0:n], T["enL"][:, 0:n], ALU.mult), reads=[B["kp"], B["enL"]], writes=[B["tk"]])
            for (lo, hi) in ((0, 64), (64, 128)):
                co = slice(lo, hi)
                pcb = T["PC"][lo:hi, 0:nch].unsqueeze(2).to_broadcast([64, nch, 64])
                yield EM.op("dve", lambda e, lo=lo, hi=hi: e.tensor_tensor(T["bdAR"][lo:hi, 0:nch, 128 + lo:128 + hi], v3(T["xr"], lo, hi, nch), v3(T["eL"], lo, hi, nch), ALU.mult), reads=[B["xr"], B["eL"]], writes=[B["bdAR"]])
                yield EM.op("dve", lambda e, lo=lo, hi=hi: e.scalar_tensor_tensor(T["bdAR"][lo:hi, 0:nch, lo:hi], v3(T["kk"], lo, hi, nch), -1.0, v3(T["eLm"], lo, hi, nch), ALU.mult, ALU.mult), reads=[B["kk"], B["eLm"]], writes=[B["bdAR"]])
                yield EM.op("pool", lambda e, lo=lo, hi=hi: e.tensor_copy(T["bdB"][lo:hi, 0:nch, lo:hi], v3(T["tb"], lo, hi, nch)), reads=[B["tb"]], writes=[B["bdB"]])
                yield EM.op("pool", lambda e, lo=lo, hi=hi: e.tensor_copy(T["bdK"][lo:hi, 0:nch, lo:hi], v3(T["tk"], lo, hi, nch)), reads=[B["tk"]], writes=[B["bdK"]])
                yield EM.op("pool", lambda e, lo=lo, hi=hi: e.tensor_copy(T["bdV"][lo:hi, 0:nch, lo:hi], v3(T["xv"], lo, hi, nch)), reads=[B["xv"]], writes=[B["bdV"]])
                yield EM.op("dve", lambda e, lo=lo, hi=hi, pcb=pcb: e.tensor_tensor(T["bdBh"][lo:hi, 0:nch, lo:hi], v3(T["tb"], lo, hi, nch), pcb, ALU.mult), reads=[B["tb"], B["PC"]], writes=[B["bdBh"]])
                yield EM.op("dve", lambda e, lo=lo, hi=hi, pcb=pcb: e.tensor_tensor(T["bdKh"][lo:hi, 0:nch, lo:hi], v3(T["tk"], lo, hi, nch), pcb, ALU.mult), reads=[B["tk"], B["PC"]], writes=[B["bdKh"]])
            def grp(dst, width, lhs_fn, rhs_fn, rb, post=None, gsz=None):
                gsz = gsz or (512 // width)
                for c0 in range(0, nch, gsz):
                    g = min(gsz, nch - c0)
                    bk, Bb = nbank()
                    for i in range(g):
                        yield EM.op("pe", lambda e, i=i, c=c0 + i: e.matmul(bk[:, i * width:(i + 1) * width], lhs_fn(c), rhs_fn(c), start=True, stop=True), reads=rb, writes=[Bb], inc=(i == g - 1))
                    src = bk[:, 0:g * width].rearrange("p (g t) -> p g t", t=width)
                    d = T[dst][:, c0:c0 + g, :]
                    if post is None:
                        yield from evac(d, B[dst], src, Bb)
                    else:
                        yield from post(d, src, Bb, c0, g)
            def post_mask2(dst):
                def f(d, src, Bb, c0, g):
                    yield EM.op("dve", lambda e: e.tensor_tensor(d, src, T["mask2"][:, 0:g, :], ALU.mult), reads=[Bb, B["mask2"]], writes=[B[dst]])
                return f
            A_ = lambda c: T["bdAR"][:, c, 0:128]; R_ = lambda c: T["bdAR"][:, c, 128:256]
            yield from grp("S1", 256, lambda c: T["bdB"][:, c, :], lambda c: T["bdAR"][:, c, :], [B["bdB"], B["bdAR"]], post=post_mask2("S1"))
            yield from grp("S2", 256, lambda c: T["bdK"][:, c, :], lambda c: T["bdAR"][:, c, :], [B["bdK"], B["bdAR"]], post=post_mask2("S2"))
            def post_N(d, src, Bb, c0, g):
                yield EM.op("dve", lambda e: e.tensor_tensor(d, src, T["mSLg"][:, 0:g, :], ALU.mult), reads=[Bb, B["mSLg"]], writes=[B["N0"]])
            yield from grp("N0", 128, A_, lambda c: T["bdB"][:, c, :], [B["bdB"], B["bdAR"]], post=post_N)
            yield EM.op("pool", lambda e: e.tensor_copy(T["M0"][:, 0:nch, :], T["S1"][:, 0:nch, 0:128]), reads=[B["S1"]], writes=[B["M0"]])
            for c0 in range(0, nch, 4):
                g = min(4, nch - c0)
                yield EM.op("pool", lambda e, c0=c0, g=g: e.tensor_tensor(T["T0"][:, c0:c0 + g, :], T["S1"][:, c0:c0 + g, 0:128], T["identg"][:, 0:g, :], ALU.add), reads=[B["S1"], B["identg"]], writes=[B["T0"]])
            cur = 0
            for k in range(1, 6):
                nx = 1 - cur
                Nc, Mc, Tc = "N%d" % cur, "M%d" % cur, "T%d" % cur; Nx, Mx, Tx = "N%d" % nx, "M%d" % nx, "T%d" % nx
                yield from grp(Nx, 128, lambda c: T[Mc][:, c, :], lambda c: T[Nc][:, c, :], [B[Mc], B[Nc]])
                if k < 5:
                    yield from grp(Mx, 128, lambda c: T[Nc][:, c, :], lambda c: T[Mc][:, c, :], [B[Mc], B[Nc]])
                def post_T(d, src, Bb, c0, g, Tc=Tc, Tx=Tx):
                    yield EM.op("dve", lambda e: e.tensor_tensor(d, src, T[Tc][:, c0:c0 + g, :], ALU.add), reads=[Bb, B[Tc]], writes=[B[Tx]])
                yield from grp(Tx, 128, lambda c: T[Nx][:, c, :], lambda c: T[Tc][:, c, :], [B[Nx], B[Tc]], post=post_T)
                cur = nx
            TT = "T%d" % cur
            for src_n, dst_n, sl_ in (("bdV", "Vb", None), ("bdKh", "Kh", None), ("bdBh", "Bh", None), ("bdAR", "At", slice(0, 128))):
                for c0 in range(0, nch, 4):
                    g = min(4, nch - c0)
                    for i in range(g):
                        c = c0 + i
                        src_ap = T[src_n][:, c, :] if sl_ is None else T[src_n][:, c, sl_]
                        yield EM.op("pe", lambda e, i=i, src_ap=src_ap: e.transpose(bankt[:, li * 512 + i * 128:li * 512 + (i + 1) * 128], src_ap, ident), reads=[B[src_n], K.B_cm], writes=[bkB[7]])
                    yield from evac(T[dst_n][:, c0:c0 + g, :], B[dst_n], bankt[:, li * 512:li * 512 + g * 128].rearrange("p (g t) -> p g t", t=128), bkB[7])
            yield from grp("AV", 128, lambda c: T["S2"][:, c, 0:128], lambda c: T["Vb"][:, c, :], [B["S2"], B["Vb"]])
            yield from grp("Dm", 128, lambda c: T[TT][:, c, :], lambda c: T["AV"][:, c, :], [B[TT], B["AV"]])
            yield from grp("XT", 128, lambda c: T[TT][:, c, :], lambda c: T["At"][:, c, :], [B[TT], B["At"]])
            yield from grp("Gm", 128, lambda c: T["XT"][:, c, :], lambda c: T["Bh"][:, c, :], [B["XT"], B["Bh"]])
            for c0 in range(0, nch, 4):
                g = min(4, nch - c0)
                bk, Bb = nbank()
                for i in range(g):
                    c = c0 + i
                    yield EM.op("pe", lambda e, i=i, c=c: e.matmul(bk[:, i * 128:(i + 1) * 128], T["Kh"][:, c, :], T["Vb"][:, c, :], start=True, stop=False), reads=[B["Kh"], B["Vb"]], writes=[Bb], inc=False)
                    yield EM.op("pe", lambda e, i=i, c=c: e.matmul(bk[:, i * 128:(i + 1) * 128], T["Bh"][:, c, :], T["Dm"][:, c, :], start=False, stop=True), reads=[B["Bh"], B["Dm"]], writes=[Bb], inc=(i == g - 1))
                yield EM.op("act", lambda e, c0=c0, g=g, bk=bk: e.copy(T["E"][:, c0:c0 + g, :], bk[:, 0:g * 128].rearrange("p (g t) -> p g t", t=128)), reads=[Bb], writes=[B["E"]])
            if with_y:
                def post_Q(d, src, Bb, c0, g):
                    yield EM.op("dve", lambda e: e.tensor_tensor(d, src, T["bdAR"][:, c0:c0 + g, 128:256], ALU.add), reads=[Bb, B["bdAR"]], writes=[B["Qc"]])
                yield from grp("Qc", 128, lambda c: T["XT"][:, c, :], lambda c: T["S1"][:, c, 128:256], [B["XT"], B["S1"]], post=post_Q)
            yield EM.op("act", lambda e: e.copy(T["Hall"][:, 0, :], T["Hf"][:, :]), reads=[B["Hf"]], writes=[B["Hall"]])
            for c in range(nch):
                tmpn = "tmpH%d" % (c % 2)
                yield EM.op("dve", lambda e, c=c, tmpn=tmpn: e.scalar_tensor_tensor(T[tmpn][:, :], T["Hf"][:, :], T["PC"][:, c:c + 1], T["E"][:, c, :], ALU.mult, ALU.add), reads=[B["Hf"], B["PC"], B["E"]], writes=[B[tmpn]])
                bk = banks[2 * li]; Bb = bkB[2 * li]
                yield EM.op("pe", lambda e, c=c, bk=bk: e.matmul(bk[:, 0:128], T["Gm"][:, c, :], T["Hall"][:, c, :], start=True, stop=True), reads=[B["Gm"], B["Hall"]], writes=[Bb])
                yield EM.op("dve", lambda e, c=c, bk=bk, tmpn=tmpn: e.tensor_tensor(T["Hall"][:, c + 1, :], bk[:, 0:128], T[tmpn][:, :], ALU.add), reads=[Bb, B[tmpn]], writes=[B["Hall"]])
                yield EM.op("dve", lambda e, c=c, bk=bk, tmpn=tmpn: e.tensor_tensor(T["Hf"][:, :], bk[:, 0:128], T[tmpn][:, :], ALU.add), reads=[Bb, B[tmpn]], writes=[B["Hf"]])
            if with_y:
                for c0 in range(0, nch, 4):
                    g = min(4, nch - c0)
                    bk, Bb = nbank()
                    for i in range(g):
                        c = c0 + i; reg = bk[:, i * 128:(i + 1) * 128]
                        yield EM.op("pe", lambda e, c=c, reg=reg: e.matmul(reg, T["Hall"][:, c, :], T["Qc"][:, c, :], start=True, stop=False), reads=[B["Hall"], B["Qc"]], writes=[Bb], inc=False)
                        yield EM.op("pe", lambda e, c=c, reg=reg: e.matmul(reg, T["Dm"][:, c, :], T["S1"][:, c, 128:256], start=False, stop=False), reads=[B["Dm"], B["S1"]], writes=[Bb], inc=False)
                        yield EM.op("pe", lambda e, c=c, reg=reg: e.matmul(reg, T["Vb"][:, c, :], T["S2"][:, c, 128:256], start=False, stop=True), reads=[B["Vb"], B["S2"]], writes=[Bb], inc=(i == g - 1))
                    v = bk[:, 0:g * 128].rearrange("p (g t) -> p g t", t=128)
                    yield EM.op("act", lambda e, c0=c0, g=g, v=v: e.copy(T["yT"][0:64, c0 * 64:(c0 + g) * 64].rearrange("p (g t) -> p g t", t=64), v[0:64, :, 0:64]), reads=[Bb], writes=[B["yT"]])
                    yield EM.op("dve", lambda e, c0=c0, g=g, v=v: e.tensor_copy(T["yT"][64:128, c0 * 64:(c0 + g) * 64].rearrange("p (g t) -> p g t", t=64), v[64:128, :, 64:128]), reads=[Bb], writes=[B["yT"]])
            if not with_y:
                return
            for c0 in range(0, n, 512):
                cn = min(512, n - c0)
                pz = "pz%d" % (cnt["z"] % 2); cnt["z"] += 1
                yield EM.op("pe", lambda e, pz=pz: e.matmul(P[pz][:, 0:cn], onesf, T["yT"][:, c0:c0 + cn], start=True, stop=True), reads=[B["yT"], K.B_cm], writes=[BP[pz]])
                yield EM.op("dve", lambda e, pz=pz: e.tensor_tensor(T["yc"][:, c0:c0 + cn], T["yT"][:, c0:c0 + cn], P[pz][:, 0:cn], ALU.subtract), reads=[B["yT"], BP[pz]], writes=[B["yc"]])
                yield EM.op("pool", lambda e: e.tensor_tensor(T["ysq"][:, c0:c0 + cn], T["yc"][:, c0:c0 + cn], T["yc"][:, c0:c0 + cn], ALU.mult), reads=[B["yc"]], writes=[B["ysq"]])
                pz = "pz%d" % (cnt["z"] % 2); cnt["z"] += 1
                yield EM.op("pe", lambda e, pz=pz: e.matmul(P[pz][:, 0:cn], onesf, T["ysq"][:, c0:c0 + cn], start=True, stop=True), reads=[B["ysq"], K.B_cm], writes=[BP[pz]])
                yield EM.op("act", lambda e, pz=pz: e.activation(T["ysq"][:, c0:c0 + cn], P[pz][:, 0:cn], AF.Sqrt, bias=K.epsc[:, 1:2], scale=1.0), reads=[BP[pz], K.B_const], writes=[B["ysq"]])
            yield EM.op("dve", lambda e: e.reciprocal(T["ysq"][:, 0:n], T["ysq"][:, 0:n]), reads=[B["ysq"]], writes=[B["ysq"]])
            yield EM.op("pool", lambda e: e.tensor_tensor(T["yc"][:, 0:n], T["yc"][:, 0:n], T["ysq"][:, 0:n], ALU.mult), reads=[B["yc"], B["ysq"]], writes=[B["yc"]])
            yield EM.op("dve", lambda e: e.tensor_scalar(T["yc"][:, 0:n], T["yc"][:, 0:n], pv[:, 5, hp:hp + 1], pv[:, 6, hp:hp + 1], ALU.mult, ALU.add), reads=[B["yc"], B_pv], writes=[B["yc"]])
            yield EM.op("pool", lambda e: e.tensor_tensor(T["yc"][:, 0:n], T["yc"][:, 0:n], T["bv"][:, 0:n], ALU.add), reads=[B["yc"], B["bv"]], writes=[B["yc"]])
            yield EM.op("dve", lambda e: e.tensor_tensor(T["ob"][:, 0:n], T["yc"][:, 0:n], T["gvv"][:, 0:n], ALU.mult), reads=[B["yc"], B["gvv"]], writes=[B["ob"]])
            rows = K.oT_d[1024 + hp * 128:1024 + (hp + 1) * 128, :]
            if samp is None:
                yield EM.dma("sp", rows[:, t0 - HALO0:t0 - HALO0 + n], T["ob"][:, 0:n], reads=[B["ob"]], writes=[B_oT], semkey="st_oT%d" % li)
            else:
                yield EM.dma("sp", rows[:, 1152 + 16 * samp:1152 + 16 * samp + 16], T["ob"][:, 48:64], reads=[B["ob"]], writes=[B_oT], semkey="st_oT%d" % li)

        def pair_prog(li, hp):
            T = TS[li]; B = BS[li]
            yield EM.op("pool", lambda e: e.memset(T["Hf"][:, :], 0.0), writes=[B["Hf"]])
            blocks = [(c0 * 64, min(NCM, 46 - c0), False) for c0 in range(0, 46, NCM)] + [(2944 + c0 * 64, min(NCM, 18 - c0), True) for c0 in range(0, 18, NCM)]
            for (t0, nch, wy) in blocks:
                yield from rw_block(li, hp, t0, nch, wy)
            yield EM.dma("sp", K.wkv_p[2 * hp, :, :], T["Hf"][0:64, 0:64], reads=[B["Hf"]], writes=[B_out], semkey="st_out3")
            yield EM.dma("sp", K.wkv_p[2 * hp + 1, :, :], T["Hf"][64:128, 64:128], reads=[B["Hf"]], writes=[B_out], semkey="st_out3")
            for s_ in range(4):
                yield EM.op("pool", lambda e: e.memset(T["Hf"][:, :], 0.0), writes=[B["Hf"]])
                yield EM.dma("sp", T["Hf"][0:64, 0:64], K.st_wkv[s_, 2 * hp, :, :], writes=[B["Hf"]], semkey="ld_H%d" % li)
                yield EM.dma("sp", T["Hf"][64:128, 64:128], K.st_wkv[s_, 2 * hp + 1, :, :], writes=[B["Hf"]], semkey="ld_H%d" % li)
                yield from rw_block(li, hp, 0, 1, True, samp=s_)
                yield EM.dma("sp", K.wkv_s[s_, 2 * hp, :, :], T["Hf"][0:64, 0:64], reads=[B["Hf"]], writes=[B_out], semkey="st_out3")
                yield EM.dma("sp", K.wkv_s[s_, 2 * hp + 1, :, :], T["Hf"][64:128, 64:128], reads=[B["Hf"]], writes=[B_out], semkey="st_out3")
        def lane_prog(li):
            for hp in range(li, 8, 2):
                yield from pair_prog(li, hp)
        g2 = phase2_gen(K, es, banks, bkB)
        lanes = [lane_prog(0), lane_prog(1)]
        alive = [g2] + lanes
        while alive:
            for g_ in list(alive):
                try:
                    next(g_)
                except StopIteration:
                    alive.remove(g_)
        S.barrier()


def phase2_gen(K, es, banks, bkB):
    S = K.S; nc = K.nc
    if True:
        sbt = lambda n, s, d: K.sbt(es, n, s, d); pst = lambda n, s, d: K.pst(es, n, s, d)
        T = {}; B = {}
        def mk(name, shape, dt):
            T[name] = sbt(name, shape, dt); B[name] = Buf(name)
        mk("lt", [128, 4, 64], F32); mk("lp", [128, 2, 64], F32); mk("lam", [128, 4], F32)
        mk("gsub", [128, 128], F32); mk("kb", [128, 24], F32)
        yield S.dma("sp", T["lt"][:, :, :], K.lamv.partition_broadcast(128), writes=[B["lt"]], semkey="a_lt")
        yield S.dma("sp", T["gsub"][:, :], K.g_subln.partition_broadcast(128), writes=[B["gsub"]], semkey="a_gs")
        yield S.dma("sp", T["kb"][:, :], K.kbias[:, :], writes=[B["kb"]], semkey="a_kb")
        yield S.op("dve", lambda e: e.tensor_tensor(T["lp"][:, :, :], T["lt"][:, :, :].rearrange("p (a b) d -> p a b d", b=2)[:, :, 0, :], T["lt"][:, :, :].rearrange("p (a b) d -> p a b d", b=2)[:, :, 1, :], ALU.mult), reads=[B["lt"]], writes=[B["lp"]])
        yield S.op("dve", lambda e: e.reduce_sum(T["lam"][:, 0:2], T["lp"][:, :, :], AX.X), reads=[B["lp"]], writes=[B["lam"]])
        yield S.op("act", lambda e: e.activation(T["lam"][:, 0:2], T["lam"][:, 0:2], AF.Exp), reads=[B["lam"]], writes=[B["lam"]])
        yield S.op("dve", lambda e: e.tensor_tensor(T["lam"][:, 2:3], T["lam"][:, 1:2], T["lam"][:, 0:1], ALU.subtract), reads=[B["lam"]], writes=[B["lam"]])
        yield S.op("dve", lambda e: e.tensor_scalar(T["lam"][:, 3:4], T["lam"][:, 2:3], -LAM_INIT, None, ALU.add), reads=[B["lam"]], writes=[B["lam"]])
        yield S.op("dve", lambda e: e.tensor_scalar(T["gsub"][:, :], T["gsub"][:, :], 1.0 - LAM_INIT, None, ALU.mult), reads=[B["gsub"]], writes=[B["gsub"]])
        nlam = T["lam"][:, 3:4]
        mk("kTh", [128, NTOK], BF16); mk("qTh", [128, NREG], BF16); mk("Vh", [128, 32, 129], BF16); mk("Vn", [16, 4, 129], BF16)
        for i in range(1):
            mk("ckT%d" % i, [128, 4096], BF16); mk("cV%d" % i, [128, 32, 129], BF16)
        for s_ in range(4):
            mk("Pp%d" % s_, [128, 32, 64], BF16); mk("Pn%d" % s_, [16, 64], BF16)
            yield S.op("pool", lambda e, s_=s_: e.memset(T["Pp%d" % s_][:, :, :], 0.0), writes=[B["Pp%d" % s_]])
            yield S.op("pool", lambda e, s_=s_: e.memset(T["Pn%d" % s_][:, :], 0.0), writes=[B["Pn%d" % s_]])
        for nm in ("Vh", "cV0"):
            yield S.op("pool", lambda e, nm=nm: e.memset(T[nm][:, :, 128:129], 1.0), writes=[B[nm]])
        yield S.op("pool", lambda e: e.memset(T["Vn"][:, :, 128:129], 1.0), writes=[B["Vn"]])
        for i in range(4):
            mk("PT%d" % i, [128, 384], BF16)
        mk("t0", [128, 128], F32); mk("o", [128, 128], F32); mk("osq", [128, 128], F32); mk("ob", [128, 128], F32)
        mk("sc", [128, 8], F32); mk("oTs", [128, NREG], BF16)
        PS = [banks[6][:, 0:384]]; BPS = [bkB[6]]
        PO = [[banks[4 + m][:, q * 129:(q + 1) * 129] for q in range(3)] for m in range(2)]; BPO = [[bkB[4 + m]] * 3 for m in range(2)]
        PSs = banks[6][:, :]; BPSs = bkB[6]
        PTR = [banks[6][:, 384:512]]; BPTR = [bkB[6]]
        B_oT = Buf("oTa")
        ident = K.cm_f[:, 0, :]
        cnt = {"s": 0, "tr": 0, "pt": 0}

        def finalize(h, np_, O0, O1, BO0, BO1, col0):
            sc = T["sc"]
            yield S.op("dve", lambda e: e.tensor_scalar(sc[0:np_, 0:1], O0[0:np_, 128:129], 1e-30, None, ALU.max), reads=[BO0], writes=[B["sc"]])
            yield S.op("dve", lambda e: e.tensor_scalar(sc[0:np_, 1:2], O1[0:np_, 128:129], 1e-30, None, ALU.max), reads=[BO1], writes=[B["sc"]])
            yield S.op("dve", lambda e: e.reciprocal(sc[0:np_, 0:2], sc[0:np_, 0:2]), reads=[B["sc"]], writes=[B["sc"]])
            yield S.op("dve", lambda e: e.tensor_tensor(sc[0:np_, 1:2], sc[0:np_, 1:2], nlam[0:np_, :], ALU.mult), reads=[B["sc"], B["lam"]], writes=[B["sc"]])
            yield S.op("dve", lambda e: e.tensor_scalar(T["t0"][0:np_, :], O0[0:np_, 0:128], sc[0:np_, 0:1], None, ALU.mult), reads=[BO0, B["sc"]], writes=[B["t0"]])
            yield S.op("dve", lambda e: e.scalar_tensor_tensor(T["o"][0:np_, :], O1[0:np_, 0:128], sc[0:np_, 1:2], T["t0"][0:np_, :], ALU.mult, ALU.add), reads=[BO1, B["sc"], B["t0"]], writes=[B["o"]])
            yield S.op("act", lambda e: e.activation(T["osq"][0:np_, :], T["o"][0:np_, :], AF.Square, accum_out=sc[0:np_, 2:3]), reads=[B["o"]], writes=[B["osq"], B["sc"]])
            yield S.op("act", lambda e: e.activation(sc[0:np_, 3:4], sc[0:np_, 2:3], AF.Sqrt, bias=K.epsc[0:np_, 0:1], scale=1.0 / 128), reads=[B["sc"], K.B_const], writes=[B["sc"]])
            yield S.op("dve", lambda e: e.reciprocal(sc[0:np_, 3:4], sc[0:np_, 3:4]), reads=[B["sc"]], writes=[B["sc"]])
            yield S.op("dve", lambda e: e.scalar_tensor_tensor(T["ob"][0:np_, :], T["o"][0:np_, :], sc[0:np_, 3:4], T["gsub"][0:np_, :], ALU.mult, ALU.mult), reads=[B["o"], B["sc"], B["gsub"]], writes=[B["ob"]])
            pt = PTR[0]; Bpt = BPTR[0]
            yield S.op("pe", lambda e: e.transpose(pt[:, 0:np_], T["ob"][0:np_, :], ident[0:np_, 0:np_]), reads=[B["ob"], K.B_cm], writes=[Bpt])
            yield S.op("act", lambda e: e.copy(T["oTs"][:, col0:col0 + np_], pt[:, 0:np_]), reads=[Bpt], writes=[B["oTs"]])

        vd_v = K.v_d[0:NSEQ, :].rearrange("(t p) c -> p t c", p=128)
        for h in range(8):
            hs = slice(h * 128, (h + 1) * 128)
            yield S.dma("sp", T["kTh"][:, :], K.kT_d[hs, :], writes=[B["kTh"]], semkey="a_kT")
            yield S.dma("sp", T["qTh"][:, :], K.qT_d[hs, :], writes=[B["qTh"]], semkey="a_qT")
            yield S.dma("pool", T["Vh"][:, :, 0:128], vd_v[:, :, hs], writes=[B["Vh"]], semkey="a_Vh")
            yield S.dma("pool", T["Vn"][:, :, 0:128], K.v_d[NSEQ:NTOK, hs].rearrange("(s t) c -> t s c", t=16), writes=[B["Vn"]], semkey="a_Vn")
            for G in range(3):
                last_kt = 23 + 3 * G + 2
                for kt in range(last_kt + 1):
                    vis = [qi for qi in range(3) if 23 + 3 * G + qi >= kt]
                    for m in range(2):
                        ms = slice(64 * m, 64 * m + 64)
                        si = 0; pti = cnt["pt"] % 4; cnt["pt"] += 1
                        yield S.op("pe", lambda e, si=si, ms=ms: e.matmul(PS[si], T["kTh"][ms, kt * 128:(kt + 1) * 128], T["qTh"][ms, G * 384:(G + 1) * 384], start=True, stop=True),
                             reads=[B["kTh"], B["qTh"]], writes=[BPS[si]])
                        PT = T["PT%d" % pti]; BPT = B["PT%d" % pti]
                        if kt < 24:
                            yield S.op("act", lambda e, si=si, PT=PT: e.activation(PT[:, :], PS[si], AF.Exp, bias=T["kb"][:, kt:kt + 1], scale=0.125), reads=[BPS[si], B["kb"]], writes=[BPT])
                        else:
                            yield S.op("act", lambda e, si=si, PT=PT: e.activation(PT[:, :], PS[si], AF.Exp, scale=0.125), reads=[BPS[si]], writes=[BPT])
                        for qi in vis:
                            if 23 + 3 * G + qi == kt:
                                yield S.op("pool", lambda e, PT=PT, qi=qi: e.memset(PT[64:128, qi * 128:qi * 128 + 64], 0.0), writes=[BPT])
                        for qi in vis:
                            yield S.op("pe", lambda e, PT=PT, qi=qi, m=m: e.matmul(PO[m][qi], PT[:, qi * 128:(qi + 1) * 128], T["Vh"][:, kt, :], start=(kt == 0 and qi == 0), stop=(kt == 23 + 3 * G + qi), skip_group_check=True),
                                 reads=[BPT, B["Vh"]], writes=[BPO[m][qi]], inc=(kt == 23 + 3 * G + qi))
                for qi in range(3):
                    yield from finalize(h, 128, PO[0][qi], PO[1][qi], BPO[0][qi], BPO[1][qi], (3 * G + qi) * 128)
            Os = [banks[4][0:64, 0:129], banks[5][0:64, 0:129]]; BOs = [BPO[0][0], BPO[1][0]]
            for s_ in range(4):
                i = 0
                ck = T["ckT%d" % i]; cv = T["cV%d" % i]; Bck = B["ckT%d" % i]; Bcv = B["cV%d" % i]
                yield S.dma("pool", ck[:, :], K.cache_kT[s_, h, :, :], writes=[Bck], semkey="a_ck%d" % i)
                yield S.dma("pool", cv[:, :, 0:128], K.cache_v[s_, :, h, :].rearrange("(t p) d -> p t d", p=128), writes=[Bcv], semkey="a_cv%d" % i)
                Pp = T["Pp%d" % s_]; Pn = T["Pn%d" % s_]; BPp = B["Pp%d" % s_]; BPn = B["Pn%d" % s_]
                qc = slice(1152 + 16 * s_, 1152 + 16 * s_ + 16)
                for m in range(2):
                    ms = slice(64 * m, 64 * m + 64)
                    for kt in range(32):
                        yield S.op("pe", lambda e, kt=kt, ms=ms: e.matmul(PSs[:, kt * 16:(kt + 1) * 16], ck[ms, kt * 128:(kt + 1) * 128], T["qTh"][ms, qc], start=True, stop=True),
                             reads=[Bck, B["qTh"]], writes=[BPSs], inc=(kt == 31))
                    yield S.op("act", lambda e: e.activation(Pp[:, :, 16 * s_:16 * s_ + 16], PSs.rearrange("p (k q) -> p k q", q=16), AF.Exp, scale=0.125), reads=[BPSs], writes=[BPp])
                    for kt in range(32):
                        yield S.op("pe", lambda e, kt=kt, m=m: e.matmul(Os[m], Pp[:, kt, :], cv[:, kt, :], start=(s_ == 0 and kt == 0), stop=False, skip_group_check=True),
                             reads=[BPp, Bcv], writes=[BOs[m]], inc=False)
                    si = 0
                    yield S.op("pe", lambda e, si=si, ms=ms: e.matmul(PS[si][0:16, 0:16], T["kTh"][ms, NSEQ + 16 * s_:NSEQ + 16 * s_ + 16], T["qTh"][ms, qc], start=True, stop=True),
                         reads=[B["kTh"], B["qTh"]], writes=[BPS[si]])
                    yield S.op("act", lambda e, si=si: e.activation(Pn[:, 16 * s_:16 * s_ + 16], PS[si][0:16, 0:16], AF.Exp, scale=0.125), reads=[BPS[si]], writes=[BPn])
                    yield S.op("pe", lambda e, m=m: e.matmul(Os[m], Pn[:, :], T["Vn"][:, s_, :], start=False, stop=(s_ == 3), skip_group_check=True),
                         reads=[BPn, B["Vn"]], writes=[BOs[m]], inc=True)
            yield from finalize(h, 64, Os[0], Os[1], BOs[0], BOs[1], 1152)
            yield S.dma("sp", K.oT_d[hs, :], T["oTs"][:, :], reads=[B["oTs"]], writes=[B_oT], semkey="st_oTa")


def phase4(K):
    S = K.S; nc = K.nc
    x1_d = K.x1_d
    NT = 10
    def trow(t):
        return (t * 128, 128) if t < 9 else (1152, 64)
    h2T_d = K.h2T_d
    with ExitStack() as es0:
        B_x1d = Buf("x1d"); B_y = Buf("y_out"); B_h2d = Buf("h2d")
        ident = K.cm_b[:, 0, :]
        with ExitStack() as es:
            sbt = lambda n, s, d: K.sbt(es, n, s, d); pst = lambda n, s, d: K.pst(es, n, s, d)
            rows = sbt("rowsA", [128, 2, 3, D], F32); B_rows = Buf("rowsA")
            S.dma("sp", rows[:, 0, :, :], K.rows_d[0:3, 0, :].partition_broadcast(128), writes=[B_rows], semkey="f_rows")
            for s_ in range(4):
                S.dma("sp", rows[16 * s_:16 * s_ + 16, 1, :, :], K.rows_d[0:3, 1 + s_, :].partition_broadcast(16), writes=[B_rows], semkey="f_rows")
            h2Ts = [sbt("h2Ts%d" % i, [128, 16, 128], BF16) for i in range(2)]; B_h2Ts = [Buf("h2Ts0"), Buf("h2Ts1")]
            wo = sbt("wo", [128, 16, D], BF16); B_wo = Buf("wo")
            S.dma("sp", wo[:, :, :], K.w_out_b.rearrange("(c p) n -> p c n", p=128), writes=[B_wo], semkey="f_wo")
            oTt = [sbt("oTt%d" % i, [128, 16, 128], BF16) for i in range(2)]; B_oTt = [Buf("oTt0"), Buf("oTt1")]
            xt = sbt("xt4", [128, D], F32); B_xt = Buf("xt4")
            tmp = sbt("tmp4", [128, D], F32); B_tmp = Buf("tmp4")
            x1 = sbt("x1", [128, D], F32); B_x1 = Buf("x1")
            h2 = sbt("h2", [128, D], BF16); B_h2 = Buf("h2")
            sc = sbt("sc4", [128, 16], F32); B_sc = Buf("sc4")
            vcol = sbt("vcol", [128, 1], F32); B_vc = Buf("vcol")
            pm = [pst("pm%d" % i, [128, 512], F32) for i in range(4)]; B_pm = [Buf("pm%d" % i) for i in range(4)]
            ptb = pst("ptb", [128, 1024], BF16); PTR = [ptb[:, i * 128:(i + 1) * 128] for i in range(4)]; _bp4 = Buf("ptr4"); B_ptr = [_bp4] * 4
            tr = [0]
            for t in range(NT):
                r0, nr = trow(t); ri = 0 if t < 9 else 1
                o = oTt[t % 2]; Bo = B_oTt[t % 2]
                S.dma("sp", o[:, :, 0:nr], K.oT_d[:, r0:r0 + nr].rearrange("(c p) t -> p c t", p=128), writes=[Bo], semkey="f_oT%d" % (t % 2))
                S.dma("sp", xt[0:nr, :], K.x_tm[r0:r0 + nr, :], writes=[B_xt], semkey="f_xt")
                S.dma("sp", vcol[0:nr, :], K.valid[0:1, HALO0 + r0:HALO0 + r0 + nr].rearrange("o t -> t o"), writes=[B_vc], semkey="f_vc")
                for g in range(4):
                    for k in range(16):
                        S.op("pe", lambda e, g=g, k=k: e.matmul(pm[g][0:nr, :], o[:, k, 0:nr], wo[:, k, g * 512:(g + 1) * 512], start=(k == 0), stop=(k == 15)),
                             reads=[Bo, B_wo], writes=[B_pm[g]], inc=(k == 15))
                for g in range(4):
                    S.op("act", lambda e, g=g: e.activation(tmp[0:nr, g * 512:(g + 1) * 512], pm[g][0:nr, :], AF.Square, accum_out=sc[0:nr, g:g + 1]), reads=[B_pm[g]], writes=[B_tmp, B_sc])
                S.op("dve", lambda e: e.reduce_sum(sc[0:nr, 4:5], sc[0:nr, 0:4], AX.X), reads=[B_sc], writes=[B_sc])
                S.op("act", lambda e: e.activation(sc[0:nr, 5:6], sc[0:nr, 4:5], AF.Sqrt, bias=K.epsc[0:nr, 0:1], scale=1.0 / D), reads=[B_sc, K.B_const], writes=[B_sc])
                S.op("dve", lambda e: e.reciprocal(sc[0:nr, 5:6], sc[0:nr, 5:6]), reads=[B_sc], writes=[B_sc])
                for g in range(4):
                    gs = slice(g * 512, (g + 1) * 512)
                    S.op("dve", lambda e, g=g, gs=gs: e.scalar_tensor_tensor(tmp[0:nr, gs], pm[g][0:nr, :], sc[0:nr, 5:6], rows[0:nr, ri, 0, gs], ALU.mult, ALU.mult), reads=[B_pm[g], B_sc, B_rows], writes=[B_tmp])
                S.op("pool", lambda e: e.tensor_tensor(x1[0:nr, :], xt[0:nr, :], tmp[0:nr, :], ALU.add), reads=[B_xt, B_tmp], writes=[B_x1])
                S.dma("pool", x1_d[r0:r0 + nr, :], x1[0:nr, :], reads=[B_x1], writes=[B_x1d], semkey="st_x1")
                S.op("act", lambda e: e.activation(tmp[0:nr, :], x1[0:nr, :], AF.Square, accum_out=sc[0:nr, 6:7]), reads=[B_x1], writes=[B_tmp, B_sc])
                S.op("act", lambda e: e.activation(sc[0:nr, 7:8], sc[0:nr, 6:7], AF.Sqrt, bias=K.epsc[0:nr, 0:1], scale=1.0 / D), reads=[B_sc, K.B_const], writes=[B_sc])
                S.op("dve", lambda e: e.reciprocal(sc[0:nr, 7:8], sc[0:nr, 7:8]), reads=[B_sc], writes=[B_sc])
                S.op("dve", lambda e: e.tensor_tensor(sc[0:nr, 7:8], sc[0:nr, 7:8], vcol[0:nr, :], ALU.mult), reads=[B_sc, B_vc], writes=[B_sc])
                S.op("dve", lambda e: e.scalar_tensor_tensor(tmp[0:nr, :], x1[0:nr, :], sc[0:nr, 7:8], rows[0:nr, ri, 1, :], ALU.mult, ALU.mult), reads=[B_x1, B_sc, B_rows], writes=[B_tmp])
                S.op("dve", lambda e: e.scalar_tensor_tensor(h2[0:nr, :], rows[0:nr, ri, 2, :], vcol[0:nr, :], tmp[0:nr, :], ALU.mult, ALU.add), reads=[B_rows, B_vc, B_tmp], writes=[B_h2])
                for c in range(16):
                    pt = PTR[tr[0] % 4]; Bpt = B_ptr[tr[0] % 4]; tr[0] += 1
                    S.op("pe", lambda e, c=c, pt=pt: e.transpose(pt[:, 0:nr], h2[0:nr, c * 128:(c + 1) * 128], ident[0:nr, 0:nr]), reads=[B_h2, K.B_cm], writes=[Bpt])
                    if c % 2 == 0:
                        S.op("act", lambda e, c=c, pt=pt: e.copy(h2Ts[t % 2][:, c, 0:nr], pt[:, 0:nr]), reads=[Bpt], writes=[B_h2Ts[t % 2]])
                    else:
                        S.op("dve", lambda e, c=c, pt=pt: e.tensor_copy(h2Ts[t % 2][:, c, 0:nr], pt[:, 0:nr]), reads=[Bpt], writes=[B_h2Ts[t % 2]])
                S.dma("pool", h2T_d[:, r0:r0 + nr].rearrange("(c p) t -> p c t", p=128), h2Ts[t % 2][:, :, 0:nr], reads=[B_h2Ts[t % 2]], writes=[B_h2d], semkey="st_h2")
            S.barrier()
        aT = K.sbt(es0, "aT", [128, 44, NREG], BF16); B_aT = Buf("aT")
        with ExitStack() as es:
            sbt = lambda n, s, d: K.sbt(es, n, s, d); pst = lambda n, s, d: K.pst(es, n, s, d)
            NE = 1226
            h2T = sbt("h2T", [128, 16, NREG], BF16); B_h2T = Buf("h2T")
            S.dma("sp", h2T[:, :, :], h2T_d.rearrange("(c p) t -> p c t", p=128), reads=[B_h2d], writes=[B_h2T], semkey="f_h2T")
            wc = sbt("wc", [128, 3, 88], F32); B_wc = Buf("wc")
            for kk_ in range(3):
                S.dma("sp", wc[:, kk_, :], K.w_conv[kk_, :].rearrange("(c p) -> p c", p=128), writes=[B_wc], semkey="f_wc")
            wu = [sbt("wu%d" % i, [128, 16, 128], BF16) for i in range(4)]; B_wu = [Buf("wu%d" % i) for i in range(4)]
            ue = [sbt("ue%d" % i, [128, NE], F32) for i in range(2)]; B_ue = [Buf("ue0"), Buf("ue1")]
            z = [sbt("z%d" % i, [128, NE], F32) for i in range(2)]; B_z = [Buf("z0"), Buf("z1")]
            for i in range(2):
                S.op("pool", lambda e, i=i: e.memset(ue[i][:, 0:2], 0.0), writes=[B_ue[i]])
            pu = [pst("pu%d" % i, [128, 512], F32) for i in range(6)]; B_pu = [Buf("pu%d" % i) for i in range(6)]
            B_co = Buf("conv_out")
            NTL = ((0, 512), (512, 512), (1024, 192))
            for j in range(44):
                for gv_ in range(2):
                    ch = j + 44 * gv_
                    w = wu[(2 * j + gv_) % 4]; Bw = B_wu[(2 * j + gv_) % 4]
                    S.dma("sp", w[:, :, :], K.w_up_b[:, ch * 128:(ch + 1) * 128].rearrange("(c p) n -> p c n", p=128), writes=[Bw], semkey="f_wu%d" % ((2 * j + gv_) % 4))
                    u = ue[gv_]; Bu = B_ue[gv_]
                    S.dma("sp", u[:, 1154:1226].rearrange("p (s t) -> p s t", t=18)[:, :, 0:2], K.st_conv[:, ch, :, :], writes=[Bu], semkey="f_cv%d" % gv_)
                    for ti, (c0, cn) in enumerate(NTL):
                        p = pu[3 * gv_ + ti]; Bp = B_pu[3 * gv_ + ti]
                        for k in range(16):
                            S.op("pe", lambda e, k=k, p=p, c0=c0, cn=cn: e.matmul(p[:, 0:cn], w[:, k, :], h2T[:, k, c0:c0 + cn], start=(k == 0), stop=(k == 15)),
                                 reads=[Bw, B_h2T], writes=[Bp], inc=(k == 15))
                        if ti < 2:
                            S.op("act", lambda e, p=p, c0=c0, cn=cn, u=u: e.copy(u[:, 2 + c0:2 + c0 + cn], p[:, 0:cn]), reads=[Bp], writes=[Bu])
                        else:
                            S.op("act", lambda e, p=p, u=u: e.copy(u[:, 2 + 1024:2 + 1152], p[:, 0:128]), reads=[Bp], writes=[Bu])
                            S.op("dve", lambda e, p=p, u=u: e.tensor_copy(u[:, 1154:1226].rearrange("p (s t) -> p s t", t=18)[:, :, 2:18], p[:, 128:192].rearrange("p (s t) -> p s t", t=16)), reads=[Bp], writes=[Bu])
                    S.dma("pool", K.conv_p[:, ch, :], u[:, 1152:1154], reads=[Bu], writes=[B_co], semkey="st_cv")
                    S.dma("pool", K.conv_s[:, ch, :, :], u[:, 1154:1226].rearrange("p (s t) -> p s t", t=18)[:, :, 16:18], reads=[Bu], writes=[B_co], semkey="st_cv")
                    zz = z[gv_]; Bz = B_z[gv_]
                    S.op("dve", lambda e, u=u, zz=zz, ch=ch: e.tensor_scalar(zz[:, 0:1224], u[:, 0:1224], wc[:, 0, ch:ch + 1], None, ALU.mult), reads=[Bu, B_wc], writes=[Bz])
                    S.op("dve", lambda e, u=u, zz=zz, ch=ch: e.scalar_tensor_tensor(zz[:, 0:1224], u[:, 1:1225], wc[:, 1, ch:ch + 1], zz[:, 0:1224], ALU.mult, ALU.add), reads=[Bu, B_wc, Bz], writes=[Bz])
                    S.op("dve", lambda e, u=u, zz=zz, ch=ch: e.scalar_tensor_tensor(zz[:, 0:1224], u[:, 2:1226], wc[:, 2, ch:ch + 1], zz[:, 0:1224], ALU.mult, ALU.add), reads=[Bu, B_wc, Bz], writes=[Bz])
                S.op("act", lambda e: e.activation(z[0][:, 0:1224], z[0][:, 0:1224], AF.Silu), reads=[B_z[0]], writes=[B_z[0]])
                S.op("pool", lambda e, j=j: e.tensor_tensor(aT[:, j, 0:1152], z[0][:, 0:1152], z[1][:, 0:1152], ALU.mult), reads=[B_z[0], B_z[1]], writes=[B_aT])
                S.op("pool", lambda e, j=j: e.tensor_tensor(aT[:, j, 1152:1216].rearrange("p (s t) -> p s t", t=16),
                                                            z[0][:, 1154:1226].rearrange("p (s t) -> p s t", t=18)[:, :, 0:16],
                                                            z[1][:, 1154:1226].rearrange("p (s t) -> p s t", t=18)[:, :, 0:16], ALU.mult), reads=[B_z[0], B_z[1]], writes=[B_aT])
            S.barrier()
        with ExitStack() as es:
            sbt = lambda n, s, d: K.sbt(es, n, s, d); pst = lambda n, s, d: K.pst(es, n, s, d)
            rows = sbt("rowsC", [128, 2, 1, D], F32); B_rows = Buf("rowsC")
            S.dma("sp", rows[:, 0, :, :], K.rows_d[3:4, 0, :].partition_broadcast(128), writes=[B_rows], semkey="f_rowsC")
            for s_ in range(4):
                S.dma("sp", rows[16 * s_:16 * s_ + 16, 1, :, :], K.rows_d[3:4, 1 + s_, :].partition_broadcast(16), writes=[B_rows], semkey="f_rowsC")
            pu = [pst("pf%d" % i, [128, 512], F32) for i in range(4)]; B_pu = [Buf("pf%d" % i) for i in range(4)]
            wd = [sbt("wd%d" % i, [128, D], BF16) for i in range(4)]; B_wd = [Buf("wd%d" % i) for i in range(4)]
            x1 = sbt("x1c", [128, D], F32); B_x1 = Buf("x1c")
            tmp = sbt("tmpc", [128, D], F32); B_tmp = Buf("tmpc")
            yo = sbt("yo", [128, D], F32); B_yo = Buf("yo")
            sc = sbt("scc", [128, 8], F32); B_sc = Buf("scc")
            pf = pu[0:4]; B_pf = B_pu[0:4]
            for t in range(1, NT):
                r0, nr = trow(t); ri = 0 if t < 9 else 1
                S.dma("sp", x1[0:nr, :], x1_d[r0:r0 + nr, :], reads=[B_x1d], writes=[B_x1], semkey="f_x1c")
                for j in range(44):
                    w = wd[j % 4]; Bw = B_wd[j % 4]
                    S.dma("sp", w[:, :], K.w_down_b[j * 128:(j + 1) * 128, :], writes=[Bw], semkey="f_wd%d" % (j % 4))
                    for g in range(4):
                        S.op("pe", lambda e, g=g, j=j, w=w: e.matmul(pf[g][0:nr, :], aT[:, j, r0:r0 + nr], w[:, g * 512:(g + 1) * 512], start=(j == 0), stop=(j == 43)),
                             reads=[Bw, B_aT], writes=[B_pf[g]], inc=(g == 3))
                for g in range(4):
                    S.op("act", lambda e, g=g: e.activation(tmp[0:nr, g * 512:(g + 1) * 512], pf[g][0:nr, :], AF.Square, accum_out=sc[0:nr, g:g + 1]), reads=[B_pf[g]], writes=[B_tmp, B_sc])
                S.op("dve", lambda e: e.reduce_sum(sc[0:nr, 4:5], sc[0:nr, 0:4], AX.X), reads=[B_sc], writes=[B_sc])
                S.op("act", lambda e: e.activation(sc[0:nr, 5:6], sc[0:nr, 4:5], AF.Sqrt, bias=K.epsc[0:nr, 0:1], scale=1.0 / D), reads=[B_sc, K.B_const], writes=[B_sc])
                S.op("dve", lambda e: e.reciprocal(sc[0:nr, 5:6], sc[0:nr, 5:6]), reads=[B_sc], writes=[B_sc])
                for g in range(4):
                    gs = slice(g * 512, (g + 1) * 512)
                    S.op("dve", lambda e, g=g, gs=gs: e.scalar_tensor_tensor(tmp[0:nr, gs], pf[g][0:nr, :], sc[0:nr, 5:6], rows[0:nr, ri, 0, gs], ALU.mult, ALU.mult), reads=[B_pf[g], B_sc, B_rows], writes=[B_tmp])
                S.op("pool", lambda e: e.tensor_tensor(yo[0:nr, :], x1[0:nr, :], tmp[0:nr, :], ALU.add), reads=[B_x1, B_tmp], writes=[B_yo])
                S.dma("pool", K.y_o[r0 - 128:r0 - 128 + nr, :], yo[0:nr, :], reads=[B_yo], writes=[B_y], semkey="st_y")
            S.barrier()


_CACHE = {}


def _consts():
    cm = np.zeros((128, 5, 128), np.float32)
    r = np.arange(128)[:, None]; c = np.arange(128)[None, :]
    cm[:, 0, :] = (r == c)
    cm[:, 1, :] = (r % 64) < (c % 64)
    cm[:, 2, :] = (r % 64) <= (c % 64)
    cm[:, 3, :] = (r % 64) > (c % 64)
    cm[:, 4, :] = ((r // 64) == (c // 64)) / 64.0
    return cm


def kernel(x_prompt, x_sample, c_prompt, c_sample, cache_k, cache_v, state_wkv, state_shift,
           state_ffn_conv, w_ada, b_ada, g_pre_mix, g_post_mix, g_pre_ffn, g_post_ffn, w_in,
           lam_q1, lam_k1, lam_q2, lam_k2, g_subln, mu_shift, w0, w_w2, a0, w_a2, w_g2, k_k, k_a,
           r_k, ln_x_w, ln_x_b, w_out, w_up, w_conv_ffn, w_down, _dbg=-1, _cores=None):
    f = lambda a: np.ascontiguousarray(np.asarray(a, dtype=np.float32))
    x_prompt = f(x_prompt); x_sample = f(x_sample); cache_k = f(cache_k); cache_v = f(cache_v)
    shared = {
        "w_ada": f(w_ada)[0], "b_ada": f(b_ada)[0], "g_pre_mix": f(g_pre_mix)[0], "g_post_mix": f(g_post_mix)[0],
        "g_pre_ffn": f(g_pre_ffn)[0], "g_post_ffn": f(g_post_ffn)[0], "w_in": f(w_in)[0],
        "lamv": np.stack([f(lam_q1)[0], f(lam_k1)[0], f(lam_q2)[0], f(lam_k2)[0]]), "g_subln": f(g_subln)[0],
        "mu_shift": f(mu_shift)[0], "w0": f(w0)[0], "w_w2": f(w_w2)[0], "a0": f(a0)[0], "w_a2": f(w_a2)[0],
        "w_g2": f(w_g2)[0], "k_k": f(k_k)[0], "k_a": f(k_a)[0], "r_k": f(r_k)[0].reshape(1024),
        "ln_x_w": f(ln_x_w)[0], "ln_x_b": f(ln_x_b)[0], "w_out": f(w_out)[0], "w_up": f(w_up)[0],
        "w_conv": f(w_conv_ffn)[0], "w_down": f(w_down)[0], "cmask": _consts(),
    }
    in_maps = []
    cores = list(range(8)) if _cores is None else list(_cores)
    for c in cores:
        b, j = c // 4, c % 4
        hi = (j + 1) * 1024; lo = hi - NSEQ
        npad = max(0, -lo)
        xs = np.zeros((NTOK, D), np.float32)
        xs[npad:NSEQ] = x_prompt[b, max(lo, 0):hi]
        xs[NSEQ:] = x_sample[4 * c:4 * c + 4].reshape(64, D)
        vmask = np.ones((NTOK,), np.float32); vmask[:npad] = 0.0
        kb = np.zeros((24,), np.float32)
        for t in range(24):
            if t * 128 < npad: kb[t] = NEG
        cT = np.concatenate([np.asarray(c_prompt, np.float32)[b:b + 1], np.asarray(c_sample, np.float32)[4 * c:4 * c + 4]], 0).T
        m = dict(shared)
        m.update({
            "xT": np.ascontiguousarray(xs.T), "valid": np.ascontiguousarray(np.broadcast_to(vmask[None, :], (128, NTOK))),
            "kbias": np.ascontiguousarray(np.broadcast_to(kb[None, :], (128, 24))),
            "x_tm": np.ascontiguousarray(xs[HALO0:]), "cT": np.ascontiguousarray(cT),
            "cache_kT": np.ascontiguousarray(cache_k[0, 4 * c:4 * c + 4].transpose(0, 2, 3, 1)),
            "cache_v": np.ascontiguousarray(cache_v[0, 4 * c:4 * c + 4]),
            "st_wkv": np.ascontiguousarray(f(state_wkv)[0, 4 * c:4 * c + 4].transpose(0, 1, 3, 2)),
            "st_shift": np.ascontiguousarray(f(state_shift)[0, 4 * c:4 * c + 4].reshape(4, 26, 128).transpose(2, 1, 0)),
            "st_conv": np.ascontiguousarray(f(state_ffn_conv)[0, 4 * c:4 * c + 4].reshape(4, 2, 88, 128).transpose(3, 2, 0, 1)),
        })
        in_maps.append(m)
    if _dbg not in _CACHE:
        _CACHE[_dbg] = build(_dbg)[0]
    nc = _CACHE[_dbg]
    res = run_bass_kernel_spmd(nc, in_maps, core_ids=list(range(len(cores)))).results
    yp = np.zeros((2, 4096, D), np.float32); ys = np.zeros((32, 16, D), np.float32)
    kp = np.zeros((1, 2, 4096, 8, 128), np.float32); vp = np.zeros_like(kp)
    ks = np.zeros((1, 32, 16, 8, 128), np.float32); vs = np.zeros_like(ks)
    wp = np.zeros((1, 2, 16, 64, 64), np.float32); wsm = np.zeros((1, 32, 16, 64, 64), np.float32)
    shp = np.zeros((1, 2, RWC), np.float32); shs = np.zeros((1, 32, RWC), np.float32)
    cp = np.zeros((1, 2, 2, 2 * DFF), np.float32); cs = np.zeros((1, 32, 2, 2 * DFF), np.float32)
    for ci, c in enumerate(cores):
        b, j = c // 4, c % 4; r = res[ci]
        yp[b, j * 1024:(j + 1) * 1024] = r["y_o"][:1024]; ys[4 * c:4 * c + 4] = r["y_o"][1024:].reshape(4, 16, D)
        kp[0, b, j * 1024:(j + 1) * 1024] = r["k_o"][:1024].reshape(1024, 8, 128); ks[0, 4 * c:4 * c + 4] = r["k_o"][1024:].reshape(4, 16, 8, 128)
        vp[0, b, j * 1024:(j + 1) * 1024] = r["v_o"][:1024].reshape(1024, 8, 128); vs[0, 4 * c:4 * c + 4] = r["v_o"][1024:].reshape(4, 16, 8, 128)
        wsm[0, 4 * c:4 * c + 4] = r["wkv_s"].transpose(0, 1, 3, 2)
        shs[0, 4 * c:4 * c + 4] = r["shift_s"].transpose(2, 1, 0).reshape(4, RWC)
        cs[0, 4 * c:4 * c + 4] = r["conv_s"].transpose(2, 3, 1, 0).reshape(4, 2, 2 * DFF)
        if j == 3:
            wp[0, b] = r["wkv_p"].transpose(0, 2, 1)
            shp[0, b] = r["shift_p"].T.reshape(RWC)
            cp[0, b] = r["conv_p"].transpose(2, 1, 0).reshape(2, 2 * DFF)
    return (yp, ys, kp, vp, wp, shp, cp, ks, vs, wsm, shs, cs)
```

```python
import numpy as np
import concourse.bass as bass
import concourse.mybir as mybir
from concourse.bass_utils import run_bass_kernel_spmd
from contextlib import ExitStack
import math

F32 = mybir.dt.float32; BF16 = mybir.dt.bfloat16
AF = mybir.ActivationFunctionType; ALU = mybir.AluOpType; AX = mybir.AxisListType

D = 2048; NSEQ = 4096; PRE = 2944; HALO0 = 2944; OWN0 = 3072; NS = 64; NTOK = 4160
NREG = NTOK - HALO0
INC = 6400; DFF = 5632; RWC = 3328
EPS = 1e-6; GN_EPS = 64e-5
LAM_INIT = 0.8 - 0.6 * math.exp(0.0)
NEG = -30000.0


class Buf:
    __slots__ = ("name", "w", "r")
    def __init__(self, name):
        self.name = name; self.w = None; self.r = {}


class Sched:
    def __init__(self, nc):
        self.nc = nc
        self.eng = {"pe": nc.tensor, "act": nc.scalar, "dve": nc.vector, "pool": nc.gpsimd, "sp": nc.sync}
        self.sems = {}; self.cnt = {}
        self.seen = {e: {} for e in self.eng}
        self._ctx = []
        self.n_inst = 0
    def new_sem(self, key):
        cm = self.nc.semaphore("s%d" % len(self.sems))
        h = cm.__enter__(); self._ctx.append(cm)
        self.sems[key] = h; self.cnt[key] = 0
    def _deps(self, reads, writes):
        deps = []
        for b in reads:
            if b.w is not None: deps.append(b.w)
        for b in writes:
            if b.w is not None: deps.append(b.w)
            deps.extend(b.r.items())
        return deps
    def _wait(self, e, deps):
        need = {}
        for k, t in deps:
            if e == "pe" and k == "e:pe": continue
            if t > need.get(k, 0): need[k] = t
        for k, t in need.items():
            if self.seen[e].get(k, 0) >= t: continue
            self.eng[e].wait_ge(self.sems[k], t)
            self.seen[e][k] = t
    def _mark(self, k, tick, reads, writes):
        for b in reads:
            if b.r.get(k, 0) < tick: b.r[k] = tick
        for b in writes:
            b.w = (k, tick); b.r = {}
    def op(self, e, fn, reads=(), writes=(), inc=True):
        self._wait(e, self._deps(reads, writes))
        ins = fn(self.eng[e])
        self.n_inst += 1
        k = "e:" + e
        if k not in self.sems: self.new_sem(k)
        if inc:
            self.cnt[k] += 1
            ins.then_inc(self.sems[k], 1)
            tick = self.cnt[k]
        else:
            tick = self.cnt[k] + 1
        self._mark(k, tick, reads, writes)
        return ins
    def dma(self, q, out, in_, reads=(), writes=(), semkey=None, **kw):
        self._wait(q, self._deps(reads, writes))
        ins = self.eng[q].dma_start(out=out, in_=in_, **kw)
        self.n_inst += 1
        if reads:
            semkey = "st_" + reads[0].name
        k = "d:" + semkey
        if k not in self.sems: self.new_sem(k)
        self.cnt[k] += 16
        ins.then_inc(self.sems[k], 16)
        self._mark(k, self.cnt[k], reads, writes)
        return ins
    def barrier(self):
        for e in self.eng:
            for k, c in self.cnt.items():
                if c > 0 and self.seen[e].get(k, 0) < c:
                    self.eng[e].wait_ge(self.sems[k], c)
                    self.seen[e][k] = c
    def close(self):
        for cm in reversed(self._ctx): cm.__exit__(None, None, None)


class Ctx:
    pass


def build(dbg=0):
    nc = bass.Bass("TRN2", target_bir_lowering=False)
    S = Sched(nc)
    K = Ctx(); K.nc = nc; K.S = S
    ins = {}; outs = {}
    def din(name, shape):
        ins[name] = nc.dram_tensor(name, list(shape), F32, kind="ExternalInput").ap(); return ins[name]
    def dout(name, shape):
        outs[name] = nc.dram_tensor(name, list(shape), F32, kind="ExternalOutput").ap(); return outs[name]
    def dscr(name, shape, dt):
        return nc.dram_tensor(name, list(shape), dt, kind="Internal").ap()
    xT = din("xT", [D, NTOK]); valid = din("valid", [128, NTOK]); kbias = din("kbias", [128, 24])
    x_tm = din("x_tm", [NREG, D]); cT = din("cT", [D, 5])
    cache_kT = din("cache_kT", [4, 8, 128, 4096]); cache_v = din("cache_v", [4, 4096, 8, 128])
    st_wkv = din("st_wkv", [4, 16, 64, 64]); st_shift = din("st_shift", [128, 26, 4]); st_conv = din("st_conv", [128, 88, 4, 2])
    w_ada = din("w_ada", [D, 6 * D]); b_ada = din("b_ada", [6 * D])
    g_pre_mix = din("g_pre_mix", [D]); g_post_mix = din("g_post_mix", [D]); g_pre_ffn = din("g_pre_ffn", [D]); g_post_ffn = din("g_post_ffn", [D])
    w_in = din("w_in", [D, INC]); lamv = din("lamv", [4, 64]); g_subln = din("g_subln", [128])
    mu_shift = din("mu_shift", [RWC]); w0 = din("w0", [1024]); w_w2 = din("w_w2", [64, 1024]); a0 = din("a0", [1024])
    w_a2 = din("w_a2", [64, 1024]); w_g2 = din("w_g2", [128, 1024]); k_k = din("k_k", [1024]); k_a = din("k_a", [1024])
    r_k = din("r_k", [1024]); ln_x_w = din("ln_x_w", [1024]); ln_x_b = din("ln_x_b", [1024])
    w_out = din("w_out", [D, D]); w_up = din("w_up", [D, 2 * DFF]); w_conv = din("w_conv", [3, 2 * DFF]); w_down = din("w_down", [DFF, D])
    cmask = din("cmask", [128, 5, 128])
    y_o = dout("y_o", [NREG - 128, D])
    k_o = dout("k_o", [NREG - 128, 1024]); v_o = dout("v_o", [NREG - 128, 1024])
    wkv_p = dout("wkv_p", [16, 64, 64]); wkv_s = dout("wkv_s", [4, 16, 64, 64])
    shift_p = dout("shift_p", [128, 26]); shift_s = dout("shift_s", [128, 26, 4])
    conv_p = dout("conv_p", [128, 88, 2]); conv_s = dout("conv_s", [128, 88, 4, 2])
    w_in_b = dscr("w_in_b", [D, INC], BF16); w_out_b = dscr("w_out_b", [D, D], BF16)
    w_up_b = dscr("w_up_b", [D, 2 * DFF], BF16); w_down_b = dscr("w_down_b", [DFF, D], BF16)
    rows_d = dscr("rows_d", [4, 5, D], F32)
    qT_d = dscr("qT_d", [1024, NREG], BF16); kT_d = dscr("kT_d", [1024, NTOK], BF16)
    v_d = dscr("v_d", [NTOK, 1024], F32); pT_d = dscr("pT_d", [RWC, NTOK], F32)
    oT_d = dscr("oT_d", [D, NREG], BF16)
    x1_d = dscr("x1_d", [NREG, D], F32); h2T_d = dscr("h2T_d", [D, NREG], BF16)
    K.__dict__.update(locals())

    es_all = ExitStack()
    uid = [0]
    def sbt(es, name, shape, dt):
        uid[0] += 1
        return es.enter_context(nc.sbuf_tensor("%s_%d" % (name, uid[0]), list(shape), dt))
    def pst(es, name, shape, dt):
        uid[0] += 1
        return es.enter_context(nc.psum_tensor("%s_%d" % (name, uid[0]), list(shape), dt))
    K.sbt = sbt; K.pst = pst
    nc_allow = nc.allow_non_contiguous_dma(reason="small param layouts")
    nc_allow.__enter__()

    cm_f = sbt(es_all, "cm_f", [128, 5, 128], F32); cm_b = sbt(es_all, "cm_b", [128, 5, 128], BF16)
    B_cm = Buf("cm")
    S.dma("sp", cm_f[:, :, :], cmask[:, :, :], writes=[B_cm], semkey="cm_f")
    S.dma("pool", cm_b[:, :, :], cmask[:, :, :], writes=[B_cm], semkey="cm_b")
    ones_b = sbt(es_all, "ones_b", [128, 128], BF16); epsc = sbt(es_all, "epsc", [128, 4], F32)
    B_const = Buf("const")
    S.op("dve", lambda e: e.memset(ones_b[:, :], 1.0), writes=[B_const])
    S.op("dve", lambda e: e.memset(epsc[:, 0:1], EPS), writes=[B_const])
    S.op("dve", lambda e: e.memset(epsc[:, 1:2], GN_EPS), writes=[B_const])
    S.op("dve", lambda e: e.memset(epsc[:, 2:3], 1e-24), writes=[B_const])
    S.op("dve", lambda e: e.memset(epsc[:, 3:4], 0.0), writes=[B_const])
    Gm = sbt(es_all, "Gm", [128, 16, 5], F32); Shm = sbt(es_all, "Shm", [128, 16, 5], F32)
    B_G = Buf("G")
    K.__dict__.update(locals())

    K.B_wcast = Buf("wcast")
    phase_cast(K, "in")
    phase0(K)
    phase1(K)
    S.barrier()
    K.cast_thunks = phase_cast(K, "rest", defer=True)
    if dbg in (0, -1):
        phase3(K)
        S.barrier()
    if dbg == -1:
        phase4(K)
        S.barrier()
    S.barrier()
    es_all.close()
    nc_allow.__exit__(None, None, None)
    S.close()
    return nc, S


def phase_cast(K, which, defer=False):
    S = K.S
    lst = {"in": ((K.w_in_b, K.w_in, D),), "rest": ((K.w_out_b, K.w_out, D), (K.w_up_b, K.w_up, D), (K.w_down_b, K.w_down, DFF))}[which]
    th = []
    for (dst, src, rows) in lst:
        for r0 in range(0, rows, 128):
            th.append(lambda dst=dst, src=src, r0=r0: S.dma("pool", dst[r0:r0 + 128, :], src[r0:r0 + 128, :], writes=[K.B_wcast], semkey="wcast_" + which))
    if defer:
        return th
    for t in th:
        t()


def phase0(K):
    S = K.S; nc = K.nc
    with ExitStack() as es:
        sbt = lambda n, s, d: K.sbt(es, n, s, d); pst = lambda n, s, d: K.pst(es, n, s, d)
        csb = sbt("csb", [128, 16, 5], F32); sT = sbt("sT", [128, 16, 5], F32)
        bT = sbt("bT", [128, 96], F32); modT = sbt("modT", [128, 96, 5], F32)
        wa = [sbt("wa%d" % i, [128, 16, 512], F32) for i in range(2)]
        psA = [pst("psA%d" % i, [128, 8], F32) for i in range(2)]
        psB = [pst("psB%d" % i, [8, 512], F32) for i in range(2)]
        gv = sbt("gv", [128, 4, 16], F32); drow = sbt("drow", [5, 4, D], F32)
        bg = [sbt("bg%d" % i, [5, 512], F32) for i in range(2)]; gg = [sbt("gg%d" % i, [5, 512], F32) for i in range(2)]
        tmpr = sbt("tmpr", [5, 512], F32)
        B_c = Buf("c"); B_s = Buf("sT"); B_b = Buf("bT"); B_mod = Buf("modT")
        B_wa = [Buf("wa0"), Buf("wa1")]; B_pA = [Buf("pA0"), Buf("pA1")]; B_pB = [Buf("pB0"), Buf("pB1")]
        B_gv = Buf("gv"); B_dr = Buf("drow"); B_rows = Buf("rows_d"); B_bg = [Buf("bg0"), Buf("bg1")]; B_gg = [Buf("gg0"), Buf("gg1")]; B_tmp = Buf("tmpr")
        S.dma("sp", csb[:, :, :], K.cT.rearrange("(c p) r -> p c r", p=128), writes=[B_c], semkey="csb")
        S.dma("sp", bT[:, :], K.b_ada.rearrange("(c p) -> p c", p=128), writes=[B_b], semkey="bT")
        gl = (K.g_pre_mix, K.g_post_mix, K.g_pre_ffn, K.g_post_ffn)
        for i, g in enumerate(gl):
            S.dma("sp", gv[:, i, :], g.rearrange("(c p) -> p c", p=128), writes=[B_gv], semkey="gv")
        S.op("act", lambda e: e.activation(sT[:, :, :], csb[:, :, :], AF.Silu), reads=[B_c], writes=[B_s])
        for g in range(24):
            w = wa[g % 2]; Bw = B_wa[g % 2]
            S.dma("sp", w[:, :, :], K.w_ada[:, g * 512:(g + 1) * 512].rearrange("(c p) n -> p c n", p=128), writes=[Bw], semkey="wa%d" % (g % 2))
            for j in range(4 if g < 8 else 0):
                ch = g * 4 + j; pa = psA[ch % 2]; Bp = B_pA[ch % 2]
                for k in range(16):
                    S.op("pe", lambda e, k=k, j=j, pa=pa: e.matmul(pa[:, 0:5], w[:, k, j * 128:(j + 1) * 128], sT[:, k, :], start=(k == 0), stop=(k == 15)),
                         reads=[Bw, B_s], writes=[Bp], inc=(k == 15))
                S.op("dve", lambda e, pa=pa, ch=ch: e.tensor_scalar(modT[:, ch, :], pa[:, 0:5], bT[:, ch:ch + 1], None, ALU.add), reads=[Bp, B_b], writes=[B_mod])
            sec = g // 4; off = (g % 4) * 512
            if sec < 2: continue
            pb = psB[g % 2]; Bp = B_pB[g % 2]
            S.dma("sp", bg[g % 2][:, :], K.b_ada[g * 512:(g + 1) * 512].partition_broadcast(5), writes=[B_bg[g % 2]], semkey="bg%d" % (g % 2))
            gsel = {2: 1, 4: 2, 5: 3}.get(sec)
            if gsel is not None:
                S.dma("sp", gg[g % 2][:, :], gl[gsel][off:off + 512].partition_broadcast(5), writes=[B_gg[g % 2]], semkey="gg%d" % (g % 2))
            for k in range(16):
                S.op("pe", lambda e, k=k, pb=pb: e.matmul(pb[0:5, :], sT[:, k, :], w[:, k, :], start=(k == 0), stop=(k == 15)),
                     reads=[Bw, B_s], writes=[Bp], inc=(k == 15))
            S.op("dve", lambda e, pb=pb, g=g: e.tensor_tensor(tmpr[:, :], pb[0:5, :], bg[g % 2][:, :], ALU.add), reads=[Bp, B_bg[g % 2]], writes=[B_tmp])
            if sec == 2:
                S.op("dve", lambda e, g=g, off=off: e.tensor_tensor(drow[:, 0, off:off + 512], tmpr[:, :], gg[g % 2][:, :], ALU.mult), reads=[B_tmp, B_gg[g % 2]], writes=[B_dr])
            elif sec == 3:
                S.op("dve", lambda e, g=g, off=off: e.tensor_copy(drow[:, 2, off:off + 512], tmpr[:, :]), reads=[B_tmp], writes=[B_dr])
            elif sec == 4:
                S.op("dve", lambda e, g=g, off=off: e.scalar_tensor_tensor(drow[:, 1, off:off + 512], tmpr[:, :], 1.0, gg[g % 2][:, :], ALU.add, ALU.mult), reads=[B_tmp, B_gg[g % 2]], writes=[B_dr])
            else:
                S.op("dve", lambda e, g=g, off=off: e.tensor_tensor(drow[:, 3, off:off + 512], tmpr[:, :], gg[g % 2][:, :], ALU.mult), reads=[B_tmp, B_gg[g % 2]], writes=[B_dr])
        S.op("dve", lambda e: e.scalar_tensor_tensor(K.Gm[:, :, :], modT[:, 16:32, :], 1.0, gv[:, 0, :].unsqueeze(2).to_broadcast([128, 16, 5]), ALU.add, ALU.mult),
             reads=[B_mod, B_gv], writes=[K.B_G])
        S.op("dve", lambda e: e.tensor_copy(K.Shm[:, :, :], modT[:, 0:16, :]), reads=[B_mod], writes=[K.B_G])
        S.dma("sp", K.rows_d.rearrange("k r d -> r k d"), drow[:, :, :], reads=[B_dr], writes=[B_rows], semkey="rows_d")
        S.barrier()


def phase1(K):
    S = K.S; nc = K.nc
    with ExitStack() as es:
        sbt = lambda n, s, d: K.sbt(es, n, s, d); pst = lambda n, s, d: K.pst(es, n, s, d)
        hT = sbt("hT", [128, 16, 2112], BF16); B_h = Buf("hT")
        xt_ = [sbt("xt%d" % i, [128, 16, 256], F32) for i in range(2)]; B_x_ = [Buf("xt0"), Buf("xt1")]
        sq_ = [sbt("sq%d" % i, [128, 16, 256], BF16) for i in range(2)]; B_sq_ = [Buf("sq0"), Buf("sq1")]
        vl = sbt("vl", [128, 512], F32); B_vl = Buf("vl")
        rs = sbt("rs", [128, 512], F32); B_rs = Buf("rs")
        t1 = sbt("t1", [128, 512], F32); B_t1 = Buf("t1")
        ta_ = [sbt("ta%d" % i, [128, 512], F32) for i in range(2)]; B_ta = [Buf("ta0"), Buf("ta1")]
        wf = [sbt("wf%d" % i, [128, 16, 128], BF16) for i in range(2)]; B_wf = [Buf("wf0"), Buf("wf1")]
        wt = [sbt("wt%d" % i, [128, 16, 512], BF16) for i in range(2)]; B_wt = [Buf("wt0"), Buf("wt1")]
        stg_b = [sbt("stgb%d" % i, [128, 2112], BF16) for i in range(2)]; B_sb = [Buf("stgb0"), Buf("stgb1")]
        stg_f = [sbt("stgf%d" % i, [128, 2112], F32) for i in range(2)]; B_sf = [Buf("stgf0"), Buf("stgf1")]
        stg_t = [sbt("stgt%d" % i, [128, 512], F32) for i in range(2)]; B_st = [Buf("stgt0"), Buf("stgt1")]
        ps_ss = pst("ps_ss", [128, 512], F32); B_pss = Buf("ps_ss")
        ps = [pst("psm%d" % i, [128, 512], F32) for i in range(4)]; B_ps = [Buf("psm%d" % i) for i in range(4)]
        B_scr = {n: Buf(n) for n in ("qT", "kT", "pT", "v", "out")}
        psi = [0]; evi = [0]
        w_in_v = K.w_in_b.rearrange("(c p) n -> p c n", p=128)
        for sb_i, (tb0, tb1) in enumerate(((0, 2048), (2048, NTOK))):
            nt = tb1 - tb0
            for ti_, t0 in enumerate(range(tb0, tb1, 256)):
                n = min(256, tb1 - t0)
                xt = xt_[ti_ % 2]; B_x = B_x_[ti_ % 2]; sq = sq_[ti_ % 2]; B_sq = B_sq_[ti_ % 2]
                S.dma("sp", xt[:, :, 0:n], K.xT[:, t0:t0 + n].rearrange("(c p) t -> p c t", p=128), writes=[B_x], semkey="xt%d" % (ti_ % 2))
                S.dma("sp", vl[:, 0:n], K.valid[:, t0:t0 + n], writes=[B_vl], semkey="vl")
                S.op("act", lambda e: e.activation(sq[:, :, 0:n], xt[:, :, 0:n], AF.Square), reads=[B_x], writes=[B_sq])
                for k in range(16):
                    S.op("pe", lambda e, k=k: e.matmul(ps_ss[:, 0:n], K.ones_b[:, :], sq[:, k, 0:n], start=(k == 0), stop=(k == 15)),
                         reads=[B_sq, K.B_const], writes=[B_pss], inc=(k == 15))
                S.op("act", lambda e: e.activation(rs[:, 0:n], ps_ss[:, 0:n], AF.Sqrt, bias=K.epsc[:, 0:1], scale=1.0 / D), reads=[B_pss, K.B_const], writes=[B_rs])
                S.op("dve", lambda e: e.reciprocal(rs[:, 0:n], rs[:, 0:n]), reads=[B_rs], writes=[B_rs])
                S.op("dve", lambda e: e.tensor_tensor(rs[:, 0:n], rs[:, 0:n], vl[:, 0:n], ALU.mult), reads=[B_rs, B_vl], writes=[B_rs])
                if t0 + n <= NSEQ:
                    groups = [(0, n, 0)]
                else:
                    groups = [(s * 16, 16, 1 + s) for s in range(4)]
                for k in range(16):
                    for (c0, cn, r) in groups:
                        hs_ = slice(t0 - tb0 + c0, t0 - tb0 + c0 + cn); cs_ = slice(c0, c0 + cn)
                        if k % 2 == 0:
                            S.op("dve", lambda e, k=k, cs_=cs_, r=r: e.scalar_tensor_tensor(t1[:, cs_], xt[:, k, cs_], K.Gm[:, k, r:r + 1], rs[:, cs_], ALU.mult, ALU.mult),
                                 reads=[B_x, K.B_G, B_rs], writes=[B_t1])
                            S.op("dve", lambda e, k=k, cs_=cs_, hs_=hs_, r=r: e.scalar_tensor_tensor(hT[:, k, hs_], vl[:, cs_], K.Shm[:, k, r:r + 1], t1[:, cs_], ALU.mult, ALU.add),
                                 reads=[B_vl, K.B_G, B_t1], writes=[B_h])
                        else:
                            ta = ta_[(k // 2) % 2]; Bta = B_ta[(k // 2) % 2]
                            S.op("act", lambda e, k=k, cs_=cs_, r=r, ta=ta: e.activation(ta[:, cs_], xt[:, k, cs_], AF.Identity, scale=K.Gm[:, k, r:r + 1]), reads=[B_x, K.B_G], writes=[Bta])
                            S.op("pool", lambda e, cs_=cs_, ta=ta: e.tensor_tensor(ta[:, cs_], ta[:, cs_], rs[:, cs_], ALU.mult), reads=[Bta, B_rs], writes=[Bta])
                            S.op("act", lambda e, k=k, cs_=cs_, r=r, ta=ta: e.activation(ta[:, cs_], ta[:, cs_], AF.Identity, bias=K.Shm[:, k, r:r + 1], scale=1.0), reads=[Bta, K.B_G], writes=[Bta])
                            S.op("pool", lambda e, k=k, cs_=cs_, hs_=hs_, ta=ta: e.tensor_tensor(hT[:, k, hs_], ta[:, cs_], vl[:, cs_], ALU.mult), reads=[Bta, B_vl], writes=[B_h])
            chunks = [("kT", 1024 + 128 * i, i, tb0) for i in range(8)] + [("pT", 3072 + 128 * i, i, tb0) for i in range(26)]
            if sb_i == 1:
                chunks += [("qT", 128 * i, i, HALO0) for i in range(8)]
            for ci, (kind, col0, idx, tlo) in enumerate(chunks):
                w = wf[ci % 2]; Bw = B_wf[ci % 2]
                S.dma("sp", w[:, :, :], w_in_v[:, :, col0:col0 + 128], writes=[Bw], semkey="wf%d" % (ci % 2))
                isf = (kind == "pT")
                stg = (stg_f if isf else stg_b)[ci % 2]; Bs = (B_sf if isf else B_sb)[ci % 2]
                for t0 in range(tlo, tb1, 512):
                    n = min(512, tb1 - t0); o = t0 - tb0
                    p = ps[psi[0] % 4]; Bp = B_ps[psi[0] % 4]; psi[0] += 1
                    for k in range(16):
                        S.op("pe", lambda e, k=k, p=p, o=o, n=n: e.matmul(p[:, 0:n], w[:, k, :], hT[:, k, o:o + n], start=(k == 0), stop=(k == 15)),
                             reads=[Bw, B_h], writes=[Bp], inc=(k == 15))
                    ev = "act" if evi[0] % 2 == 0 else "dve"; evi[0] += 1
                    if ev == "act":
                        S.op("act", lambda e, p=p, o=o, n=n: e.copy(stg[:, o:o + n], p[:, 0:n]), reads=[Bp], writes=[Bs])
                    else:
                        S.op("dve", lambda e, p=p, o=o, n=n: e.tensor_copy(stg[:, o:o + n], p[:, 0:n]), reads=[Bp], writes=[Bs])
                lo = tlo - tb0
                if kind == "kT":
                    S.dma("pool", K.kT_d[idx * 128:(idx + 1) * 128, tlo:tb1], stg[:, lo:nt], reads=[Bs], writes=[B_scr["kT"]], semkey="st_kT%d" % (ci % 2))
                elif kind == "qT":
                    S.dma("pool", K.qT_d[idx * 128:(idx + 1) * 128, :], stg[:, lo:nt], reads=[Bs], writes=[B_scr["qT"]], semkey="st_qT%d" % (ci % 2))
                else:
                    S.dma("pool", K.pT_d[idx * 128:(idx + 1) * 128, tlo:tb1], stg[:, lo:nt], reads=[Bs], writes=[B_scr["pT"]], semkey="st_pT%d" % (ci % 2))
                    if sb_i == 1:
                        c = NSEQ - 1 - tb0
                        S.dma("pool", K.shift_p[:, idx:idx + 1], stg[:, c:c + 1], reads=[Bs], writes=[B_scr["out"]], semkey="st_out")
                        c = NSEQ - tb0
                        S.dma("pool", K.shift_s[:, idx, :], stg[:, c:c + 64].rearrange("p (s t) -> p s t", t=16)[:, :, 15], reads=[Bs], writes=[B_scr["out"]], semkey="st_out")
            tms = [("v", 2048 + 512 * g, g, tb0) for g in range(2)]
            if sb_i == 1:
                tms += [("k", 1024 + 512 * g, g, OWN0) for g in range(2)]
            for ci, (kind, col0, g, tlo) in enumerate(tms):
                w = wt[ci % 2]; Bw = B_wt[ci % 2]
                S.dma("sp", w[:, :, :], w_in_v[:, :, col0:col0 + 512], writes=[Bw], semkey="wt%d" % (ci % 2))
                for t0 in range(tlo, tb1, 128):
                    n = min(128, tb1 - t0); o = t0 - tb0
                    p = ps[psi[0] % 4]; Bp = B_ps[psi[0] % 4]; psi[0] += 1
                    for k in range(16):
                        S.op("pe", lambda e, k=k, p=p, o=o, n=n: e.matmul(p[0:n, :], hT[:, k, o:o + n], w[:, k, :], start=(k == 0), stop=(k == 15)),
                             reads=[Bw, B_h], writes=[Bp], inc=(k == 15))
                    st = stg_t[evi[0] % 2]; Bst = B_st[evi[0] % 2]
                    ev = "act" if evi[0] % 2 == 0 else "dve"; evi[0] += 1
                    if ev == "act":
                        S.op("act", lambda e, p=p, n=n, st=st: e.copy(st[0:n, :], p[0:n, :]), reads=[Bp], writes=[Bst])
                    else:
                        S.op("dve", lambda e, p=p, n=n, st=st: e.tensor_copy(st[0:n, :], p[0:n, :]), reads=[Bp], writes=[Bst])
                    if kind == "v":
                        S.dma("pool", K.v_d[t0:t0 + n, g * 512:(g + 1) * 512], st[0:n, :], reads=[Bst], writes=[B_scr["v"]], semkey="st_v")
                    if t0 >= OWN0:
                        dst = K.v_o if kind == "v" else K.k_o
                        S.dma("pool", dst[t0 - OWN0:t0 - OWN0 + n, g * 512:(g + 1) * 512], st[0:n, :], reads=[Bst], writes=[B_scr["out"]], semkey="st_out")
        S.barrier()


def phase3(K):
    S = K.S; nc = K.nc
    NMAX = 256; NCM = 4
    S_ = K.S
    class Emit:
        def __init__(self): self.prog = None
        def op(self, *a, **k):
            return getattr(S_, "op")(*a, **k)
        def dma(self, *a, **k):
            return getattr(S_, "dma")(*a, **k)
    EM = Emit()
    with ExitStack() as es:
        sbt = lambda n, s, d: K.sbt(es, n, s, d); pst = lambda n, s, d: K.pst(es, n, s, d)
        TS = [dict(), dict()]; BS = [dict(), dict()]
        def mk(name, shape, dt, shared=False):
            if shared:
                t = sbt(name, shape, dt); bb = Buf(name)
                for li in range(2):
                    TS[li][name] = t; BS[li][name] = bb
            else:
                for li in range(2):
                    TS[li][name] = sbt(name + "_l%d" % li, shape, dt); BS[li][name] = Buf(name + "_l%d" % li)
        T = TS[0]; B = BS[0]
        pv = sbt("pv", [128, 7, 8], F32); B_pv = Buf("pv")
        mu = sbt("mu", [128, 26], F32)
        ww2 = sbt("ww2", [128, 1024], BF16); wa2 = sbt("wa2", [128, 1024], BF16); wg2 = sbt("wg2", [128, 1024], BF16)
        for i, v in enumerate((K.w0, K.a0, K.k_k, K.k_a, K.r_k, K.ln_x_w, K.ln_x_b)):
            EM.dma("sp", pv[:, i, :], v.rearrange("(c p) -> p c", p=128), writes=[B_pv], semkey="pv")
        EM.dma("sp", mu[:, :], K.mu_shift.rearrange("(c p) -> p c", p=128), writes=[B_pv], semkey="pv")
        EM.dma("pool", ww2[0:64, :], K.w_w2[:, :], writes=[B_pv], semkey="pvc")
        EM.dma("pool", wa2[64:128, :], K.w_a2[:, :], writes=[B_pv], semkey="pvc")
        EM.dma("pool", wg2[:, :], K.w_g2[:, :], writes=[B_pv], semkey="pvc")
        segm = sbt("segm", [128, NMAX], F32); vrw = sbt("vrw", [128, 64], F32)
        EM.op("dve", lambda e: e.memset(segm[:, :], 1.0), writes=[B_pv])
        EM.op("dve", lambda e: e.memset(segm[:, :].rearrange("p (c t) -> p c t", t=64)[:, :, 0:1], 0.0), writes=[B_pv])
        EM.op("dve", lambda e: e.memset(vrw[:, 0:48], 0.0), writes=[B_pv])
        EM.op("dve", lambda e: e.memset(vrw[:, 48:64], 1.0), writes=[B_pv])
        for nm in ("pr_r", "pr_k", "pr_v", "pr_wa", "pr_gd"):
            mk(nm, [128, NMAX + 1], F32)
        for nm in ("dtmp", "xr", "xk", "xv", "xwa", "xgd", "sigw", "av", "gvv", "ld", "L", "eL", "enL", "eLm", "kk", "kk2", "rn", "kp", "tb", "tk", "bv", "yT", "yc", "ysq"):
            mk(nm, [128, NMAX], F32)
        mk("twa", [128, NMAX], BF16); mk("sg", [128, NMAX], BF16); mk("ob", [128, NMAX], BF16)
        mk("PC", [128, NCM], F32)
        mk("bdAR", [128, NCM, 256], BF16); mk("bdB", [128, NCM, 128], BF16); mk("bdK", [128, NCM, 128], BF16)
        mk("bdBh", [128, NCM, 128], BF16); mk("bdKh", [128, NCM, 128], BF16); mk("bdV", [128, NCM, 128], BF16)
        for li in range(2):
            for nm in ("bdAR", "bdB", "bdK", "bdBh", "bdKh", "bdV"):
                EM.op("pool", lambda e, nm=nm, li=li: e.memset(TS[li][nm][:, :, :], 0.0), writes=[BS[li][nm]])
        mk("S1", [128, NCM, 256], BF16); mk("S2", [128, NCM, 256], BF16)
        for nm in ("N0", "N1", "M0", "M1", "T0", "T1", "Vb", "Kh", "Bh", "At", "AV", "Dm", "XT", "Gm", "Qc"):
            mk(nm, [128, NCM, 128], BF16)
        mk("E", [128, NCM, 128], F32); mk("Hall", [128, NCM + 1, 128], BF16)
        mk("tmpH0", [128, 128], F32); mk("tmpH1", [128, 128], F32)
        mk("Hf", [128, 128], F32)
        mk("mask2", [128, 2, 256], BF16, True); mk("identg", [128, 4, 128], BF16, True); mk("mSLg", [128, 4, 128], BF16, True)
        for i in range(2):
            EM.op("pool", lambda e, i=i: e.tensor_copy(T["mask2"][:, i, :], K.cm_b[:, 1:3, :].rearrange("p a b -> p (a b)")), reads=[K.B_cm], writes=[B["mask2"]])
        for i in range(4):
            EM.op("pool", lambda e, i=i: e.tensor_copy(T["identg"][:, i, :], K.cm_b[:, 0, :]), reads=[K.B_cm], writes=[B["identg"]])
            EM.op("pool", lambda e, i=i: e.tensor_copy(T["mSLg"][:, i, :], K.cm_b[:, 3, :]), reads=[K.B_cm], writes=[B["mSLg"]])
        banks = [pst("pb%d" % i, [128, 512], F32) for i in range(7)]
        bankt = pst("pbt", [128, 1024], BF16)
        bkB = [Buf("bank%d" % i) for i in range(8)]
        rots = [[0], [0]]
        cnts = [{"z": 0, "inv": 0, "ev": 0}, {"z": 0, "inv": 0, "ev": 0}]
        B_oT = Buf("oT"); B_out = Buf("out3")
        ident = K.cm_b[:, 0, :]; mSU = K.cm_b[:, 1, :]; mIU = K.cm_b[:, 2, :]; mSL = K.cm_b[:, 3, :]
        onesf = K.cm_f[:, 4, :]
        C1 = -math.exp(-0.5)

        def v3(ap, lo, hi, nch):
            return ap[lo:hi, 0:nch * 64].rearrange("p (c t) -> p c t", t=64)

        def evac(dst, dstB, src, srcB, extra_reads=()):
            e = "act" if cnt["ev"] % 2 == 0 else "dve"; cnt["ev"] += 1
            if e == "act":
                EM.op("act", lambda en: en.copy(dst, src), reads=[srcB] + list(extra_reads), writes=[dstB])
            else:
                EM.op("dve", lambda en: en.tensor_copy(dst, src), reads=[srcB] + list(extra_reads), writes=[dstB])

        def mm(pname, lhsT, rhs, rb, start=True, stop=True):
            EM.op("pe", lambda e: e.matmul(P[pname], lhsT, rhs, start=start, stop=stop), reads=rb, writes=[BP[pname]], inc=stop)

        def rw_block(li, hp, t0, nch, with_y, samp=None):
            T = TS[li]; B = BS[li]; cnt = cnts[li]; rot = rots[li]
            P = {"pz0": banks[2 * li], "pz1": banks[2 * li + 1]}; BP = {"pz0": bkB[2 * li], "pz1": bkB[2 * li + 1]}
            def nbank():
                i = 2 * li + (rot[0] % 2); rot[0] += 1
                return banks[i], bkB[i]
            def evac(dst, dstB, src, srcB, extra_reads=()):
                e = "act" if cnt["ev"] % 4 != 3 else "dve"; cnt["ev"] += 1
                if e == "act":
                    yield EM.op("act", lambda en: en.copy(dst, src), reads=[srcB] + list(extra_reads), writes=[dstB])
                else:
                    yield EM.op("dve", lambda en: en.tensor_copy(dst, src), reads=[srcB] + list(extra_reads), writes=[dstB])
            n = nch * 64
            chs = {"pr_r": hp, "pr_k": 8 + hp, "pr_v": 16 + hp, "pr_wa": 24, "pr_gd": 25}
            for nm, ch in chs.items():
                t = T[nm]; rows = K.pT_d[ch * 128:(ch + 1) * 128, :]
                if samp is not None:
                    yield EM.op("pool", lambda e, t=t: e.memset(t[:, 0:48], 0.0), writes=[B[nm]])
                    yield EM.dma("sp", t[:, 48:49], K.st_shift[:, ch, samp:samp + 1], writes=[B[nm]], semkey="ld%d_" % li + nm)
                    yield EM.dma("sp", t[:, 49:65], rows[:, NSEQ + 16 * samp:NSEQ + 16 * samp + 16], writes=[B[nm]], semkey="ld%d_" % li + nm)
                elif t0 == 0:
                    yield EM.op("pool", lambda e, t=t: e.memset(t[:, 0:1], 0.0), writes=[B[nm]])
                    yield EM.dma("sp", t[:, 1:n + 1], rows[:, 0:n], writes=[B[nm]], semkey="ld%d_" % li + nm)
                else:
                    yield EM.dma("sp", t[:, 0:n + 1], rows[:, t0 - 1:t0 + n], writes=[B[nm]], semkey="ld%d_" % li + nm)
            for nm, xn, ch in (("pr_r", "xr", hp), ("pr_k", "xk", 8 + hp), ("pr_v", "xv", 16 + hp), ("pr_wa", "xwa", 24), ("pr_gd", "xgd", 25)):
                t = T[nm]
                yield EM.op("pool", lambda e, t=t: e.tensor_tensor(T["dtmp"][:, 0:n], t[:, 0:n], t[:, 1:n + 1], ALU.subtract), reads=[B[nm]], writes=[B["dtmp"]])
                yield EM.op("dve", lambda e, t=t, xn=xn, ch=ch: e.scalar_tensor_tensor(T[xn][:, 0:n], T["dtmp"][:, 0:n], mu[:, ch:ch + 1], t[:, 1:n + 1], ALU.mult, ALU.add),
                     reads=[B["dtmp"], B[nm], B_pv], writes=[B[xn]])
            if samp is not None:
                for xn in ("xr", "xk", "xv", "xwa", "xgd"):
                    yield EM.op("pool", lambda e, xn=xn: e.tensor_tensor(T[xn][:, 0:n], T[xn][:, 0:n], vrw[:, 0:n], ALU.mult), reads=[B[xn], B_pv], writes=[B[xn]])
            yield EM.op("act", lambda e: e.activation(T["twa"][0:64, 0:n], T["xwa"][0:64, 0:n], AF.Tanh), reads=[B["xwa"]], writes=[B["twa"]])
            yield EM.op("dve", lambda e: e.tensor_copy(T["twa"][64:128, 0:n], T["xwa"][64:128, 0:n]), reads=[B["xwa"]], writes=[B["twa"]])
            yield EM.op("act", lambda e: e.activation(T["sg"][:, 0:n], T["xgd"][:, 0:n], AF.Sigmoid), reads=[B["xgd"]], writes=[B["sg"]])
            cs = slice(hp * 128, (hp + 1) * 128)
            for c0 in range(0, n, 512):
                cn = min(512, n - c0)
                pz = "pz%d" % (cnt["z"] % 2); cnt["z"] += 1
                yield EM.op("pe", lambda e, pz=pz: e.matmul(P[pz][:, 0:cn], ww2[0:64, cs], T["twa"][0:64, c0:c0 + cn], start=True, stop=True), reads=[B["twa"], B_pv], writes=[BP[pz]])
                yield EM.op("act", lambda e, pz=pz: e.activation(T["sigw"][:, c0:c0 + cn], P[pz][:, 0:cn], AF.Sigmoid, bias=pv[:, 0, hp:hp + 1], scale=1.0), reads=[BP[pz], B_pv], writes=[B["sigw"]])
                pz = "pz%d" % (cnt["z"] % 2); cnt["z"] += 1
                yield EM.op("pe", lambda e, pz=pz: e.matmul(P[pz][:, 0:cn], wa2[64:128, cs], T["twa"][64:128, c0:c0 + cn], start=True, stop=True), reads=[B["twa"], B_pv], writes=[BP[pz]])
                yield EM.op("act", lambda e, pz=pz: e.activation(T["av"][:, c0:c0 + cn], P[pz][:, 0:cn], AF.Sigmoid, bias=pv[:, 1, hp:hp + 1], scale=1.0), reads=[BP[pz], B_pv], writes=[B["av"]])
                if with_y:
                    pz = "pz%d" % (cnt["z"] % 2); cnt["z"] += 1
                    yield EM.op("pe", lambda e, pz=pz: e.matmul(P[pz][:, 0:cn], wg2[:, cs], T["sg"][:, c0:c0 + cn], start=True, stop=True), reads=[B["sg"], B_pv], writes=[BP[pz]])
                    yield EM.op("dve", lambda e, pz=pz: e.tensor_copy(T["gvv"][:, c0:c0 + cn], P[pz][:, 0:cn]), reads=[BP[pz]], writes=[B["gvv"]])
            if samp is not None:
                yield EM.op("dve", lambda e: e.scalar_tensor_tensor(T["ld"][:, 0:n], T["sigw"][:, 0:n], C1, vrw[:, 0:n], ALU.mult, ALU.mult), reads=[B["sigw"], B_pv], writes=[B["ld"]])
            else:
                yield EM.op("dve", lambda e: e.tensor_scalar(T["ld"][:, 0:n], T["sigw"][:, 0:n], C1, None, ALU.mult), reads=[B["sigw"]], writes=[B["ld"]])
            yield EM.op("dve", lambda e: e.tensor_tensor_scan(T["L"][:, 0:n], segm[:, 0:n], T["ld"][:, 0:n], 0.0, ALU.mult, ALU.add), reads=[B["ld"], B_pv], writes=[B["L"]])
            yield EM.op("act", lambda e: e.activation(T["eL"][:, 0:n], T["L"][:, 0:n], AF.Exp), reads=[B["L"]], writes=[B["eL"]])
            yield EM.op("act", lambda e: e.activation(T["enL"][:, 0:n], T["L"][:, 0:n], AF.Exp, scale=-1.0), reads=[B["L"]], writes=[B["enL"]])
            yield EM.op("pool", lambda e: e.tensor_tensor(T["eLm"][:, 0:n], T["L"][:, 0:n], T["ld"][:, 0:n], ALU.subtract), reads=[B["L"], B["ld"]], writes=[B["eLm"]])
            yield EM.op("act", lambda e: e.activation(T["eLm"][:, 0:n], T["eLm"][:, 0:n], AF.Exp), reads=[B["eLm"]], writes=[B["eLm"]])
            yield EM.op("dve", lambda e: e.tensor_copy(T["PC"][:, 0:nch], T["eL"][:, 0:n].rearrange("p (c t) -> p c t", t=64)[:, :, 63]), reads=[B["eL"]], writes=[B["PC"]])
            yield EM.op("dve", lambda e: e.tensor_scalar(T["kk"][:, 0:n], T["xk"][:, 0:n], pv[:, 2, hp:hp + 1], None, ALU.mult), reads=[B["xk"], B_pv], writes=[B["kk"]])
            yield EM.op("pool", lambda e: e.tensor_tensor(T["kk2"][:, 0:n], T["kk"][:, 0:n], T["kk"][:, 0:n], ALU.mult), reads=[B["kk"]], writes=[B["kk2"]])
            for c0 in range(0, n, 512):
                cn = min(512, n - c0)
                pz = "pz%d" % (cnt["z"] % 2); cnt["z"] += 1
                yield EM.op("pe", lambda e, pz=pz: e.matmul(P[pz][:, 0:cn], onesf, T["kk2"][:, c0:c0 + cn], start=True, stop=True), reads=[B["kk2"], K.B_cm], writes=[BP[pz]])
                yield EM.op("act", lambda e, pz=pz: e.activation(T["rn"][:, c0:c0 + cn], P[pz][:, 0:cn], AF.Sqrt, bias=K.epsc[:, 2:3], scale=64.0), reads=[BP[pz], K.B_const], writes=[B["rn"]])
            yield EM.op("dve", lambda e: e.reciprocal(T["rn"][:, 0:n], T["rn"][:, 0:n]), reads=[B["rn"]], writes=[B["rn"]])
            yield EM.op("pool", lambda e: e.tensor_tensor(T["kk"][:, 0:n], T["kk"][:, 0:n], T["rn"][:, 0:n], ALU.mult), reads=[B["kk"], B["rn"]], writes=[B["kk"]])
            yield EM.op("dve", lambda e: e.tensor_scalar(T["kp"][:, 0:n], T["av"][:, 0:n], -1.0, pv[:, 3, hp:hp + 1], ALU.add, ALU.mult), reads=[B["av"], B_pv], writes=[B["kp"]])
            yield EM.op("dve", lambda e: e.scalar_tensor_tensor(T["kp"][:, 0:n], T["kp"][:, 0:n], 1.0, T["xk"][:, 0:n], ALU.add, ALU.mult), reads=[B["kp"], B["xk"]], writes=[B["kp"]])
            if with_y:
                yield EM.op("dve", lambda e: e.scalar_tensor_tensor(T["kk2"][:, 0:n], T["xr"][:, 0:n], pv[:, 4, hp:hp + 1], T["kp"][:, 0:n], ALU.mult, ALU.mult), reads=[B["xr"], B["kp"], B_pv], writes=[B["kk2"]])
                for c0 in range(0, n, 512):
                    cn = min(512, n - c0)
                    pz = "pz%d" % (cnt["z"] % 2); cnt["z"] += 1
                    yield EM.op("pe", lambda e, pz=pz: e.matmul(P[pz][:, 0:cn], onesf, T["kk2"][:, c0:c0 + cn], start=True, stop=True), reads=[B["kk2"], K.B_cm], writes=[BP[pz]])
                    yield EM.op("dve", lambda e, pz=pz: e.scalar_tensor_tensor(T["bv"][:, c0:c0 + cn], P[pz][:, 0:cn], 64.0, T["xv"][:, c0:c0 + cn], ALU.mult, ALU.mult), reads=[BP[pz], B["xv"]], writes=[B["bv"]])
            yield EM.op("pool", lambda e: e.tensor_tensor(T["tb"][:, 0:n], T["kk"][:, 0:n], T["av"][:, 0:n], ALU.mult), reads=[B["kk"], B["av"]], writes=[B["tb"]])
            yield EM.op("pool", lambda e: e.tensor_tensor(T["tb"][:, 0:n], T["tb"][:, 0:n], T["enL"][:, 0:n], ALU.mult), reads=[B["tb"], B["enL"]], writes=[B["tb"]])
            yield EM.op("pool", lambda e: e.tensor_tensor(T["tk"][:, 0:n], T["kp"][:, 0:n], T["enL"][:, 0:n], ALU.mult), reads=[B["kp"], B["enL"]], writes=[B["tk"]])
            for (lo, hi) in ((0, 64), (64, 128)):
                co = slice(lo, hi)
                pcb = T["PC"][lo:hi, 0:nch].unsqueeze(2).to_broadcast([64, nch, 64])
                yield EM.op("dve", lambda e, lo=lo, hi=hi: e.tensor_tensor(T["bdAR"][lo:hi, 0:nch, 128 + lo:128 + hi], v3(T["xr"], lo, hi, nch), v3(T["eL"], lo, hi, nch), ALU.mult), reads=[B["xr"], B["eL"]], writes=[B["bdAR"]])
                yield EM.op("dve", lambda e, lo=lo, hi=hi: e.scalar_tensor_tensor(T["bdAR"][lo:hi, 0:nch, lo:hi], v3(T["kk"], lo, hi, nch), -1.0, v3(T["eLm"], lo, hi, nch), ALU.mult, ALU.mult), reads=[B["kk"], B["eLm"]], writes=[B["bdAR"]])
                yield EM.op("pool", lambda e, lo=lo, hi=hi: e.tensor_copy(T["bdB"][lo:hi, 0:nch, lo:hi], v3(T["tb"], lo, hi, nch)), reads=[B["tb"]], writes=[B["bdB"]])
                yield EM.op("pool", lambda e, lo=lo, hi=hi: e.tensor_copy(T["bdK"][lo:hi, 0:nch, lo:hi], v3(T["tk"], lo, hi, nch)), reads=[B["tk"]], writes=[B["bdK"]])
                yield EM.op("pool", lambda e, lo=lo, hi=hi: e.tensor_copy(T["bdV"][lo:hi, 0:nch, lo:hi], v3(T["xv"], lo, hi, nch)), reads=[B["xv"]], writes=[B["bdV"]])
                yield EM.op("dve", lambda e, lo=lo, hi=hi, pcb=pcb: e.tensor_tensor(T["bdBh"][lo:hi, 0:nch, lo:hi], v3(T["tb"], lo, hi, nch), pcb, ALU.mult), reads=[B["tb"], B["PC"]], writes=[B["bdBh"]])
                yield EM.op("dve", lambda e, lo=lo, hi=hi, pcb=pcb: e.tensor_tensor(T["bdKh"][lo:hi, 0:nch, lo:hi], v3(T["tk"], lo, hi, nch), pcb, ALU.mult), reads=[B["tk"], B["PC"]], writes=[B["bdKh"]])
            def grp(dst, width, lhs_fn, rhs_fn, rb, post=None, gsz=None):
                gsz = gsz or (512 // width)
                for c0 in range(0, nch, gsz):
                    g = min(gsz, nch - c0)
                    bk, Bb = nbank()
                    for i in range(g):
                        yield EM.op("pe", lambda e, i=i, c=c0 + i: e.matmul(bk[:, i * width:(i + 1) * width], lhs_fn(c), rhs_fn(c), start=True, stop=True), reads=rb, writes=[Bb], inc=(i == g - 1))
                    src = bk[:, 0:g * width].rearrange("p (g t) -> p g t", t=width)
                    d = T[dst][:, c0:c0 + g, :]
                    if post is None:
                        yield from evac(d, B[dst], src, Bb)
                    else:
                        yield from post(d, src, Bb, c0, g)
            def post_mask2(dst):
                def f(d, src, Bb, c0, g):
                    yield EM.op("dve", lambda e: e.tensor_tensor(d, src, T["mask2"][:, 0:g, :], ALU.mult), reads=[Bb, B["mask2"]], writes=[B[dst]])
                return f
            A_ = lambda c: T["bdAR"][:, c, 0:128]; R_ = lambda c: T["bdAR"][:, c, 128:256]
            yield from grp("S1", 256, lambda c: T["bdB"][:, c, :], lambda c: T["bdAR"][:, c, :], [B["bdB"], B["bdAR"]], post=post_mask2("S1"))
            yield from grp("S2", 256, lambda c: T["bdK"][:, c, :], lambda c: T["bdAR"][:, c, :], [B["bdK"], B["bdAR"]], post=post_mask2("S2"))
            def post_N(d, src, Bb, c0, g):
                yield EM.op("dve", lambda e: e.tensor_tensor(d, src, T["mSLg"][:, 0:g, :], ALU.mult), reads=[Bb, B["mSLg"]], writes=[B["N0"]])
            yield from grp("N0", 128, A_, lambda c: T["bdB"][:, c, :], [B["bdB"], B["bdAR"]], post=post_N)
            yield EM.op("pool", lambda e: e.tensor_copy(T["M0"][:, 0:nch, :], T["S1"][:, 0:nch, 0:128]), reads=[B["S1"]], writes=[B["M0"]])
            for c0 in range(0, nch, 4):
                g = min(4, nch - c0)
                yield EM.op("pool", lambda e, c0=c0, g=g: e.tensor_tensor(T["T0"][:, c0:c0 + g, :], T["S1"][:, c0:c0 + g, 0:128], T["identg"][:, 0:g, :], ALU.add), reads=[B["S1"], B["identg"]], writes=[B["T0"]])
            cur = 0
            for k in range(1, 6):
                nx = 1 - cur
                Nc, Mc, Tc = "N%d" % cur, "M%d" % cur, "T%d" % cur; Nx, Mx, Tx = "N%d" % nx, "M%d" % nx, "T%d" % nx
                yield from grp(Nx, 128, lambda c: T[Mc][:, c, :], lambda c: T[Nc][:, c, :], [B[Mc], B[Nc]])
                if k < 5:
                    yield from grp(Mx, 128, lambda c: T[Nc][:, c, :], lambda c: T[Mc][:, c, :], [B[Mc], B[Nc]])
                def post_T(d, src, Bb, c0, g, Tc=Tc, Tx=Tx):
                    yield EM.op("dve", lambda e: e.tensor_tensor(d, src, T[Tc][:, c0:c0 + g, :], ALU.add), reads=[Bb, B[Tc]], writes=[B[Tx]])
                yield from grp(Tx, 128, lambda c: T[Nx][:, c, :], lambda c: T[Tc][:, c, :], [B[Nx], B[Tc]], post=post_T)
                cur = nx
            TT = "T%d" % cur
            for src_n, dst_n, sl_ in (("bdV", "Vb", None), ("bdKh", "Kh", None), ("bdBh", "Bh", None), ("bdAR", "At", slice(0, 128))):
                for c0 in range(0, nch, 4):
                    g = min(4, nch - c0)
                    for i in range(g):
                        c = c0 + i
                        src_ap = T[src_n][:, c, :] if sl_ is None else T[src_n][:, c, sl_]
                        yield EM.op("pe", lambda e, i=i, src_ap=src_ap: e.transpose(bankt[:, li * 512 + i * 128:li * 512 + (i + 1) * 128], src_ap, ident), reads=[B[src_n], K.B_cm], writes=[bkB[7]])
                    yield from evac(T[dst_n][:, c0:c0 + g, :], B[dst_n], bankt[:, li * 512:li * 512 + g * 128].rearrange("p (g t) -> p g t", t=128), bkB[7])
            yield from grp("AV", 128, lambda c: T["S2"][:, c, 0:128], lambda c: T["Vb"][:, c, :], [B["S2"], B["Vb"]])
            yield from grp("Dm", 128, lambda c: T[TT][:, c, :], lambda c: T["AV"][:, c, :], [B[TT], B["AV"]])
            yield from grp("XT", 128, lambda c: T[TT][:, c, :], lambda c: T["At"][:, c, :], [B[TT], B["At"]])
            yield from grp("Gm", 128, lambda c: T["XT"][:, c, :], lambda c: T["Bh"][:, c, :], [B["XT"], B["Bh"]])
            for c0 in range(0, nch, 4):
                g = min(4, nch - c0)
                bk, Bb = nbank()
                for i in range(g):
                    c = c0 + i
                    yield EM.op("pe", lambda e, i=i, c=c: e.matmul(bk[:, i * 128:(i + 1) * 128], T["Kh"][:, c, :], T["Vb"][:, c, :], start=True, stop=False), reads=[B["Kh"], B["Vb"]], writes=[Bb], inc=False)
                    yield EM.op("pe", lambda e, i=i, c=c: e.matmul(bk[:, i * 128:(i + 1) * 128], T["Bh"][:, c, :], T["Dm"][:, c, :], start=False, stop=True), reads=[B["Bh"], B["Dm"]], writes=[Bb], inc=(i == g - 1))
                yield EM.op("act", lambda e, c0=c0, g=g, bk=bk: e.copy(T["E"][:, c0:c0 + g, :], bk[:, 0:g * 128].rearrange("p (g t) -> p g t", t=128)), reads=[Bb], writes=[B["E"]])
            if with_y:
                def post_Q(d, src, Bb, c0, g):
                    yield EM.op("dve", lambda e: e.tensor_tensor(d, src, T["bdAR"][:, c0:c0 + g, 128:256], ALU.add), reads=[Bb, B["bdAR"]], writes=[B["Qc"]])
                yield from grp("Qc", 128, lambda c: T["XT"][:, c, :], lambda c: T["S1"][:, c, 128:256], [B["XT"], B["S1"]], post=post_Q)
            yield EM.op("act", lambda e: e.copy(T["Hall"][:, 0, :], T["Hf"][:, :]), reads=[B["Hf"]], writes=[B["Hall"]])
            for c in range(nch):
                tmpn = "tmpH%d" % (c % 2)
                yield EM.op("dve", lambda e, c=c, tmpn=tmpn: e.scalar_tensor_tensor(T[tmpn][:, :], T["Hf"][:, :], T["PC"][:, c:c + 1], T["E"][:, c, :], ALU.mult, ALU.add), reads=[B["Hf"], B["PC"], B["E"]], writes=[B[tmpn]])
                bk = banks[2 * li]; Bb = bkB[2 * li]
                yield EM.op("pe", lambda e, c=c, bk=bk: e.matmul(bk[:, 0:128], T["Gm"][:, c, :], T["Hall"][:, c, :], start=True, stop=True), reads=[B["Gm"], B["Hall"]], writes=[Bb])
                yield EM.op("dve", lambda e, c=c, bk=bk, tmpn=tmpn: e.tensor_tensor(T["Hall"][:, c + 1, :], bk[:, 0:128], T[tmpn][:, :], ALU.add), reads=[Bb, B[tmpn]], writes=[B["Hall"]])
                yield EM.op("dve", lambda e, c=c, bk=bk, tmpn=tmpn: e.tensor_tensor(T["Hf"][:, :], bk[:, 0:128], T[tmpn][:, :], ALU.add), reads=[Bb, B[tmpn]], writes=[B["Hf"]])
            if with_y:
                for c0 in range(0, nch, 4):
                    g = min(4, nch - c0)
                    bk, Bb = nbank()
                    for i in range(g):
                        c = c0 + i; reg = bk[:, i * 128:(i + 1) * 128]
                        yield EM.op("pe", lambda e, c=c, reg=reg: e.matmul(reg, T["Hall"][:, c, :], T["Qc"][:, c, :], start=True, stop=False), reads=[B["Hall"], B["Qc"]], writes=[Bb], inc=False)
                        yield EM.op("pe", lambda e, c=c, reg=reg: e.matmul(reg, T["Dm"][:, c, :], T["S1"][:, c, 128:256], start=False, stop=False), reads=[B["Dm"], B["S1"]], writes=[Bb], inc=False)
                        yield EM.op("pe", lambda e, c=c, reg=reg: e.matmul(reg, T["Vb"][:, c, :], T["S2"][:, c, 128:256], start=False, stop=True), reads=[B["Vb"], B["S2"]], writes=[Bb], inc=(i == g - 1))
                    v = bk[:, 0:g * 128].rearrange("p (g t) -> p g t", t=128)
                    yield EM.op("act", lambda e, c0=c0, g=g, v=v: e.copy(T["yT"][0:64, c0 * 64:(c0 + g) * 64].rearrange("p (g t) -> p g t", t=64), v[0:64, :, 0:64]), reads=[Bb], writes=[B["yT"]])
                    yield EM.op("dve", lambda e, c0=c0, g=g, v=v: e.tensor_copy(T["yT"][64:128, c0 * 64:(c0 + g) * 64].rearrange("p (g t) -> p g t", t=64), v[64:128, :, 64:128]), reads=[Bb], writes=[B["yT"]])
            if not with_y:
                return
            for c0 in range(0, n, 512):
                cn = min(512, n - c0)
                pz = "pz%d" % (cnt["z"] % 2); cnt["z"] += 1
                yield EM.op("pe", lambda e, pz=pz: e.matmul(P[pz][:, 0:cn], onesf, T["yT"][:, c0:c0 + cn], start=True, stop=True), reads=[B["yT"], K.B_cm], writes=[BP[pz]])
                yield EM.op("dve", lambda e, pz=pz: e.tensor_tensor(T["yc"][:, c0:c0 + cn], T["yT"][:, c0:c0 + cn], P[pz][:, 0:cn], ALU.subtract), reads=[B["yT"], BP[pz]], writes=[B["yc"]])
                yield EM.op("pool", lambda e: e.tensor_tensor(T["ysq"][:, c0:c0 + cn], T["yc"][:, c0:c0 + cn], T["yc"][:, c0:c0 + cn], ALU.mult), reads=[B["yc"]], writes=[B["ysq"]])
                pz = "pz%d" % (cnt["z"] % 2); cnt["z"] += 1
                yield EM.op("pe", lambda e, pz=pz: e.matmul(P[pz][:, 0:cn], onesf, T["ysq"][:, c0:c0 + cn], start=True, stop=True), reads=[B["ysq"], K.B_cm], writes=[BP[pz]])
                yield EM.op("act", lambda e, pz=pz: e.activation(T["ysq"][:, c0:c0 + cn], P[pz][:, 0:cn], AF.Sqrt, bias=K.epsc[:, 1:2], scale=1.0), reads=[BP[pz], K.B_const], writes=[B["ysq"]])
            yield EM.op("dve", lambda e: e.reciprocal(T["ysq"][:, 0:n], T["ysq"][:, 0:n]), reads=[B["ysq"]], writes=[B["ysq"]])
            yield EM.op("pool", lambda e: e.tensor_tensor(T["yc"][:, 0:n], T["yc"][:, 0:n], T["ysq"][:, 0:n], ALU.mult), reads=[B["yc"], B["ysq"]], writes=[B["yc"]])
            yield EM.op("dve", lambda e: e.tensor_scalar(T["yc"][:, 0:n], T["yc"][:, 0:n], pv[:, 5, hp:hp + 1], pv[:, 6, hp:hp + 1], ALU.mult, ALU.add), reads=[B["yc"], B_pv], writes=[B["yc"]])
            yield EM.op("pool", lambda e: e.tensor_tensor(T["yc"][:, 0:n], T["yc"][:, 0:n], T["bv"][:, 0:n], ALU.add), reads=[B["yc"], B["bv"]], writes=[B["yc"]])
            yield EM.op("dve", lambda e: e.tensor_tensor(T["ob"][:, 0:n], T["yc"][:, 0:n], T["gvv"][:, 0:n], ALU.mult), reads=[B["yc"], B["gvv"]], writes=[B["ob"]])
            rows = K.oT_d[1024 + hp * 128:1024 + (hp + 1) * 128, :]
            if samp is None:
                yield EM.dma("sp", rows[:, t0 - HALO0:t0 - HALO0 + n], T["ob"][:, 0:n], reads=[B["ob"]], writes=[B_oT], semkey="st_oT%d" % li)
            else:
                yield EM.dma("sp", rows[:, 1152 + 16 * samp:1152 + 16 * samp + 16], T["ob"][:, 48:64], reads=[B["ob"]], writes=[B_oT], semkey="st_oT%d" % li)

        def pair_prog(li, hp):
            T = TS[li]; B = BS[li]
            yield EM.op("pool", lambda e: e.memset(T["Hf"][:, :], 0.0), writes=[B["Hf"]])
            blocks = [(c0 * 64, min(NCM, 46 - c0), False) for c0 in range(0, 46, NCM)] + [(2944 + c0 * 64, min(NCM, 18 - c0), True) for c0 in range(0, 18, NCM)]
            for (t0, nch, wy) in blocks:
                yield from rw_block(li, hp, t0, nch, wy)
            yield EM.dma("sp", K.wkv_p[2 * hp, :, :], T["Hf"][0:64, 0:64], reads=[B["Hf"]], writes=[B_out], semkey="st_out3")
            yield EM.dma("sp", K.wkv_p[2 * hp + 1, :, :], T["Hf"][64:128, 64:128], reads=[B["Hf"]], writes=[B_out], semkey="st_out3")
            for s_ in range(4):
                yield EM.op("pool", lambda e: e.memset(T["Hf"][:, :], 0.0), writes=[B["Hf"]])
                yield EM.dma("sp", T["Hf"][0:64, 0:64], K.st_wkv[s_, 2 * hp, :, :], writes=[B["Hf"]], semkey="ld_H%d" % li)
                yield EM.dma("sp", T["Hf"][64:128, 64:128], K.st_wkv[s_, 2 * hp + 1, :, :], writes=[B["Hf"]], semkey="ld_H%d" % li)
                yield from rw_block(li, hp, 0, 1, True, samp=s_)
                yield EM.dma("sp", K.wkv_s[s_, 2 * hp, :, :], T["Hf"][0:64, 0:64], reads=[B["Hf"]], writes=[B_out], semkey="st_out3")
                yield EM.dma("sp", K.wkv_s[s_, 2 * hp + 1, :, :], T["Hf"][64:128, 64:128], reads=[B["Hf"]], writes=[B_out], semkey="st_out3")
        def lane_prog(li):
            for hp in range(li, 8, 2):
                yield from pair_prog(li, hp)
        g2 = phase2_gen(K, es, banks, bkB)
        lanes = [lane_prog(0), lane_prog(1)]
        alive = [g2] + lanes
        it_ = 0
        while alive:
            it_ += 1
            if it_ % 150 == 0 and K.cast_thunks:
                K.cast_thunks.pop(0)()
            for g_ in list(alive):
                try:
                    next(g_)
                except StopIteration:
                    alive.remove(g_)
        while K.cast_thunks:
            K.cast_thunks.pop(0)()
        S.barrier()


def phase2_gen(K, es, banks, bkB):
    S = K.S; nc = K.nc
    if True:
        sbt = lambda n, s, d: K.sbt(es, n, s, d); pst = lambda n, s, d: K.pst(es, n, s, d)
        T = {}; B = {}
        def mk(name, shape, dt):
            T[name] = sbt(name, shape, dt); B[name] = Buf(name)
        mk("lt", [128, 4, 64], F32); mk("lp", [128, 2, 64], F32); mk("lam", [128, 4], F32)
        mk("gsub", [128, 128], F32); mk("kb", [128, 24], F32)
        yield S.dma("sp", T["lt"][:, :, :], K.lamv.partition_broadcast(128), writes=[B["lt"]], semkey="a_lt")
        yield S.dma("sp", T["gsub"][:, :], K.g_subln.partition_broadcast(128), writes=[B["gsub"]], semkey="a_gs")
        yield S.dma("sp", T["kb"][:, :], K.kbias[:, :], writes=[B["kb"]], semkey="a_kb")
        yield S.op("dve", lambda e: e.tensor_tensor(T["lp"][:, :, :], T["lt"][:, :, :].rearrange("p (a b) d -> p a b d", b=2)[:, :, 0, :], T["lt"][:, :, :].rearrange("p (a b) d -> p a b d", b=2)[:, :, 1, :], ALU.mult), reads=[B["lt"]], writes=[B["lp"]])
        yield S.op("dve", lambda e: e.reduce_sum(T["lam"][:, 0:2], T["lp"][:, :, :], AX.X), reads=[B["lp"]], writes=[B["lam"]])
        yield S.op("act", lambda e: e.activation(T["lam"][:, 0:2], T["lam"][:, 0:2], AF.Exp), reads=[B["lam"]], writes=[B["lam"]])
        yield S.op("dve", lambda e: e.tensor_tensor(T["lam"][:, 2:3], T["lam"][:, 1:2], T["lam"][:, 0:1], ALU.subtract), reads=[B["lam"]], writes=[B["lam"]])
        yield S.op("dve", lambda e: e.tensor_scalar(T["lam"][:, 3:4], T["lam"][:, 2:3], -LAM_INIT, None, ALU.add), reads=[B["lam"]], writes=[B["lam"]])
        yield S.op("dve", lambda e: e.tensor_scalar(T["gsub"][:, :], T["gsub"][:, :], 1.0 - LAM_INIT, None, ALU.mult), reads=[B["gsub"]], writes=[B["gsub"]])
        nlam = T["lam"][:, 3:4]
        mk("kTh", [128, NTOK], BF16); mk("qTh", [128, NREG], BF16); mk("Vh", [128, 32, 129], BF16); mk("Vn", [16, 4, 129], BF16)
        for i in range(1):
            mk("ckT%d" % i, [128, 4096], BF16); mk("cV%d" % i, [128, 32, 129], BF16)
        for s_ in range(4):
            mk("Pp%d" % s_, [128, 32, 64], BF16); mk("Pn%d" % s_, [16, 64], BF16)
            yield S.op("pool", lambda e, s_=s_: e.memset(T["Pp%d" % s_][:, :, :], 0.0), writes=[B["Pp%d" % s_]])
            yield S.op("pool", lambda e, s_=s_: e.memset(T["Pn%d" % s_][:, :], 0.0), writes=[B["Pn%d" % s_]])
        for nm in ("Vh", "cV0"):
            yield S.op("pool", lambda e, nm=nm: e.memset(T[nm][:, :, 128:129], 1.0), writes=[B[nm]])
        yield S.op("pool", lambda e: e.memset(T["Vn"][:, :, 128:129], 1.0), writes=[B["Vn"]])
        for i in range(4):
            mk("PT%d" % i, [128, 384], BF16)
        mk("t0", [128, 128], F32); mk("o", [128, 128], F32); mk("osq", [128, 128], F32); mk("ob", [128, 128], F32)
        mk("sc", [128, 8], F32); mk("oTs", [128, NREG], BF16)
        PS = [banks[6][:, 0:384]]; BPS = [bkB[6]]
        PO = [[banks[4 + m][:, q * 129:(q + 1) * 129] for q in range(3)] for m in range(2)]; BPO = [[bkB[4 + m]] * 3 for m in range(2)]
        PSs = banks[6][:, :]; BPSs = bkB[6]
        PTR = [banks[6][:, 384:512]]; BPTR = [bkB[6]]
        B_oT = Buf("oTa")
        ident = K.cm_f[:, 0, :]
        cnt = {"s": 0, "tr": 0, "pt": 0}

        def finalize(h, np_, O0, O1, BO0, BO1, col0):
            sc = T["sc"]
            yield S.op("dve", lambda e: e.tensor_scalar(sc[0:np_, 0:1], O0[0:np_, 128:129], 1e-30, None, ALU.max), reads=[BO0], writes=[B["sc"]])
            yield S.op("dve", lambda e: e.tensor_scalar(sc[0:np_, 1:2], O1[0:np_, 128:129], 1e-30, None, ALU.max), reads=[BO1], writes=[B["sc"]])
            yield S.op("dve", lambda e: e.reciprocal(sc[0:np_, 0:2], sc[0:np_, 0:2]), reads=[B["sc"]], writes=[B["sc"]])
            yield S.op("dve", lambda e: e.tensor_tensor(sc[0:np_, 1:2], sc[0:np_, 1:2], nlam[0:np_, :], ALU.mult), reads=[B["sc"], B["lam"]], writes=[B["sc"]])
            yield S.op("dve", lambda e: e.tensor_scalar(T["t0"][0:np_, :], O0[0:np_, 0:128], sc[0:np_, 0:1], None, ALU.mult), reads=[BO0, B["sc"]], writes=[B["t0"]])
            yield S.op("dve", lambda e: e.scalar_tensor_tensor(T["o"][0:np_, :], O1[0:np_, 0:128], sc[0:np_, 1:2], T["t0"][0:np_, :], ALU.mult, ALU.add), reads=[BO1, B["sc"], B["t0"]], writes=[B["o"]])
            yield S.op("act", lambda e: e.activation(T["osq"][0:np_, :], T["o"][0:np_, :], AF.Square, accum_out=sc[0:np_, 2:3]), reads=[B["o"]], writes=[B["osq"], B["sc"]])
            yield S.op("act", lambda e: e.activation(sc[0:np_, 3:4], sc[0:np_, 2:3], AF.Sqrt, bias=K.epsc[0:np_, 0:1], scale=1.0 / 128), reads=[B["sc"], K.B_const], writes=[B["sc"]])
            yield S.op("dve", lambda e: e.reciprocal(sc[0:np_, 3:4], sc[0:np_, 3:4]), reads=[B["sc"]], writes=[B["sc"]])
            yield S.op("dve", lambda e: e.scalar_tensor_tensor(T["ob"][0:np_, :], T["o"][0:np_, :], sc[0:np_, 3:4], T["gsub"][0:np_, :], ALU.mult, ALU.mult), reads=[B["o"], B["sc"], B["gsub"]], writes=[B["ob"]])
            pt = PTR[0]; Bpt = BPTR[0]
            yield S.op("pe", lambda e: e.transpose(pt[:, 0:np_], T["ob"][0:np_, :], ident[0:np_, 0:np_]), reads=[B["ob"], K.B_cm], writes=[Bpt])
            yield S.op("act", lambda e: e.copy(T["oTs"][:, col0:col0 + np_], pt[:, 0:np_]), reads=[Bpt], writes=[B["oTs"]])

        vd_v = K.v_d[0:NSEQ, :].rearrange("(t p) c -> p t c", p=128)
        for h in range(8):
            hs = slice(h * 128, (h + 1) * 128)
            yield S.dma("sp", T["kTh"][:, :], K.kT_d[hs, :], writes=[B["kTh"]], semkey="a_kT")
            yield S.dma("sp", T["qTh"][:, :], K.qT_d[hs, :], writes=[B["qTh"]], semkey="a_qT")
            yield S.dma("pool", T["Vh"][:, :, 0:128], vd_v[:, :, hs], writes=[B["Vh"]], semkey="a_Vh")
            yield S.dma("pool", T["Vn"][:, :, 0:128], K.v_d[NSEQ:NTOK, hs].rearrange("(s t) c -> t s c", t=16), writes=[B["Vn"]], semkey="a_Vn")
            for G in range(3):
                last_kt = 23 + 3 * G + 2
                for kt in range(last_kt + 1):
                    vis = [qi for qi in range(3) if 23 + 3 * G + qi >= kt]
                    for m in range(2):
                        ms = slice(64 * m, 64 * m + 64)
                        si = 0; pti = cnt["pt"] % 4; cnt["pt"] += 1
                        yield S.op("pe", lambda e, si=si, ms=ms: e.matmul(PS[si], T["kTh"][ms, kt * 128:(kt + 1) * 128], T["qTh"][ms, G * 384:(G + 1) * 384], start=True, stop=True),
                             reads=[B["kTh"], B["qTh"]], writes=[BPS[si]])
                        PT = T["PT%d" % pti]; BPT = B["PT%d" % pti]
                        if kt < 24:
                            yield S.op("act", lambda e, si=si, PT=PT: e.activation(PT[:, :], PS[si], AF.Exp, bias=T["kb"][:, kt:kt + 1], scale=0.125), reads=[BPS[si], B["kb"]], writes=[BPT])
                        else:
                            yield S.op("act", lambda e, si=si, PT=PT: e.activation(PT[:, :], PS[si], AF.Exp, scale=0.125), reads=[BPS[si]], writes=[BPT])
                        for qi in vis:
                            if 23 + 3 * G + qi == kt:
                                yield S.op("pool", lambda e, PT=PT, qi=qi: e.memset(PT[64:128, qi * 128:qi * 128 + 64], 0.0), writes=[BPT])
                        for qi in vis:
                            yield S.op("pe", lambda e, PT=PT, qi=qi, m=m: e.matmul(PO[m][qi], PT[:, qi * 128:(qi + 1) * 128], T["Vh"][:, kt, :], start=(kt == 0 and qi == 0), stop=(kt == 23 + 3 * G + qi), skip_group_check=True),
                                 reads=[BPT, B["Vh"]], writes=[BPO[m][qi]], inc=(kt == 23 + 3 * G + qi))
                for qi in range(3):
                    yield from finalize(h, 128, PO[0][qi], PO[1][qi], BPO[0][qi], BPO[1][qi], (3 * G + qi) * 128)
            Os = [banks[4][0:64, 0:129], banks[5][0:64, 0:129]]; BOs = [BPO[0][0], BPO[1][0]]
            for s_ in range(4):
                i = 0
                ck = T["ckT%d" % i]; cv = T["cV%d" % i]; Bck = B["ckT%d" % i]; Bcv = B["cV%d" % i]
                yield S.dma("pool", ck[:, :], K.cache_kT[s_, h, :, :], writes=[Bck], semkey="a_ck%d" % i)
                yield S.dma("pool", cv[:, :, 0:128], K.cache_v[s_, :, h, :].rearrange("(t p) d -> p t d", p=128), writes=[Bcv], semkey="a_cv%d" % i)
                Pp = T["Pp%d" % s_]; Pn = T["Pn%d" % s_]; BPp = B["Pp%d" % s_]; BPn = B["Pn%d" % s_]
                qc = slice(1152 + 16 * s_, 1152 + 16 * s_ + 16)
                for m in range(2):
                    ms = slice(64 * m, 64 * m + 64)
                    for kt in range(32):
                        yield S.op("pe", lambda e, kt=kt, ms=ms: e.matmul(PSs[:, kt * 16:(kt + 1) * 16], ck[ms, kt * 128:(kt + 1) * 128], T["qTh"][ms, qc], start=True, stop=True),
                             reads=[Bck, B["qTh"]], writes=[BPSs], inc=(kt == 31))
                    yield S.op("act", lambda e: e.activation(Pp[:, :, 16 * s_:16 * s_ + 16], PSs.rearrange("p (k q) -> p k q", q=16), AF.Exp, scale=0.125), reads=[BPSs], writes=[BPp])
                    for kt in range(32):
                        yield S.op("pe", lambda e, kt=kt, m=m: e.matmul(Os[m], Pp[:, kt, :], cv[:, kt, :], start=(s_ == 0 and kt == 0), stop=False, skip_group_check=True),
                             reads=[BPp, Bcv], writes=[BOs[m]], inc=False)
                    si = 0
                    yield S.op("pe", lambda e, si=si, ms=ms: e.matmul(PS[si][0:16, 0:16], T["kTh"][ms, NSEQ + 16 * s_:NSEQ + 16 * s_ + 16], T["qTh"][ms, qc], start=True, stop=True),
                         reads=[B["kTh"], B["qTh"]], writes=[BPS[si]])
                    yield S.op("act", lambda e, si=si: e.activation(Pn[:, 16 * s_:16 * s_ + 16], PS[si][0:16, 0:16], AF.Exp, scale=0.125), reads=[BPS[si]], writes=[BPn])
                    yield S.op("pe", lambda e, m=m: e.matmul(Os[m], Pn[:, :], T["Vn"][:, s_, :], start=False, stop=(s_ == 3), skip_group_check=True),
                         reads=[BPn, B["Vn"]], writes=[BOs[m]], inc=True)
            yield from finalize(h, 64, Os[0], Os[1], BOs[0], BOs[1], 1152)
            yield S.dma("sp", K.oT_d[hs, :], T["oTs"][:, :], reads=[B["oTs"]], writes=[B_oT], semkey="st_oTa")


def phase4(K):
    S = K.S; nc = K.nc
    x1_d = K.x1_d
    NT = 10
    def trow(t):
        return (t * 128, 128) if t < 9 else (1152, 64)
    h2T_d = K.h2T_d
    with ExitStack() as es0:
        B_x1d = Buf("x1d"); B_y = Buf("y_out"); B_h2d = Buf("h2d")
        ident = K.cm_b[:, 0, :]
        with ExitStack() as es:
            sbt = lambda n, s, d: K.sbt(es, n, s, d); pst = lambda n, s, d: K.pst(es, n, s, d)
            rows = sbt("rowsA", [128, 2, 3, D], F32); B_rows = Buf("rowsA")
            S.dma("sp", rows[:, 0, :, :], K.rows_d[0:3, 0, :].partition_broadcast(128), writes=[B_rows], semkey="f_rows")
            for s_ in range(4):
                S.dma("sp", rows[16 * s_:16 * s_ + 16, 1, :, :], K.rows_d[0:3, 1 + s_, :].partition_broadcast(16), writes=[B_rows], semkey="f_rows")
            h2Ts = [sbt("h2Ts%d" % i, [128, 16, 128], BF16) for i in range(2)]; B_h2Ts = [Buf("h2Ts0"), Buf("h2Ts1")]
            wo = sbt("wo", [128, 16, D], BF16); B_wo = Buf("wo")
            S.dma("sp", wo[:, :, :], K.w_out_b.rearrange("(c p) n -> p c n", p=128), writes=[B_wo], semkey="f_wo")
            oTt = [sbt("oTt%d" % i, [128, 16, 128], BF16) for i in range(2)]; B_oTt = [Buf("oTt0"), Buf("oTt1")]
            xt = sbt("xt4", [128, D], F32); B_xt = Buf("xt4")
            tmp = sbt("tmp4", [128, D], F32); B_tmp = Buf("tmp4")
            x1 = sbt("x1", [128, D], F32); B_x1 = Buf("x1")
            h2 = sbt("h2", [128, D], BF16); B_h2 = Buf("h2")
            sc = sbt("sc4", [128, 16], F32); B_sc = Buf("sc4")
            vcol = sbt("vcol", [128, 1], F32); B_vc = Buf("vcol")
            pm = [pst("pm%d" % i, [128, 512], F32) for i in range(4)]; B_pm = [Buf("pm%d" % i) for i in range(4)]
            ptb = pst("ptb", [128, 1024], BF16); PTR = [ptb[:, i * 128:(i + 1) * 128] for i in range(4)]; _bp4 = Buf("ptr4"); B_ptr = [_bp4] * 4
            tr = [0]
            for t in range(NT):
                r0, nr = trow(t); ri = 0 if t < 9 else 1
                o = oTt[t % 2]; Bo = B_oTt[t % 2]
                S.dma("sp", o[:, :, 0:nr], K.oT_d[:, r0:r0 + nr].rearrange("(c p) t -> p c t", p=128), writes=[Bo], semkey="f_oT%d" % (t % 2))
                S.dma("sp", xt[0:nr, :], K.x_tm[r0:r0 + nr, :], writes=[B_xt], semkey="f_xt")
                S.dma("sp", vcol[0:nr, :], K.valid[0:1, HALO0 + r0:HALO0 + r0 + nr].rearrange("o t -> t o"), writes=[B_vc], semkey="f_vc")
                for g in range(4):
                    for k in range(16):
                        S.op("pe", lambda e, g=g, k=k: e.matmul(pm[g][0:nr, :], o[:, k, 0:nr], wo[:, k, g * 512:(g + 1) * 512], start=(k == 0), stop=(k == 15)),
                             reads=[Bo, B_wo], writes=[B_pm[g]], inc=(k == 15))
                for g in range(4):
                    S.op("act", lambda e, g=g: e.activation(tmp[0:nr, g * 512:(g + 1) * 512], pm[g][0:nr, :], AF.Square, accum_out=sc[0:nr, g:g + 1]), reads=[B_pm[g]], writes=[B_tmp, B_sc])
                S.op("dve", lambda e: e.reduce_sum(sc[0:nr, 4:5], sc[0:nr, 0:4], AX.X), reads=[B_sc], writes=[B_sc])
                S.op("act", lambda e: e.activation(sc[0:nr, 5:6], sc[0:nr, 4:5], AF.Sqrt, bias=K.epsc[0:nr, 0:1], scale=1.0 / D), reads=[B_sc, K.B_const], writes=[B_sc])
                S.op("dve", lambda e: e.reciprocal(sc[0:nr, 5:6], sc[0:nr, 5:6]), reads=[B_sc], writes=[B_sc])
                for g in range(4):
                    gs = slice(g * 512, (g + 1) * 512)
                    S.op("dve", lambda e, g=g, gs=gs: e.scalar_tensor_tensor(tmp[0:nr, gs], pm[g][0:nr, :], sc[0:nr, 5:6], rows[0:nr, ri, 0, gs], ALU.mult, ALU.mult), reads=[B_pm[g], B_sc, B_rows], writes=[B_tmp])
                S.op("pool", lambda e: e.tensor_tensor(x1[0:nr, :], xt[0:nr, :], tmp[0:nr, :], ALU.add), reads=[B_xt, B_tmp], writes=[B_x1])
                S.dma("pool", x1_d[r0:r0 + nr, :], x1[0:nr, :], reads=[B_x1], writes=[B_x1d], semkey="st_x1")
                S.op("act", lambda e: e.activation(tmp[0:nr, :], x1[0:nr, :], AF.Square, accum_out=sc[0:nr, 6:7]), reads=[B_x1], writes=[B_tmp, B_sc])
                S.op("act", lambda e: e.activation(sc[0:nr, 7:8], sc[0:nr, 6:7], AF.Sqrt, bias=K.epsc[0:nr, 0:1], scale=1.0 / D), reads=[B_sc, K.B_const], writes=[B_sc])
                S.op("dve", lambda e: e.reciprocal(sc[0:nr, 7:8], sc[0:nr, 7:8]), reads=[B_sc], writes=[B_sc])
                S.op("dve", lambda e: e.tensor_tensor(sc[0:nr, 7:8], sc[0:nr, 7:8], vcol[0:nr, :], ALU.mult), reads=[B_sc, B_vc], writes=[B_sc])
                S.op("dve", lambda e: e.scalar_tensor_tensor(tmp[0:nr, :], x1[0:nr, :], sc[0:nr, 7:8], rows[0:nr, ri, 1, :], ALU.mult, ALU.mult), reads=[B_x1, B_sc, B_rows], writes=[B_tmp])
                S.op("dve", lambda e: e.scalar_tensor_tensor(h2[0:nr, :], rows[0:nr, ri, 2, :], vcol[0:nr, :], tmp[0:nr, :], ALU.mult, ALU.add), reads=[B_rows, B_vc, B_tmp], writes=[B_h2])
                for c in range(16):
                    pt = PTR[tr[0] % 4]; Bpt = B_ptr[tr[0] % 4]; tr[0] += 1
                    S.op("pe", lambda e, c=c, pt=pt: e.transpose(pt[:, 0:nr], h2[0:nr, c * 128:(c + 1) * 128], ident[0:nr, 0:nr]), reads=[B_h2, K.B_cm], writes=[Bpt])
                    if c % 2 == 0:
                        S.op("act", lambda e, c=c, pt=pt: e.copy(h2Ts[t % 2][:, c, 0:nr], pt[:, 0:nr]), reads=[Bpt], writes=[B_h2Ts[t % 2]])
                    else:
                        S.op("dve", lambda e, c=c, pt=pt: e.tensor_copy(h2Ts[t % 2][:, c, 0:nr], pt[:, 0:nr]), reads=[Bpt], writes=[B_h2Ts[t % 2]])
                S.dma("pool", h2T_d[:, r0:r0 + nr].rearrange("(c p) t -> p c t", p=128), h2Ts[t % 2][:, :, 0:nr], reads=[B_h2Ts[t % 2]], writes=[B_h2d], semkey="st_h2")
            S.barrier()
        aT = K.sbt(es0, "aT", [128, 44, NREG], BF16); B_aT = Buf("aT")
        with ExitStack() as es:
            sbt = lambda n, s, d: K.sbt(es, n, s, d); pst = lambda n, s, d: K.pst(es, n, s, d)
            NE = 1226
            h2T = sbt("h2T", [128, 16, NREG], BF16); B_h2T = Buf("h2T")
            S.dma("sp", h2T[:, :, :], h2T_d.rearrange("(c p) t -> p c t", p=128), reads=[B_h2d], writes=[B_h2T], semkey="f_h2T")
            wc = sbt("wc", [128, 3, 88], F32); B_wc = Buf("wc")
            for kk_ in range(3):
                S.dma("sp", wc[:, kk_, :], K.w_conv[kk_, :].rearrange("(c p) -> p c", p=128), writes=[B_wc], semkey="f_wc")
            wu = [sbt("wu%d" % i, [128, 16, 128], BF16) for i in range(4)]; B_wu = [Buf("wu%d" % i) for i in range(4)]
            ue = [sbt("ue%d" % i, [128, NE], F32) for i in range(2)]; B_ue = [Buf("ue0"), Buf("ue1")]
            z = [sbt("z%d" % i, [128, NE], F32) for i in range(2)]; B_z = [Buf("z0"), Buf("z1")]
            for i in range(2):
                S.op("pool", lambda e, i=i: e.memset(ue[i][:, 0:2], 0.0), writes=[B_ue[i]])
            pu = [pst("pu%d" % i, [128, 512], F32) for i in range(6)]; B_pu = [Buf("pu%d" % i) for i in range(6)]
            B_co = Buf("conv_out")
            NTL = ((0, 512), (512, 512), (1024, 192))
            for j in range(44):
                for gv_ in range(2):
                    ch = j + 44 * gv_
                    w = wu[(2 * j + gv_) % 4]; Bw = B_wu[(2 * j + gv_) % 4]
                    S.dma("sp", w[:, :, :], K.w_up_b[:, ch * 128:(ch + 1) * 128].rearrange("(c p) n -> p c n", p=128), writes=[Bw], semkey="f_wu%d" % ((2 * j + gv_) % 4))
                    u = ue[gv_]; Bu = B_ue[gv_]
                    S.dma("sp", u[:, 1154:1226].rearrange("p (s t) -> p s t", t=18)[:, :, 0:2], K.st_conv[:, ch, :, :], writes=[Bu], semkey="f_cv%d" % gv_)
                    for ti, (c0, cn) in enumerate(NTL):
                        p = pu[3 * gv_ + ti]; Bp = B_pu[3 * gv_ + ti]
                        for k in range(16):
                            S.op("pe", lambda e, k=k, p=p, c0=c0, cn=cn: e.matmul(p[:, 0:cn], w[:, k, :], h2T[:, k, c0:c0 + cn], start=(k == 0), stop=(k == 15)),
                                 reads=[Bw, B_h2T], writes=[Bp], inc=(k == 15))
                        if ti < 2:
                            S.op("act", lambda e, p=p, c0=c0, cn=cn, u=u: e.copy(u[:, 2 + c0:2 + c0 + cn], p[:, 0:cn]), reads=[Bp], writes=[Bu])
                        else:
                            S.op("act", lambda e, p=p, u=u: e.copy(u[:, 2 + 1024:2 + 1152], p[:, 0:128]), reads=[Bp], writes=[Bu])
                            S.op("dve", lambda e, p=p, u=u: e.tensor_copy(u[:, 1154:1226].rearrange("p (s t) -> p s t", t=18)[:, :, 2:18], p[:, 128:192].rearrange("p (s t) -> p s t", t=16)), reads=[Bp], writes=[Bu])
                    S.dma("pool", K.conv_p[:, ch, :], u[:, 1152:1154], reads=[Bu], writes=[B_co], semkey="st_cv")
                    S.dma("pool", K.conv_s[:, ch, :, :], u[:, 1154:1226].rearrange("p (s t) -> p s t", t=18)[:, :, 16:18], reads=[Bu], writes=[B_co], semkey="st_cv")
                    zz = z[gv_]; Bz = B_z[gv_]
                    S.op("dve", lambda e, u=u, zz=zz, ch=ch: e.tensor_scalar(zz[:, 0:1224], u[:, 0:1224], wc[:, 0, ch:ch + 1], None, ALU.mult), reads=[Bu, B_wc], writes=[Bz])
                    S.op("dve", lambda e, u=u, zz=zz, ch=ch: e.scalar_tensor_tensor(zz[:, 0:1224], u[:, 1:1225], wc[:, 1, ch:ch + 1], zz[:, 0:1224], ALU.mult, ALU.add), reads=[Bu, B_wc, Bz], writes=[Bz])
                    S.op("dve", lambda e, u=u, zz=zz, ch=ch: e.scalar_tensor_tensor(zz[:, 0:1224], u[:, 2:1226], wc[:, 2, ch:ch + 1], zz[:, 0:1224], ALU.mult, ALU.add), reads=[Bu, B_wc, Bz], writes=[Bz])
                S.op("act", lambda e: e.activation(z[0][:, 0:1224], z[0][:, 0:1224], AF.Silu), reads=[B_z[0]], writes=[B_z[0]])
                S.op("pool", lambda e, j=j: e.tensor_tensor(aT[:, j, 0:1152], z[0][:, 0:1152], z[1][:, 0:1152], ALU.mult), reads=[B_z[0], B_z[1]], writes=[B_aT])
                S.op("pool", lambda e, j=j: e.tensor_tensor(aT[:, j, 1152:1216].rearrange("p (s t) -> p s t", t=16),
                                                            z[0][:, 1154:1226].rearrange("p (s t) -> p s t", t=18)[:, :, 0:16],
                                                            z[1][:, 1154:1226].rearrange("p (s t) -> p s t", t=18)[:, :, 0:16], ALU.mult), reads=[B_z[0], B_z[1]], writes=[B_aT])
            S.barrier()
        with ExitStack() as es:
            sbt = lambda n, s, d: K.sbt(es, n, s, d); pst = lambda n, s, d: K.pst(es, n, s, d)
            rows = sbt("rowsC", [128, 2, 1, D], F32); B_rows = Buf("rowsC")
            S.dma("sp", rows[:, 0, :, :], K.rows_d[3:4, 0, :].partition_broadcast(128), writes=[B_rows], semkey="f_rowsC")
            for s_ in range(4):
                S.dma("sp", rows[16 * s_:16 * s_ + 16, 1, :, :], K.rows_d[3:4, 1 + s_, :].partition_broadcast(16), writes=[B_rows], semkey="f_rowsC")
            pu = [pst("pf%d" % i, [128, 512], F32) for i in range(8)]; B_pu = [Buf("pf%d" % i) for i in range(8)]
            wd = [sbt("wd%d" % i, [128, D], BF16) for i in range(4)]; B_wd = [Buf("wd%d" % i) for i in range(4)]
            x1 = sbt("x1c", [128, D], F32); B_x1 = Buf("x1c")
            tmp = sbt("tmpc", [128, D], F32); B_tmp = Buf("tmpc")
            yo = sbt("yo", [128, D], F32); B_yo = Buf("yo")
            sc = sbt("scc", [128, 8], F32); B_sc = Buf("scc")
            x1b = [x1, sbt("x1c2", [128, D], F32)]; B_x1b = [B_x1, Buf("x1c2")]
            for tg in range(1, NT, 2):
                tl = [t for t in (tg, tg + 1) if t < NT]
                for li_, t in enumerate(tl):
                    r0, nr = trow(t)
                    S.dma("sp", x1b[li_][0:nr, :], x1_d[r0:r0 + nr, :], writes=[B_x1b[li_]], semkey="f_x1c%d" % li_)
                for j in range(44):
                    w = wd[j % 4]; Bw = B_wd[j % 4]
                    S.dma("sp", w[:, :], K.w_down_b[j * 128:(j + 1) * 128, :], writes=[Bw], semkey="f_wd%d" % (j % 4))
                    for li_, t in enumerate(tl):
                        r0, nr = trow(t)
                        for g in range(4):
                            pi = li_ * 4 + g
                            S.op("pe", lambda e, g=g, j=j, w=w, pi=pi, r0=r0, nr=nr: e.matmul(pu[pi][0:nr, :], aT[:, j, r0:r0 + nr], w[:, g * 512:(g + 1) * 512], start=(j == 0), stop=(j == 43)),
                                 reads=[Bw, B_aT], writes=[B_pu[pi]], inc=(g == 3))
                for li_, t in enumerate(tl):
                    r0, nr = trow(t); ri = 0 if t < 9 else 1
                    pf = pu[li_ * 4:li_ * 4 + 4]; B_pf = B_pu[li_ * 4:li_ * 4 + 4]
                    x1 = x1b[li_]; B_x1 = B_x1b[li_]
                    for g in range(4):
                        S.op("act", lambda e, g=g, pf=pf, nr=nr: e.activation(tmp[0:nr, g * 512:(g + 1) * 512], pf[g][0:nr, :], AF.Square, accum_out=sc[0:nr, g:g + 1]), reads=[B_pf[g]], writes=[B_tmp, B_sc])
                    S.op("dve", lambda e, nr=nr: e.reduce_sum(sc[0:nr, 4:5], sc[0:nr, 0:4], AX.X), reads=[B_sc], writes=[B_sc])
                    S.op("act", lambda e, nr=nr: e.activation(sc[0:nr, 5:6], sc[0:nr, 4:5], AF.Sqrt, bias=K.epsc[0:nr, 0:1], scale=1.0 / D), reads=[B_sc, K.B_const], writes=[B_sc])
                    S.op("dve", lambda e, nr=nr: e.reciprocal(sc[0:nr, 5:6], sc[0:nr, 5:6]), reads=[B_sc], writes=[B_sc])
                    for g in range(4):
                        gs = slice(g * 512, (g + 1) * 512)
                        S.op("dve", lambda e, g=g, gs=gs, pf=pf, nr=nr, ri=ri: e.scalar_tensor_tensor(tmp[0:nr, gs], pf[g][0:nr, :], sc[0:nr, 5:6], rows[0:nr, ri, 0, gs], ALU.mult, ALU.mult), reads=[B_pf[g], B_sc, B_rows], writes=[B_tmp])
                    S.op("pool", lambda e, nr=nr, x1=x1: e.tensor_tensor(yo[0:nr, :], x1[0:nr, :], tmp[0:nr, :], ALU.add), reads=[B_x1, B_tmp], writes=[B_yo])
                    S.dma("pool", K.y_o[r0 - 128:r0 - 128 + nr, :], yo[0:nr, :], reads=[B_yo], writes=[B_y], semkey="st_y")
            S.barrier()


_CACHE = {}


def _consts():
    cm = np.zeros((128, 5, 128), np.float32)
    r = np.arange(128)[:, None]; c = np.arange(128)[None, :]
    cm[:, 0, :] = (r == c)
    cm[:, 1, :] = (r % 64) < (c % 64)
    cm[:, 2, :] = (r % 64) <= (c % 64)
    cm[:, 3, :] = (r % 64) > (c % 64)
    cm[:, 4, :] = ((r // 64) == (c // 64)) / 64.0
    return cm


def kernel(x_prompt, x_sample, c_prompt, c_sample, cache_k, cache_v, state_wkv, state_shift,
           state_ffn_conv, w_ada, b_ada, g_pre_mix, g_post_mix, g_pre_ffn, g_post_ffn, w_in,
           lam_q1, lam_k1, lam_q2, lam_k2, g_subln, mu_shift, w0, w_w2, a0, w_a2, w_g2, k_k, k_a,
           r_k, ln_x_w, ln_x_b, w_out, w_up, w_conv_ffn, w_down, _dbg=-1, _cores=None):
    f = lambda a: np.ascontiguousarray(np.asarray(a, dtype=np.float32))
    x_prompt = f(x_prompt); x_sample = f(x_sample); cache_k = f(cache_k); cache_v = f(cache_v)
    shared = {
        "w_ada": f(w_ada)[0], "b_ada": f(b_ada)[0], "g_pre_mix": f(g_pre_mix)[0], "g_post_mix": f(g_post_mix)[0],
        "g_pre_ffn": f(g_pre_ffn)[0], "g_post_ffn": f(g_post_ffn)[0], "w_in": f(w_in)[0],
        "lamv": np.stack([f(lam_q1)[0], f(lam_k1)[0], f(lam_q2)[0], f(lam_k2)[0]]), "g_subln": f(g_subln)[0],
        "mu_shift": f(mu_shift)[0], "w0": f(w0)[0], "w_w2": f(w_w2)[0], "a0": f(a0)[0], "w_a2": f(w_a2)[0],
        "w_g2": f(w_g2)[0], "k_k": f(k_k)[0], "k_a": f(k_a)[0], "r_k": f(r_k)[0].reshape(1024),
        "ln_x_w": f(ln_x_w)[0], "ln_x_b": f(ln_x_b)[0], "w_out": f(w_out)[0], "w_up": f(w_up)[0],
        "w_conv": f(w_conv_ffn)[0], "w_down": f(w_down)[0], "cmask": _consts(),
    }
    in_maps = []
    cores = list(range(8)) if _cores is None else list(_cores)
    for c in cores:
        b, j = c // 4, c % 4
        hi = (j + 1) * 1024; lo = hi - NSEQ
        npad = max(0, -lo)
        xs = np.zeros((NTOK, D), np.float32)
        xs[npad:NSEQ] = x_prompt[b, max(lo, 0):hi]
        xs[NSEQ:] = x_sample[4 * c:4 * c + 4].reshape(64, D)
        vmask = np.ones((NTOK,), np.float32); vmask[:npad] = 0.0
        kb = np.zeros((24,), np.float32)
        for t in range(24):
            if t * 128 < npad: kb[t] = NEG
        cT = np.concatenate([np.asarray(c_prompt, np.float32)[b:b + 1], np.asarray(c_sample, np.float32)[4 * c:4 * c + 4]], 0).T
        m = dict(shared)
        m.update({
            "xT": np.ascontiguousarray(xs.T), "valid": np.ascontiguousarray(np.broadcast_to(vmask[None, :], (128, NTOK))),
            "kbias": np.ascontiguousarray(np.broadcast_to(kb[None, :], (128, 24))),
            "x_tm": np.ascontiguousarray(xs[HALO0:]), "cT": np.ascontiguousarray(cT),
            "cache_kT": np.ascontiguousarray(cache_k[0, 4 * c:4 * c + 4].transpose(0, 2, 3, 1)),
            "cache_v": np.ascontiguousarray(cache_v[0, 4 * c:4 * c + 4]),
            "st_wkv": np.ascontiguousarray(f(state_wkv)[0, 4 * c:4 * c + 4].transpose(0, 1, 3, 2)),
            "st_shift": np.ascontiguousarray(f(state_shift)[0, 4 * c:4 * c + 4].reshape(4, 26, 128).transpose(2, 1, 0)),
            "st_conv": np.ascontiguousarray(f(state_ffn_conv)[0, 4 * c:4 * c + 4].reshape(4, 2, 88, 128).transpose(3, 2, 0, 1)),
        })
        in_maps.append(m)
    if _dbg not in _CACHE:
        _CACHE[_dbg] = build(_dbg)[0]
    nc = _CACHE[_dbg]
    res = run_bass_kernel_spmd(nc, in_maps, core_ids=list(range(len(cores)))).results
    yp = np.zeros((2, 4096, D), np.float32); ys = np.zeros((32, 16, D), np.float32)
    kp = np.zeros((1, 2, 4096, 8, 128), np.float32); vp = np.zeros_like(kp)
    ks = np.zeros((1, 32, 16, 8, 128), np.float32); vs = np.zeros_like(ks)
    wp = np.zeros((1, 2, 16, 64, 64), np.float32); wsm = np.zeros((1, 32, 16, 64, 64), np.float32)
    shp = np.zeros((1, 2, RWC), np.float32); shs = np.zeros((1, 32, RWC), np.float32)
    cp = np.zeros((1, 2, 2, 2 * DFF), np.float32); cs = np.zeros((1, 32, 2, 2 * DFF), np.float32)
    for ci, c in enumerate(cores):
        b, j = c // 4, c % 4; r = res[ci]
        yp[b, j * 1024:(j + 1) * 1024] = r["y_o"][:1024]; ys[4 * c:4 * c + 4] = r["y_o"][1024:].reshape(4, 16, D)
        kp[0, b, j * 1024:(j + 1) * 1024] = r["k_o"][:1024].reshape(1024, 8, 128); ks[0, 4 * c:4 * c + 4] = r["k_o"][1024:].reshape(4, 16, 8, 128)
        vp[0, b, j * 1024:(j + 1) * 1024] = r["v_o"][:1024].reshape(1024, 8, 128); vs[0, 4 * c:4 * c + 4] = r["v_o"][1024:].reshape(4, 16, 8, 128)
        wsm[0, 4 * c:4 * c + 4] = r["wkv_s"].transpose(0, 1, 3, 2)
        shs[0, 4 * c:4 * c + 4] = r["shift_s"].transpose(2, 1, 0).reshape(4, RWC)
        cs[0, 4 * c:4 * c + 4] = r["conv_s"].transpose(2, 3, 1, 0).reshape(4, 2, 2 * DFF)
        if j == 3:
            wp[0, b] = r["wkv_p"].transpose(0, 2, 1)
            shp[0, b] = r["shift_p"].T.reshape(RWC)
            cp[0, b] = r["conv_p"].transpose(2, 1, 0).reshape(2, 2 * DFF)
    return (yp, ys, kp, vp, wp, shp, cp, ks, vs, wsm, shs, cs)
```
